# Optimizing a Trainium2 kernel written in Bass

```python
import jax, jax.numpy as jnp
from jax import lax
import numpy as np

D_MODEL = 2048
BATCH = 4
SEQ = 2048
DEPTH = 2
DEC_BATCH = 128
DEC_SEQ = 8
PAST_LEN = 16384
PAGE_SIZE = 128

HG_HEADS = 8
HG_DK = 128
HG_DV = 128
HG_WIDTH = HG_HEADS * HG_DK
HG_CHUNK = 64
HG_F_MIN = 1e-30
RW_HEADS = 16
RW_N = 64
RW_WIDTH = RW_HEADS * RW_N
RW_DECAY_LORA = 64
RW_AAA_LORA = 64
RW_GATE_LORA = 128
RW_SHIFT_WIDTH = 3 * RW_WIDTH + RW_DECAY_LORA + RW_AAA_LORA + RW_GATE_LORA
RW_GN_EPS = 64e-5
SC_WIDTH = 1024
SC_KSIZE = 3
N_BRANCH = 3
D_FF = 4 * D_MODEL
NORM_EPS = 1e-6
OFF_HG = 0
OFF_RW = OFF_HG + 4 * HG_WIDTH
OFF_SC = OFF_RW + RW_SHIFT_WIDTH
OFF_GATE = OFF_SC + 3 * SC_WIDTH
IN_TOTAL = OFF_GATE + N_BRANCH * D_MODEL

kernel_name = "hybrid_hgrn2_rwkv7_shortconv_decode_step"


def _rmsnorm(x, g):
    xf = x.astype(jnp.float32)
    return xf * lax.rsqrt(jnp.mean(xf * xf, -1, keepdims=True) + NORM_EPS) * g.astype(jnp.float32)


def _hgrn2_chunked(q, k, v, logf, S0):
    Bn, T, H, dk = q.shape
    C = min(HG_CHUNK, T)
    nc = -(-T // C)
    pad = nc * C - T

    def prep(a):
        a = jnp.pad(a, ((0, 0), (0, pad), (0, 0), (0, 0)))
        return a.reshape(Bn, nc, C, H, a.shape[-1]).transpose(1, 0, 3, 2, 4)

    qc, kc, vc, gc = prep(q), prep(k), prep(v), prep(logf)
    mask = jnp.tril(jnp.ones((C, C), dtype=jnp.float32))[:, :, None]

    def step(S, inp):
        q_, k_, v_, g_ = inp
        b = jnp.cumsum(g_, axis=2)
        inter = jnp.einsum('bhtd,bhde->bhte', q_ * jnp.exp(b), S)
        diff = b[:, :, :, None, :] - b[:, :, None, :, :]
        decay = jnp.exp(jnp.minimum(diff, 0.0)) * mask
        A = jnp.einsum('bhtd,bhsd,bhtsd->bhts', q_, k_, decay)
        o = inter + jnp.einsum('bhts,bhse->bhte', A, v_)
        b_last = b[:, :, -1]
        S = jnp.exp(b_last)[..., None] * S + jnp.einsum(
            'bhsd,bhse->bhde', k_ * jnp.exp(b_last[:, :, None] - b), v_)
        return S, o

    S, o = lax.scan(step, S0, (qc, kc, vc, gc))
    o = o.transpose(1, 0, 3, 2, 4).reshape(Bn, nc * C, H, v.shape[-1])[:, :T]
    return o, S


def _rwkv7_scan(r, w, k, v, kk, a, S0):
    def step(S, inp):
        r_, w_, k_, v_, kk_, a_ = inp
        sa = jnp.einsum('bhij,bhj->bhi', S, -kk_)
        S = (S * w_[:, :, None, :] + sa[..., None] * (kk_ * a_)[:, :, None, :]
             + v_[..., None] * k_[:, :, None, :])
        return S, jnp.einsum('bhij,bhj->bhi', S, r_)

    xs = tuple(t.transpose(1, 0, 2, 3) for t in (r, w, k, v, kk, a))
    S, o = lax.scan(step, S0, xs)
    return o.transpose(1, 0, 2, 3), S


def trunk_layer(x, hg_S0, rw_S0, rw_prev, sc_buf, lb,
                g_pre_mix, g_post_mix, g_pre_mlp, g_post_mlp, w_in,
                hg_norm, w_pa,
                rw_mu, rw_w0, rw_w2, rw_a0, rw_a2, rw_g2, rw_k_k, rw_k_a, rw_r_k,
                rw_ln_w, rw_ln_b, w_pb,
                sc_conv_w, w_pc, w_o, w_ff1, w_ff2):
    f32 = jnp.float32
    Bn, T, _ = x.shape
    h = _rmsnorm(x, g_pre_mix).astype(x.dtype)
    proj = jnp.einsum('btd,de->bte', h, w_in).astype(f32)

    hq, hf, hi, hog = jnp.split(proj[..., OFF_HG:OFF_RW], 4, axis=-1)
    q = jax.nn.silu(hq).reshape(Bn, T, HG_HEADS, HG_DK)
    lbh = lb.astype(f32).reshape(HG_HEADS, HG_DK)
    f = lbh + (1.0 - lbh) * jax.nn.sigmoid(hf.reshape(Bn, T, HG_HEADS, HG_DK))
    logf = jnp.log(jnp.maximum(f, HG_F_MIN))
    k_hg = 1.0 - f
    v_hg = hi.reshape(Bn, T, HG_HEADS, HG_DV)
    o_hg, hg_S = _hgrn2_chunked(q, k_hg, v_hg, logf, hg_S0.astype(f32))
    o_hg = o_hg * lax.rsqrt(jnp.mean(o_hg * o_hg, -1, keepdims=True) + NORM_EPS)
    o_hg = o_hg.reshape(Bn, T, HG_WIDTH) * hg_norm * jax.nn.silu(hog)
    y_a = o_hg @ w_pa

    rw = proj[..., OFF_RW:OFF_SC]
    prev = jnp.concatenate([rw_prev.astype(f32)[:, None], rw[:, :-1]], axis=1)
    xs = rw + (prev - rw) * rw_mu
    cuts = [RW_WIDTH, 2 * RW_WIDTH, 3 * RW_WIDTH, 3 * RW_WIDTH + RW_DECAY_LORA,
            3 * RW_WIDTH + RW_DECAY_LORA + RW_AAA_LORA]
    r, kr, vr, wd, ad, gd = jnp.split(xs, cuts, axis=-1)
    w_log = -jax.nn.softplus(-(rw_w0 + jnp.tanh(wd) @ rw_w2)) - 0.5
    decay = jnp.exp(-jnp.exp(w_log))
    a = jax.nn.sigmoid(rw_a0 + ad @ rw_a2)
    g = jax.nn.sigmoid(gd) @ rw_g2
    hs = lambda t: t.reshape(Bn, T, RW_HEADS, RW_N)
    kk = hs(kr * rw_k_k)
    kk = kk / jnp.maximum(jnp.sqrt(jnp.sum(kk * kk, -1, keepdims=True)), 1e-12)
    k = kr * (1.0 + (a - 1.0) * rw_k_a)
    r_, w_, k_, v_, a_ = hs(r), hs(decay), hs(k), hs(vr), hs(a)
    o_rw, rw_S = _rwkv7_scan(r_, w_, k_, v_, kk, a_, rw_S0.astype(f32))
    mu = jnp.mean(o_rw, -1, keepdims=True)
    var = jnp.mean(jnp.square(o_rw - mu), -1, keepdims=True)
    o_rw = ((o_rw - mu) * lax.rsqrt(var + RW_GN_EPS)).reshape(Bn, T, RW_WIDTH) * rw_ln_w + rw_ln_b
    bonus = jnp.sum(r_ * k_ * rw_r_k, -1, keepdims=True) * v_
    o_rw = (o_rw + bonus.reshape(Bn, T, RW_WIDTH)) * g
    y_b = o_rw @ w_pb
    rw_last = rw[:, -1]

    sb, sc, sh = jnp.split(proj[..., OFF_SC:OFF_GATE], 3, axis=-1)
    u = sc * sh
    full = jnp.concatenate([sc_buf.astype(f32), u], axis=1)
    conv = sum(sc_conv_w[j] * full[:, j:j + T] for j in range(SC_KSIZE))
    y_c = (sb * conv) @ w_pc
    sc_new = full[:, T:]

    gates = jax.nn.sigmoid(proj[..., OFF_GATE:].reshape(Bn, T, N_BRANCH, D_MODEL))
    m = gates[:, :, 0] * y_a + gates[:, :, 1] * y_b + gates[:, :, 2] * y_c
    mix = m @ w_o
    x1 = x.astype(f32) + _rmsnorm(mix, g_post_mix)
    h2 = _rmsnorm(x1, g_pre_mlp).astype(x.dtype)
    ff = jnp.square(jax.nn.relu(h2 @ w_ff1)) @ w_ff2
    y = (x1 + _rmsnorm(ff, g_post_mlp)).astype(x.dtype)
    return y, hg_S, rw_S, rw_last, sc_new


def setup_inputs(seed: int = 0) -> dict:
    key = jax.random.key(seed)
    ks = iter(jax.random.split(key, 40))
    nrm = lambda shape, s: jax.random.normal(next(ks), shape, jnp.float32) * s
    L = DEPTH
    return {
        "x_prompt": nrm((BATCH, SEQ, D_MODEL), 1.0),
        "x_sample": nrm((DEC_BATCH, DEC_SEQ, D_MODEL), 1.0),
        "state_hgrn": nrm((L, DEC_BATCH, HG_HEADS, HG_DK, HG_DV), 0.3),
        "state_rwkv": nrm((L, DEC_BATCH, RW_HEADS, RW_N, RW_N), 0.1),
        "state_rwkv_shift": nrm((L, DEC_BATCH, RW_SHIFT_WIDTH), 1.0),
        "state_conv": nrm((L, DEC_BATCH, SC_KSIZE - 1, SC_WIDTH), 0.5),
        "norm_pre_mix": 1.0 + nrm((L, D_MODEL), 0.02),
        "norm_post_mix": 1.0 + nrm((L, D_MODEL), 0.02),
        "norm_pre_mlp": 1.0 + nrm((L, D_MODEL), 0.02),
        "norm_post_mlp": 1.0 + nrm((L, D_MODEL), 0.02),
        "w_in": nrm((L, D_MODEL, IN_TOTAL), D_MODEL ** -0.5),
        "hg_lb_logits": nrm((L, HG_WIDTH), 0.5),
        "hg_norm": 1.0 + nrm((L, HG_WIDTH), 0.02),
        "w_pa": nrm((L, HG_WIDTH, D_MODEL), HG_WIDTH ** -0.5),
        "rw_mu": jax.random.uniform(next(ks), (L, RW_SHIFT_WIDTH), jnp.float32),
        "rw_w0": -2.0 + nrm((L, RW_WIDTH), 0.5),
        "rw_w2": nrm((L, RW_DECAY_LORA, RW_WIDTH), 0.3 * RW_DECAY_LORA ** -0.5),
        "rw_a0": nrm((L, RW_WIDTH), 0.1),
        "rw_a2": nrm((L, RW_AAA_LORA, RW_WIDTH), 0.5 * RW_AAA_LORA ** -0.5),
        "rw_g2": nrm((L, RW_GATE_LORA, RW_WIDTH), RW_GATE_LORA ** -0.5),
        "rw_k_k": 0.85 + nrm((L, RW_WIDTH), 0.02),
        "rw_k_a": 1.0 + nrm((L, RW_WIDTH), 0.02),
        "rw_r_k": nrm((L, RW_HEADS, RW_N), 0.1),
        "rw_ln_w": 1.0 + nrm((L, RW_WIDTH), 0.02),
        "rw_ln_b": nrm((L, RW_WIDTH), 0.01),
        "w_pb": nrm((L, RW_WIDTH, D_MODEL), RW_WIDTH ** -0.5),
        "sc_conv_w": nrm((L, SC_KSIZE, SC_WIDTH), 0.5),
        "w_pc": nrm((L, SC_WIDTH, D_MODEL), SC_WIDTH ** -0.5),
        "w_o": nrm((L, D_MODEL, D_MODEL), D_MODEL ** -0.5),
        "w_ff1": nrm((L, D_MODEL, D_FF), D_MODEL ** -0.5),
        "w_ff2": nrm((L, D_FF, D_MODEL), D_FF ** -0.5),
    }


def reference(x_prompt, x_sample, state_hgrn, state_rwkv, state_rwkv_shift, state_conv,
              norm_pre_mix, norm_post_mix, norm_pre_mlp, norm_post_mlp, w_in,
              hg_lb_logits, hg_norm, w_pa,
              rw_mu, rw_w0, rw_w2, rw_a0, rw_a2, rw_g2, rw_k_k, rw_k_a, rw_r_k,
              rw_ln_w, rw_ln_b, w_pb,
              sc_conv_w, w_pc, w_o, w_ff1, w_ff2):
    f32 = jnp.float32
    P = jax.nn.softmax(hg_lb_logits.astype(f32), axis=0)
    lb_all = jnp.clip(jnp.cumsum(P, axis=0) - P[0:1], 0.0, 1.0)
    Bp = x_prompt.shape[0]
    yp, ys = x_prompt, x_sample
    p_hg, p_rw, p_sh, p_cv = [], [], [], []
    s_hg, s_rw, s_sh, s_cv = [], [], [], []
    for l in range(DEPTH):
        lw = (norm_pre_mix[l], norm_post_mix[l], norm_pre_mlp[l], norm_post_mlp[l], w_in[l],
              hg_norm[l], w_pa[l],
              rw_mu[l], rw_w0[l], rw_w2[l], rw_a0[l], rw_a2[l], rw_g2[l], rw_k_k[l], rw_k_a[l],
              rw_r_k[l], rw_ln_w[l], rw_ln_b[l], w_pb[l],
              sc_conv_w[l], w_pc[l], w_o[l], w_ff1[l], w_ff2[l])
        yp, a_, b_, c_, d_ = trunk_layer(
            yp,
            jnp.zeros((Bp, HG_HEADS, HG_DK, HG_DV), f32),
            jnp.zeros((Bp, RW_HEADS, RW_N, RW_N), f32),
            jnp.zeros((Bp, RW_SHIFT_WIDTH), f32),
            jnp.zeros((Bp, SC_KSIZE - 1, SC_WIDTH), f32),
            lb_all[l], *lw)
        p_hg.append(a_); p_rw.append(b_); p_sh.append(c_); p_cv.append(d_)
        ys, a_, b_, c_, d_ = trunk_layer(
            ys, state_hgrn[l], state_rwkv[l], state_rwkv_shift[l], state_conv[l],
            lb_all[l], *lw)
        s_hg.append(a_); s_rw.append(b_); s_sh.append(c_); s_cv.append(d_)
    return (yp, ys,
            jnp.stack(p_hg), jnp.stack(p_rw), jnp.stack(p_sh), jnp.stack(p_cv),
            jnp.stack(s_hg), jnp.stack(s_rw), jnp.stack(s_sh), jnp.stack(s_cv))
```

```python
import contextlib
import numpy as np
import concourse.bass as bass
import concourse.mybir as mybir
from concourse.bass_utils import run_bass_kernel_spmd

F32 = mybir.dt.float32
BF16 = mybir.dt.bfloat16
AF = mybir.ActivationFunctionType
ALU = mybir.AluOpType
AX = mybir.AxisListType

D = 2048
L = 2
NCORE = 8
HGH = 8
RWP = 8
SHW = 3328
DFF = 8192
EPS = 1e-6
GN_EPS = 64e-5
DEC_C = float(np.exp(-0.5))


class Op:
    __slots__ = ("eng", "fn", "deps", "chan", "seq", "signal", "is_dma")

    def __init__(self, eng, fn, chan, is_dma):
        self.eng = eng
        self.fn = fn
        self.chan = chan
        self.is_dma = is_dma
        self.deps = {}
        self.seq = 0
        self.signal = False


class _Rec:
    def __init__(self):
        self.call = None

    def __getattr__(self, name):
        def f(*a, **k):
            self.call = (name, a, k)
            return None
        return f


class _Stop(Exception):
    pass


class Prog:
    ENGS = ("pe", "act", "dve", "pool", "sp")

    def __init__(self):
        self.ops = {e: [] for e in self.ENGS}
        self.chan_ops = {}
        self.last_w = {}
        self.readers = {}
        self.waited = {e: {} for e in self.ENGS}
        self.fences = {}
        self.touched = {}

    def fence(self, *regions):
        for r in regions:
            f = self.fences.setdefault(r, {})
            for c, s in self.touched.get(r, {}).items():
                if f.get(c, 0) < s:
                    f[c] = s
            self.touched[r] = {}

    def fence_group(self, regions):
        u = {}
        for r in regions:
            for d in (self.fences.get(r, {}), self.touched.get(r, {})):
                for c, s in d.items():
                    if u.get(c, 0) < s:
                        u[c] = s
        for r in regions:
            self.fences[r] = dict(u)
            self.touched[r] = {}

    def _add(self, eng, fn, reads, writes, chan, is_dma):
        rec = _Rec()
        fn(rec)
        assert rec.call is not None
        op = Op(eng, rec.call, chan, is_dma)
        lst = self.chan_ops.setdefault(chan, [])
        lst.append(op)
        op.seq = len(lst)
        deps = {}
        pe_pe = (eng == "pe" and not is_dma)

        def need(c, s):
            if pe_pe and c == "pe":
                return
            if deps.get(c, 0) < s:
                deps[c] = s

        regs = set()
        for r in reads:
            o = self.last_w.get(r)
            if o is not None:
                need(o.chan, o.seq)
            regs.add(r.split(":")[0])
        for w in writes:
            o = self.last_w.get(w)
            if o is not None:
                need(o.chan, o.seq)
            for o in self.readers.get(w, ()):
                need(o.chan, o.seq)
            regs.add(w.split(":")[0])
        for rg in regs:
            for c, s in self.fences.get(rg, {}).items():
                need(c, s)
            t = self.touched.setdefault(rg, {})
            if t.get(chan, 0) < op.seq:
                t[chan] = op.seq
        w_seen = self.waited[eng]
        for c, s in deps.items():
            if c == eng and not is_dma and op.seq - s >= 3:
                continue
            if w_seen.get(c, 0) < s:
                w_seen[c] = s
                op.deps[c] = s
                self.chan_ops[c][s - 1].signal = True
        for r in reads:
            self.readers.setdefault(r, []).append(op)
        for w in writes:
            self.last_w[w] = op
            self.readers[w] = []
        self.ops[eng].append(op)
        return op

    def op(self, eng, fn, reads=(), writes=()):
        return self._add(eng, fn, reads, writes, eng, False)

    def dma(self, eng, fn, reads=(), writes=(), slot=None):
        return self._add(eng, fn, reads, writes, "dma:" + slot, True)

    def finish(self, eng="sp"):
        op = Op(eng, None, eng, False)
        for c, lst in self.chan_ops.items():
            if c.startswith("dma:") and lst:
                s = len(lst)
                if self.waited[eng].get(c, 0) < s:
                    op.deps[c] = s
                    lst[-1].signal = True
        self.ops[eng].append(op)

    def emit(self, nc):
        semval = {}
        for c, lst in self.chan_ops.items():
            n = 0
            for o in lst:
                if o.signal:
                    n += 1
                semval[(c, o.seq)] = n
        nsig = {c: sum(1 for o in lst if o.signal) for c, lst in self.chan_ops.items()}
        chans = [c for c in self.chan_ops if nsig[c] > 0]
        print("channels", len(chans), "ops", {e: len(v) for e, v in self.ops.items()},
              "maxsig", max(nsig.values()) if nsig else 0, flush=True)
        with contextlib.ExitStack() as st:
            sems = {}
            for i, c in enumerate(chans):
                sems[c] = st.enter_context(nc.semaphore("s%d" % i))
            block = st.enter_context(nc.Block())

            def run(engname, e):
                for o in self.ops[engname]:
                    for c, s in o.deps.items():
                        mult = 16 if c.startswith("dma:") else 1
                        e.wait_ge(sems[c], semval[(c, s)] * mult)
                    if o.fn is None:
                        continue
                    name, a, k = o.fn
                    ins = getattr(e, name)(*a, **k)
                    if o.signal:
                        ins.then_inc(sems[o.chan], 16 if o.is_dma else 1)

            @block.tensor
            def _(e):
                run("pe", e)

            @block.scalar
            def _(e):
                run("act", e)

            @block.vector
            def _(e):
                run("dve", e)

            @block.gpsimd
            def _(e):
                run("pool", e)

            @block.sync
            def _(e):
                run("sp", e)


def make_consts():
    t = np.arange(128)
    same64 = (t[:, None] // 64) == (t[None, :] // 64)
    same8 = (t[:, None] // 8) == (t[None, :] // 8)
    ge = t[None, :] >= t[:, None]
    gt = t[None, :] > t[:, None]
    lt = t[None, :] < t[:, None]
    keep = {}
    keep["ident"] = np.eye(128)
    keep["ones"] = np.ones((128, 128))
    keep["bd2"] = same64
    keep["pat8"] = np.tile((t[None, :] % 8 != 0), (128, 1))
    keep["ind2"] = (t[:, None] // 64) == np.arange(2)[None, :]
    keep["ind16"] = (t[:, None] // 8) == np.arange(16)[None, :]
    sc = np.zeros((128, 8))
    sc[:, 0] = EPS
    sc[:, 1] = GN_EPS
    keep["sc"] = sc
    masks = {}
    masks["hgM_p"] = ge & same64
    masks["hgM_s"] = ge & same8
    masks["rwL_p"] = lt
    masks["rwL_s"] = lt & same8
    masks["rwST_p"] = gt
    masks["rwST_s"] = gt & same8
    masks["rwIT_p"] = ge
    masks["rwIT_s"] = ge & same8
    masks["bd2b"] = same64

    def pack(items):
        off = {}
        cols = []
        o = 0
        for k, v in items.items():
            v = np.asarray(v, dtype=np.float32)
            off[k] = (o, v.shape[1])
            o += v.shape[1]
            cols.append(v)
        return np.concatenate(cols, axis=1), off

    a, ao = pack(keep)
    b, bo = pack(masks)
    return a, ao, b, bo


CONSTS, COFF, MASKS, MOFF = make_consts()
NCONST = CONSTS.shape[1]
NMASK = MASKS.shape[1]

VOFF = {}
_o = 0
for _n, _w in [("gpm", 16), ("gpl", 16), ("lb0", 8), ("lb1", 8), ("hgn", 8), ("mu", 26), ("w0", 8), ("a0", 8),
               ("kk", 8), ("ka", 8), ("rk", 8), ("lnw", 8), ("lnb", 8), ("cw", 24)]:
    VOFF[_n] = (_o, _w)
    _o += _w
NV = _o

MARKS = []
BLOCKS = [
    dict(tgs=[("p", 0, 4), ("p", 4, 4), ("s", 16, 1)]),
    dict(tgs=[("p", 8, 4), ("p", 12, 4)]),
]
for _b in BLOCKS:
    tiles = []
    tgs = []
    c = 0
    for kind, g0, nt in _b["tgs"]:
        tgs.append(dict(kind=kind, g0=g0, nt=nt, t0=c, n=nt * 128))
        for i in range(nt):
            tiles.append(g0 + i)
        c += nt * 128
    _b["tiles"] = tiles
    _b["T"] = c
    _b["tg"] = tgs


def build_program(dbg=None):
    nc = bass.Bass("TRN2", target_bir_lowering=False)
    P = Prog()

    def din(name, shape):
        return nc.dram_tensor(name, list(shape), F32, kind="ExternalInput").ap()

    def dout(name, shape):
        return nc.dram_tensor(name, list(shape), F32, kind="ExternalOutput").ap()

    xp = din("xp", [2048, D])
    xs = din("xs", [128, D])
    consts_d = din("consts", [128, NCONST])
    masks_d = din("masks", [128, NMASK])
    vecs_d = din("vecs", [L, 128, NV])
    gb_d = din("gb", [L, 2, D])
    w_hg = din("w_hg", [L, HGH, 2, 128, 4096])
    w_rwl = din("w_rwl", [L, 128, 4096])
    w_rwrk = din("w_rwrk", [L, RWP, 128, 4096])
    w_rwv = din("w_rwv", [L, RWP, 128, 2048])
    w_scbc = din("w_scbc", [L, 8, 128, 4096])
    w_sch = din("w_sch", [L, 8, 128, 2048])
    w_g = din("w_g", [L, 16, 3, 128, 2048])
    w_p = din("w_p", [L, 16, 128, 3072])
    w_o = din("w_o", [L, 8, 128, 4096])
    w_f1 = din("w_f1", [L, 32, 128, 4096])
    w_f2 = din("w_f2", [L, 8, 4, 128, 4096])
    w2_d = din("lw2", [L, 64, 1024])
    a2_d = din("la2", [L, 64, 1024])
    g2_d = din("lg2", [L, 128, 1024])
    st_hg = din("st_hg", [L, 16, HGH, 128, 128])
    st_rw = din("st_rw", [L, 16, 16, 64, 64])
    st_sh = din("st_sh", [L, 16, SHW])
    st_cv = din("st_cv", [L, 16, 2, 1024])

    yp = dout("yp", [2048, D])
    ys = dout("ys", [128, D])
    o_hg_p = dout("o_hg_p", [L, HGH, 128, 128])
    o_rw_p = dout("o_rw_p", [L, 16, 64, 64])
    o_sh_p = dout("o_sh_p", [L, 1, SHW])
    o_cv_p = dout("o_cv_p", [L, 2, 1024])
    o_hg_s = dout("o_hg_s", [L, 16, HGH, 128, 128])
    o_rw_s = dout("o_rw_s", [L, 16, 16, 64, 64])
    o_sh_s = dout("o_sh_s", [L, 16, SHW])
    o_cv_s = dout("o_cv_s", [L, 16, 2, 1024])
    dbg_out = {}
    if dbg:
        for n, sh in dbg.items():
            dbg_out[n] = dout("dbg_" + n, sh)

    y1 = nc.dram_tensor("y1s", [17 * 128, D], F32).ap()
    x1s = nc.dram_tensor("x1s", [9 * 128, D], F32).ap()

    st = contextlib.ExitStack()
    with st:
        SZ_HT = 36864
        SZ_OA = 55296
        SZ_R3 = 73728
        SZ_W = 3 * 8192
        SZ_GB = 8192
        SZ_C = 14080
        total = SZ_HT + SZ_OA + SZ_R3 + SZ_W + SZ_GB + SZ_C
        arena = st.enter_context(nc.sbuf_tensor("arena", [128, total // 4], F32))
        base = {}
        o = 0
        for n, s in [("HT", SZ_HT), ("OA", SZ_OA), ("R3", SZ_R3), ("W", SZ_W), ("GB", SZ_GB), ("C", SZ_C)]:
            base[n] = o
            o += s

        class Region:
            def __init__(self, name, size):
                self.name = name
                self.size = size
                self.cur = 0

            def reset(self, at=0):
                self.cur = at

            def alloc(self, nbytes, dtype=F32, shape=None, parts=None):
                nbytes = (nbytes + 31) // 32 * 32
                assert self.cur + nbytes <= self.size, (self.name, self.cur, nbytes, self.size)
                off = base[self.name] + self.cur
                self.cur += nbytes
                ap = arena[:, off // 4:(off + nbytes) // 4]
                if dtype == BF16:
                    ap = ap.bitcast(BF16)
                return ap

        def f32buf(reg, n):
            return reg.alloc(4 * n)[:, 0:n]

        def bfbuf(reg, n):
            return reg.alloc(2 * n, BF16)[:, 0:n]

        RHT = Region("HT", SZ_HT)
        ROA = Region("OA", SZ_OA)
        RR3 = Region("R3", SZ_R3)
        RW = Region("W", SZ_W)
        RGB = Region("GB", SZ_GB)
        RC = Region("C", SZ_C)

        hT = RHT.alloc(SZ_HT, BF16).rearrange("p (k t) -> p k t", k=16)
        o_all = ROA.alloc(SZ_OA, BF16).rearrange("p (k t) -> p k t", k=24)
        wslots = [RW.alloc(8192, BF16) for _ in range(3)]
        GB = RGB.alloc(8192)

        CN = f32buf(RC, NCONST)
        VEC = f32buf(RC, NV)
        DER = f32buf(RC, 40)
        identb = bfbuf(RC, 128)
        HGS = f32buf(RC, HGH * 128).rearrange("p (h e) -> p h e", h=HGH)
        RWS = f32buf(RC, RWP * 128).rearrange("p (h e) -> p h e", h=RWP)

        MK = bfbuf(RC, NMASK)
        PRVC = f32buf(RC, 32)
        CVH = f32buf(RC, 16).rearrange("p (c j) -> p c j", c=8)

        def cst(name):
            if name in MOFF:
                o_, w_ = MOFF[name]
                return MK[:, o_:o_ + w_]
            o_, w_ = COFF[name]
            return CN[:, o_:o_ + w_]

        def vec(name, c=None, n=1):
            o_, w_ = VOFF[name]
            if c is None:
                return VEC[:, o_:o_ + w_]
            return VEC[:, o_ + c:o_ + c + n]

        banks = [st.enter_context(nc.psum_tensor("ps%d" % i, [128, 512], F32)) for i in range(8)]
        bank_ctr = [0]

        def psbank():
            i = bank_ctr[0] % 8
            bank_ctr[0] += 1
            return banks[i], "PS:%d" % i

        wctr = [0]

        def wload(src, n):
            s = wctr[0] % 3
            wctr[0] += 1
            dst = wslots[s][:, 0:n]
            P.dma("pool", lambda e: e.dma_start(out=dst, in_=src), writes=["W:%d" % s], slot="W%d" % s)
            return wslots[s][:, 0:n], "W:%d" % s

        def act(fn, reads, writes):
            P.op("act", fn, reads, writes)

        def dve(fn, reads, writes):
            P.op("dve", fn, reads, writes)

        def pe(fn, reads, writes):
            P.op("pe", fn, reads, writes)

        def sp_dma(out, in_, reads, writes, slot):
            P.dma("sp", lambda e: e.dma_start(out=out, in_=in_), reads, writes, slot)

        def rsqrt_col(dst, src, scale, eps_ap, rd, wr, tmp):
            act(lambda e: e.activation(out=tmp, in_=src, func=AF.Sqrt, bias=eps_ap, scale=scale), rd, [wr + ".t"])
            dve(lambda e: e.reciprocal(out=dst, in_=tmp), [wr + ".t"], [wr])

        eps6 = cst("sc")[:, 0:1]
        epsgn = cst("sc")[:, 1:2]
        zero_c = cst("sc")[:, 2:3]

        def hres(t0, n):
            return ["HT:t%d" % i for i in range(t0 // 128, (t0 + n + 127) // 128)]

        def tgroups(T):
            r = []
            t = 0
            while t < T:
                n = min(512, T - t)
                r.append((t, n))
                t += n
            return r

        def dbg_dump(name, ap, res):
            if dbg and name in dbg_out:
                sp_dma(dbg_out[name], ap, res, [], "dbg_" + name)

        sp_dma(CN, consts_d, [], ["C:cn0"], "cn")
        RR3.reset()
        _mtmp = f32buf(RR3, NMASK)
        sp_dma(_mtmp, masks_d, [], ["R3:mtmp"], "mtmp")
        dve(lambda e: e.tensor_copy(out=MK, in_=_mtmp), ["R3:mtmp", "C:cn0"], ["C:cn"])
        dve(lambda e: e.tensor_copy(out=identb, in_=cst("ident")), ["C:cn"], ["C:identb"])
        identf = cst("ident")

        def xsrc(l, g):
            if l == 0:
                return xp[g * 128:(g + 1) * 128, :] if g < 16 else xs
            return y1[g * 128:(g + 1) * 128, :]

        def ydst(l, g):
            if l == 0:
                return y1[g * 128:(g + 1) * 128, :]
            return yp[g * 128:(g + 1) * 128, :] if g < 16 else ys

        def yres(l, g):
            return "DR:y%d_%d" % (l, g)

        def fence_all():
            P.fence("HT", "OA", "R3", "GB", "PS")

        def layer_setup(l):
            sp_dma(VEC, vecs_d[l], [], ["C:vec"], "vec")
            lb = DER[:, 0:8]
            oml = DER[:, 8:16]
            omka = DER[:, 16:24]
            if l == 0:
                dve(lambda e: e.memset(lb, 0.0), ["C:vec"], ["C:der"])
            else:
                dve(lambda e: e.tensor_tensor(out=lb, in0=vec("lb1"), in1=vec("lb0"), op=ALU.subtract), ["C:vec"], ["C:der"])
                act(lambda e: e.activation(out=lb, in_=lb, func=AF.Sigmoid), ["C:der"], ["C:der"])
            dve(lambda e: e.tensor_scalar(out=oml, in0=lb, scalar1=-1.0, scalar2=1.0, op0=ALU.mult, op1=ALU.add), ["C:der"], ["C:der"])
            dve(lambda e: e.tensor_scalar(out=omka, in0=vec("ka"), scalar1=-1.0, scalar2=1.0, op0=ALU.mult, op1=ALU.add), ["C:vec", "C:der"], ["C:der"])
            dve(lambda e: e.memset(HGS, 0.0), [], ["C:hgs"])
            dve(lambda e: e.memset(RWS, 0.0), [], ["C:rws%d" % _p for _p in range(RWP)])
            dve(lambda e: e.memset(PRVC, 0.0), [], ["C:prvc%d" % _c for _c in range(26)])
            dve(lambda e: e.memset(CVH, 0.0), [], ["C:cvh"])

        def norm_transpose(src_ap, src_res, ss_src, dstT, i, gname, reg, tagp):
            junk = tagp["junk"]
            ssc = tagp["ss"]
            xn = tagp["xn"]
            act(lambda e: e.activation(out=junk, in_=src_ap, func=AF.Square, accum_out=ssc[:, 0:1]), [src_res], ["R3:junk", "R3:ss"])
            rsqrt_col(ssc[:, 1:2], ssc[:, 0:1], 1.0 / D, eps6, ["R3:ss", "C:cn"], "R3:rstd", ssc[:, 2:3])
            act(lambda e: e.activation(out=xn, in_=src_ap, func=AF.Copy, scale=ssc[:, 1:2]), [src_res, "R3:rstd"], ["R3:xn"])
            for half in range(2):
                bk, br = psbank()
                bkb = bk.bitcast(BF16)
                for c in range(8):
                    cc = half * 8 + c
                    pe(lambda e, c=c, cc=cc, bkb=bkb: e.transpose(out=bkb[:, c * 128:(c + 1) * 128], in_=xn[:, cc * 128:(cc + 1) * 128], identity=identb),
                       ["R3:xn", "C:identb"], [br])
                g3 = vec(gname, half * 8, 8).unsqueeze(2).to_broadcast([128, 8, 128])
                dve(lambda e, half=half, bkb=bkb, g3=g3: e.tensor_tensor(
                    out=dstT[:, half * 8:half * 8 + 8, i * 128:(i + 1) * 128],
                    in0=bkb[:, 0:1024].rearrange("p (c t) -> p c t", c=8), in1=g3, op=ALU.mult),
                    [br, "C:vec"], ["HT:t%d" % i])

        def p0(l, blk):
            fence_all()
            RR3.reset()
            XT = [f32buf(RR3, D) for _ in range(2)]
            tagp = dict(junk=bfbuf(RR3, D), ss=f32buf(RR3, 8), xn=bfbuf(RR3, D))
            for i, g in enumerate(blk["tiles"]):
                s = i % 2
                sp_dma(XT[s], xsrc(l, g), [yres(l - 1, g)] if l > 0 else [], ["R3:xt%d" % s], "xt%d" % s)
                norm_transpose(XT[s], "R3:xt%d" % s, None, hT, i, "gpm", RR3, tagp)

        def mm_fm(wt, wres, kidx, c0, inT, in_res_fn, t0, n, nk):
            bk, br = psbank()
            for k in range(nk):
                pe(lambda e, k=k, bk=bk: e.matmul(bk[:, 0:n], lhsT=wt[:, kidx + k, c0:c0 + 128], rhs=inT[:, k, t0:t0 + n],
                                                  start=(k == 0), stop=(k == nk - 1)),
                   [wres] + in_res_fn(t0, n), [br])
            return bk, br

        def p1_hgrn(l, blk, bi):
            lb = DER[:, 0:8]
            oml = DER[:, 8:16]
            for h in range(HGH):
                P.fence("R3")
                RR3.reset()
                wA_, wAr = wload(w_hg[l, h, 0], 4096)
                wA = wA_.rearrange("p (k c) -> p k c", k=16)
                wB_, wBr = wload(w_hg[l, h, 1], 4096)
                wB = wB_.rearrange("p (k c) -> p k c", k=16)
                NB = 512
                SQ = f32buf(RR3, NB); FF = f32buf(RR3, NB); SOG = f32buf(RR3, NB); KF = f32buf(RR3, NB)
                BC = f32buf(RR3, NB); TM = f32buf(RR3, NB); TM2 = f32buf(RR3, NB); ORAW = f32buf(RR3, NB)
                QT = bfbuf(RR3, NB); KT = bfbuf(RR3, NB); KHT = bfbuf(RR3, NB)
                V = bfbuf(RR3, NB).rearrange("p (i e) -> p i e", i=4)
                KH = bfbuf(RR3, NB).rearrange("p (i e) -> p i e", i=4)
                AT = bfbuf(RR3, 128)
                VB = bfbuf(RR3, 2048).rearrange("p (s e) -> p s e", s=16)
                GAM = f32buf(RR3, 16)
                SB = [bfbuf(RR3, 128) for _ in range(2)]
                SS = f32buf(RR3, 2048).rearrange("p (s e) -> p s e", s=16)
                SSb = bfbuf(RR3, 2048).rearrange("p (s e) -> p s e", s=16)
                sbc = [0]
                for tg in blk["tg"]:
                    t0, n, kind, nt = tg["t0"], tg["n"], tg["kind"], tg["nt"]
                    for (dst, wt, wr, c0, fn, nm) in [(SQ, wA, wAr, 0, AF.Silu, "sq"), (FF, wA, wAr, 128, AF.Sigmoid, "ff"), (SOG, wB, wBr, 0, AF.Silu, "sog")]:
                        bk, br = mm_fm(wt, wr, 0, c0, hT, hres, t0, n, 16)
                        act(lambda e, dst=dst, bk=bk, fn=fn: e.activation(out=dst[:, 0:n], in_=bk[:, 0:n], func=fn), [br], ["R3:" + nm])
                    bk, br = psbank()
                    for i in range(nt):
                        for k in range(16):
                            pe(lambda e, i=i, k=k, bk=bk: e.matmul(bk[:, i * 128:(i + 1) * 128], lhsT=hT[:, k, t0 + i * 128:t0 + (i + 1) * 128],
                                                                   rhs=wB[:, k, 128:256], start=(k == 0), stop=(k == 15)),
                               [wBr] + hres(t0 + i * 128, 128), [br])
                    dve(lambda e, bk=bk: e.tensor_copy(out=V[:, 0:nt, :], in_=bk[:, 0:n].rearrange("p (i e) -> p i e", i=nt)), [br], ["R3:v"])
                    dve(lambda e: e.tensor_scalar(out=FF[:, 0:n], in0=FF[:, 0:n], scalar1=oml[:, h:h + 1], scalar2=lb[:, h:h + 1], op0=ALU.mult, op1=ALU.add),
                        ["R3:ff", "C:der"], ["R3:ff"])
                    dve(lambda e: e.tensor_scalar(out=KF[:, 0:n], in0=FF[:, 0:n], scalar1=-1.0, scalar2=1.0, op0=ALU.mult, op1=ALU.add), ["R3:ff"], ["R3:kf"])
                    dve(lambda e: e.tensor_scalar_max(out=FF[:, 0:n], in0=FF[:, 0:n], scalar1=1e-30), ["R3:ff"], ["R3:ff"])
                    act(lambda e: e.activation(out=FF[:, 0:n], in_=FF[:, 0:n], func=AF.Ln), ["R3:ff"], ["R3:ff"])
                    if kind == "p":
                        for c in range(nt * 2):
                            dve(lambda e, c=c: e.tensor_tensor_scan(out=BC[:, c * 64:(c + 1) * 64], data0=cst("ones")[:, 0:64], data1=FF[:, c * 64:(c + 1) * 64],
                                                                    initial=0.0, op0=ALU.mult, op1=ALU.add), ["R3:ff", "C:cn"], ["R3:bc"])
                        nseg_t, sl = 2, 64
                    else:
                        dve(lambda e: e.tensor_tensor_scan(out=BC[:, 0:128], data0=cst("pat8"), data1=FF[:, 0:128], initial=0.0, op0=ALU.mult, op1=ALU.add),
                            ["R3:ff", "C:cn"], ["R3:bc"])
                        nseg_t, sl = 16, 8
                    nseg = nseg_t * nt
                    act(lambda e: e.activation(out=TM[:, 0:n], in_=BC[:, 0:n], func=AF.Exp), ["R3:bc"], ["R3:tm"])
                    dve(lambda e: e.tensor_tensor(out=QT[:, 0:n], in0=SQ[:, 0:n], in1=TM[:, 0:n], op=ALU.mult), ["R3:sq", "R3:tm"], ["R3:qt"])
                    act(lambda e: e.activation(out=TM2[:, 0:n], in_=BC[:, 0:n], func=AF.Exp, scale=-1.0), ["R3:bc"], ["R3:tm2"])
                    dve(lambda e: e.tensor_tensor(out=KT[:, 0:n], in0=KF[:, 0:n], in1=TM2[:, 0:n], op=ALU.mult), ["R3:kf", "R3:tm2"], ["R3:kt"])
                    B3 = BC[:, 0:n].rearrange("p (s j) -> p s j", j=sl)
                    bend = B3[:, :, sl - 1:sl]
                    dve(lambda e: e.tensor_tensor(out=TM[:, 0:n].rearrange("p (s j) -> p s j", j=sl), in0=bend.to_broadcast([128, nseg, sl]), in1=B3, op=ALU.subtract),
                        ["R3:bc", "R3:qt"], ["R3:tm"])
                    act(lambda e: e.activation(out=TM[:, 0:n], in_=TM[:, 0:n], func=AF.Exp), ["R3:tm"], ["R3:tm"])
                    dve(lambda e: e.tensor_tensor(out=KHT[:, 0:n], in0=KF[:, 0:n], in1=TM[:, 0:n], op=ALU.mult), ["R3:kf", "R3:tm"], ["R3:kht"])
                    act(lambda e: e.activation(out=GAM[:, 0:nseg].unsqueeze(2), in_=bend, func=AF.Exp), ["R3:bc"], ["R3:gam"])
                    dve(lambda e: e.tensor_scalar(out=SOG[:, 0:n], in0=SOG[:, 0:n], scalar1=vec("hgn", h), scalar2=None, op0=ALU.mult), ["R3:sog", "C:vec"], ["R3:sog"])
                    bk, br = psbank()
                    bkb = bk.bitcast(BF16)
                    for i in range(nt):
                        pe(lambda e, i=i, bkb=bkb: e.transpose(out=bkb[:, i * 128:(i + 1) * 128], in_=KHT[:, i * 128:(i + 1) * 128], identity=identb),
                           ["R3:kht", "C:identb"], [br])
                    dve(lambda e, bkb=bkb: e.tensor_copy(out=KH[:, 0:nt, :], in_=bkb[:, 0:n].rearrange("p (i e) -> p i e", i=nt)), [br], ["R3:kh"])
                    if kind == "s":
                        sp_dma(SS, st_hg[l, :, h].rearrange("s d e -> d s e"), [], ["R3:ss_"], "hgss")
                        act(lambda e: e.activation(out=SSb, in_=SS, func=AF.Copy), ["R3:ss_"], ["R3:ssb"])
                    for i in range(nt):
                        cs = slice(i * 128, (i + 1) * 128)
                        bkA, brA = psbank()
                        pe(lambda e, bkA=bkA, cs=cs: e.matmul(bkA[:, 0:128], lhsT=KT[:, cs], rhs=QT[:, cs], start=True, stop=True), ["R3:kt", "R3:qt"], [brA])
                        mk = cst("hgM_p") if kind == "p" else cst("hgM_s")
                        dve(lambda e, bkA=bkA, mk=mk: e.tensor_tensor(out=AT, in0=bkA[:, 0:128], in1=mk, op=ALU.mult), [brA, "C:cn"], ["R3:at"])
                        ind = cst("ind2") if kind == "p" else cst("ind16")
                        dve(lambda e, i=i, ind=ind: e.tensor_tensor(out=VB[:, 0:nseg_t, :], in0=V[:, i:i + 1, :].to_broadcast([128, nseg_t, 128]),
                                                                    in1=ind.unsqueeze(2).to_broadcast([128, nseg_t, 128]), op=ALU.mult), ["R3:v", "C:cn"], ["R3:vb"])
                        bkO, brO = psbank()
                        pe(lambda e, bkO=bkO, i=i: e.matmul(bkO[:, 0:128], lhsT=V[:, i, :], rhs=AT, start=True, stop=False), ["R3:v", "R3:at"], [brO])
                        if kind == "p":
                            bkD, brD = psbank()
                            pe(lambda e, bkD=bkD, i=i: e.matmul(bkD[:, 0:256], lhsT=KH[:, i, :], rhs=VB[:, 0:2, :].rearrange("p s e -> p (s e)"), start=True, stop=True),
                               ["R3:kh", "R3:vb"], [brD])
                            for sg in range(2):
                                sb = SB[sbc[0] % 2]
                                sbr = "R3:sb%d" % (sbc[0] % 2)
                                sbc[0] += 1
                                act(lambda e, sb=sb: e.activation(out=sb, in_=HGS[:, h, :], func=AF.Copy), ["C:hgs"], [sbr])
                                cc = slice(i * 128 + sg * 64, i * 128 + sg * 64 + 64)
                                pe(lambda e, bkO=bkO, sb=sb, cc=cc, sg=sg: e.matmul(bkO[:, sg * 64:(sg + 1) * 64], lhsT=sb, rhs=QT[:, cc], start=False, stop=(sg == 1)),
                                   [sbr, "R3:qt"], [brO])
                                gi = i * 2 + sg
                                dve(lambda e, bkD=bkD, sg=sg, gi=gi: e.scalar_tensor_tensor(out=HGS[:, h, :], in0=HGS[:, h, :], scalar=GAM[:, gi:gi + 1],
                                                                                            in1=bkD[:, sg * 128:(sg + 1) * 128], op0=ALU.mult, op1=ALU.add),
                                    ["C:hgs", "R3:gam", brD], ["C:hgs"])
                        else:
                            for sg in range(16):
                                pe(lambda e, bkO=bkO, sg=sg: e.matmul(bkO[:, sg * 8:(sg + 1) * 8], lhsT=SSb[:, sg, :], rhs=QT[:, sg * 8:(sg + 1) * 8], start=False, stop=(sg == 15)),
                                   ["R3:ssb", "R3:qt"], [brO])
                            dve(lambda e: e.tensor_tensor(out=SS, in0=SS, in1=GAM[:, 0:16].unsqueeze(2).to_broadcast([128, 16, 128]), op=ALU.mult),
                                ["R3:ss_", "R3:gam", "R3:ssb"], ["R3:ss_"])
                            for g4 in range(4):
                                bkD, brD = psbank()
                                pe(lambda e, bkD=bkD, g4=g4: e.matmul(bkD[:, 0:512], lhsT=KH[:, 0, :], rhs=VB[:, g4 * 4:(g4 + 1) * 4, :].rearrange("p s e -> p (s e)"),
                                                                      start=True, stop=True), ["R3:kh", "R3:vb"], [brD])
                                dve(lambda e, bkD=bkD, g4=g4: e.tensor_tensor(out=SS[:, g4 * 4:(g4 + 1) * 4, :], in0=SS[:, g4 * 4:(g4 + 1) * 4, :],
                                                                              in1=bkD[:, 0:512].rearrange("p (s e) -> p s e", s=4), op=ALU.add),
                                    ["R3:ss_", brD], ["R3:ss_"])
                            sp_dma(o_hg_s[l, :, h].rearrange("s d e -> d s e"), SS, ["R3:ss_"], [], "hgss")
                        act(lambda e, bkO=bkO, cs=cs: e.activation(out=ORAW[:, cs], in_=bkO[:, 0:128], func=AF.Copy), [brO], ["R3:oraw"])
                    act(lambda e: e.activation(out=TM2[:, 0:n], in_=ORAW[:, 0:n], func=AF.Square), ["R3:oraw"], ["R3:tm2"])
                    bk, br = psbank()
                    pe(lambda e, bk=bk: e.matmul(bk[:, 0:n], lhsT=cst("ones"), rhs=TM2[:, 0:n], start=True, stop=True), ["R3:tm2", "C:cn"], [br])
                    act(lambda e, bk=bk: e.activation(out=TM[:, 0:n], in_=bk[:, 0:n], func=AF.Sqrt, bias=eps6, scale=1.0 / 128), [br, "C:cn"], ["R3:tm"])
                    dve(lambda e: e.reciprocal(out=TM[:, 0:n], in_=TM[:, 0:n]), ["R3:tm"], ["R3:tm"])
                    dve(lambda e: e.tensor_tensor(out=TM[:, 0:n], in0=TM[:, 0:n], in1=ORAW[:, 0:n], op=ALU.mult), ["R3:tm", "R3:oraw"], ["R3:tm"])
                    dve(lambda e: e.tensor_tensor(out=o_all[:, h, t0:t0 + n], in0=TM[:, 0:n], in1=SOG[:, 0:n], op=ALU.mult), ["R3:tm", "R3:sog"], ["OA:c%d" % h])
                if bi == 1:
                    sp_dma(o_hg_p[l, h], HGS[:, h, :], ["C:hgs"], [], "hgs_out%d" % h)

        def p1_rwkv(l, blk, bi):
            omka = DER[:, 16:24]
            P.fence_group(["R3", "RA", "RB", "GB", "W"])
            RR3.reset()
            LORA = bfbuf(RR3, 2 * 1152).rearrange("p (c t) -> p c t", c=2)
            SHs = f32buf(RR3, 26 * 16).rearrange("p (c s) -> p c s", c=26)
            LASTs = f32buf(RR3, 26 * 16).rearrange("p (c s) -> p c s", c=26)
            RAWSET = []
            for _g in range(2):
                RAWSET.append((f32buf(RR3, 3 * 264).rearrange("p (j t) -> p j t", j=3),
                               f32buf(RR3, 3 * 144).rearrange("p (j s t) -> p j s t", j=3, s=16), "R3:g%d" % _g))
            mark1 = RR3.cur
            SETSZ = ((SZ_R3 - mark1) // 2) // 32 * 32
            RGB.reset()
            W2b = bfbuf(RGB, 1024); A2b = bfbuf(RGB, 1024); G2b = bfbuf(RGB, 1024)
            P.dma("pool", lambda e: e.dma_start(out=W2b[0:64, :], in_=w2_d[l]), [], ["GB:w2b"], "w2b")
            P.dma("pool", lambda e: e.dma_start(out=A2b[64:128, :], in_=a2_d[l]), [], ["GB:a2b"], "a2b")
            P.dma("pool", lambda e: e.dma_start(out=G2b, in_=g2_d[l]), [], ["GB:g2b"], "g2b")
            has_s = any(tg["kind"] == "s" for tg in blk["tg"])
            rw_tgs = []
            for tg in blk["tg"]:
                if tg["kind"] == "p":
                    for hh in range(2):
                        rw_tgs.append(dict(kind="p", g0=tg["g0"] + 2 * hh, nt=2, t0=tg["t0"] + 256 * hh, n=256))
                else:
                    rw_tgs.append(tg)
            if has_s:
                SHI = f32buf(RR3, SHW)
                sp_dma(SHI[0:16, :], st_sh[l], [], ["R3:shi"], "shi")
                for c0 in range(0, 26, 4):
                    ncn = min(4, 26 - c0)
                    bk, br = psbank()
                    for c in range(ncn):
                        pe(lambda e, c=c, c0=c0, bk=bk: e.matmul(bk[:, c * 16:(c + 1) * 16], lhsT=SHI[0:16, (c0 + c) * 128:(c0 + c + 1) * 128], rhs=identf[0:16, 0:16],
                                                                 start=True, stop=True), ["R3:shi", "C:cn"], [br])
                    dve(lambda e, c0=c0, ncn=ncn, bk=bk: e.tensor_copy(out=SHs[:, c0:c0 + ncn, :], in_=bk[:, 0:ncn * 16].rearrange("p (c s) -> p c s", c=ncn)), [br], ["R3:shs"])

            def proj_chunk(wt, wr, c0, chunk, tg, dstX, dres, j, rawset):
                RAW, RAWs, rpre = rawset
                t0, n, kind = tg["t0"], tg["n"], tg["kind"]
                bk, br = mm_fm(wt, wr, 0, c0, hT, hres, t0, n, 16)
                mu_c = vec("mu", chunk)
                rr = rpre + "raw%d" % j
                pv = "C:prvc%d" % chunk
                if kind == "p":
                    if t0 == 0:
                        act(lambda e: e.activation(out=RAW[:, j, 0:1], in_=PRVC[:, chunk:chunk + 1], func=AF.Copy), [pv], [rr])
                    act(lambda e, bk=bk: e.activation(out=RAW[:, j, 1:1 + n], in_=bk[:, 0:n], func=AF.Copy), [br], [rr])
                    dve(lambda e: e.tensor_tensor(out=dstX[:, 0:n], in0=RAW[:, j, 0:n], in1=RAW[:, j, 1:1 + n], op=ALU.subtract), [rr], [dres])
                    dve(lambda e: e.scalar_tensor_tensor(out=dstX[:, 0:n], in0=dstX[:, 0:n], scalar=mu_c, in1=RAW[:, j, 1:1 + n], op0=ALU.mult, op1=ALU.add),
                        [dres, rr, "C:vec"], [dres])
                    act(lambda e: e.activation(out=PRVC[:, chunk:chunk + 1], in_=RAW[:, j, n:n + 1], func=AF.Copy), [rr], [pv])
                    act(lambda e: e.activation(out=RAW[:, j, 0:1], in_=PRVC[:, chunk:chunk + 1], func=AF.Copy), [pv, dres], [rr])
                else:
                    rs_ = rpre + "raws%d" % j
                    act(lambda e: e.activation(out=RAWs[:, j, :, 0:1], in_=SHs[:, chunk, :].unsqueeze(2), func=AF.Copy), ["R3:shs"], [rs_])
                    act(lambda e, bk=bk: e.activation(out=RAWs[:, j, :, 1:9], in_=bk[:, 0:128].rearrange("p (s t) -> p s t", s=16), func=AF.Copy), [br], [rs_])
                    d3 = dstX[:, 0:128].rearrange("p (s t) -> p s t", s=16)
                    dve(lambda e: e.tensor_tensor(out=d3, in0=RAWs[:, j, :, 0:8], in1=RAWs[:, j, :, 1:9], op=ALU.subtract), [rs_], [dres])
                    dve(lambda e: e.scalar_tensor_tensor(out=d3, in0=d3, scalar=mu_c, in1=RAWs[:, j, :, 1:9], op0=ALU.mult, op1=ALU.add),
                        [dres, rs_, "C:vec"], [dres])
                    act(lambda e: e.activation(out=LASTs[:, chunk, :].unsqueeze(2), in_=RAWs[:, j, :, 8:9], func=AF.Copy), [rs_], ["R3:lasts%d" % chunk])

            XL = [f32buf(RR3, 256) for _ in range(2)]
            wl_, wlr = wload(w_rwl[l], 4096)
            wl = wl_.rearrange("p (k c) -> p k c", k=16)
            for tg in rw_tgs:
                t0, n = tg["t0"], tg["n"]
                for j in range(2):
                    proj_chunk(wl, wlr, j * 128, 24 + j, tg, XL[j], "R3:xl%d" % j, j, RAWSET[0])
                act(lambda e, t0=t0, n=n: e.activation(out=LORA[0:64, 0, t0:t0 + n], in_=XL[0][0:64, 0:n], func=AF.Tanh), ["R3:xl0"], ["R3:lora"])
                act(lambda e, t0=t0, n=n: e.activation(out=LORA[64:128, 0, t0:t0 + n], in_=XL[0][64:128, 0:n], func=AF.Copy), ["R3:xl0"], ["R3:lora"])
                act(lambda e, t0=t0, n=n: e.activation(out=LORA[:, 1, t0:t0 + n], in_=XL[1][:, 0:n], func=AF.Sigmoid), ["R3:xl1"], ["R3:lora"])
            if _STOP == 'rwl': raise _Stop()
            P.fence_group(["R3", "RA", "RB", "W"])

            def unit(p, tg, wrk, wrkr, wv, wvr, RP, setoff, setsz, rawset):
                pc = slice(p * 128, (p + 1) * 128)
                rwsn = "C:rws%d" % p
                t0, n, kind, nt = tg["t0"], tg["n"], tg["kind"], tg["nt"]
                P.fence(RP[:-1])
                RR3.reset(setoff)
                NB = n
                XR = f32buf(RR3, NB); XK = f32buf(RR3, NB); XV = f32buf(RR3, NB)
                S_ = f32buf(RR3, NB); A_ = f32buf(RR3, NB); G_ = f32buf(RR3, NB); KAP = f32buf(RR3, NB)
                CB = f32buf(RR3, NB); T1 = f32buf(RR3, NB); T2 = f32buf(RR3, NB); BON = f32buf(RR3, NB); ORAW = f32buf(RR3, NB)
                QR = bfbuf(RR3, 2 * NB).rearrange("p (c t) -> p c t", c=2)
                KB = bfbuf(RR3, 2 * NB).rearrange("p (c t) -> p c t", c=2)
                VXb = bfbuf(RR3, NB)
                TMv = bfbuf(RR3, 3 * NB).rearrange("p (c i e) -> p c i e", c=3, i=nt)
                GAM = f32buf(RR3, 16)
                Pm = [bfbuf(RR3, 128) for _ in range(2 * nt)]; PTm = [bfbuf(RR3, 128) for _ in range(2 * nt)]
                TTf = [f32buf(RR3, 128) for _ in range(2 * nt)]; TTb = [bfbuf(RR3, 128) for _ in range(2 * nt)]
                AKK = [bfbuf(RR3, 128) for _ in range(2 * nt)]; ARK = [bfbuf(RR3, 128) for _ in range(2 * nt)]; ARB = [bfbuf(RR3, 128) for _ in range(2 * nt)]
                Wb = bfbuf(RR3, 128); Un = bfbuf(RR3, 128)
                Vm = [bfbuf(RR3, 128) for _ in range(2)]; Unm = [bfbuf(RR3, 128) for _ in range(2)]
                Hb = bfbuf(RR3, 128)
                HT1 = f32buf(RR3, 128)
                if kind == "s":
                    SBD0 = f32buf(RR3, 2048).rearrange("p (s e) -> p s e", s=16)
                    HF = f32buf(RR3, 2048).rearrange("p (s e) -> p s e", s=16)
                    HBb = bfbuf(RR3, 2048).rearrange("p (s e) -> p s e", s=16)
                    ZK = bfbuf(RR3, 2048).rearrange("p (s t) -> p s t", s=16)
                    VBK = bfbuf(RR3, 2048).rearrange("p (s e) -> p s e", s=16)
                    UNB = bfbuf(RR3, 2048).rearrange("p (s e) -> p s e", s=16)
                assert RR3.cur <= setoff + setsz, (RR3.cur, setoff, setsz, kind)
                for h2 in range(2):
                    oc = slice((1 - h2) * 64, (2 - h2) * 64)
                    dve(lambda e, h2=h2, oc=oc: e.memset(Vm[h2][:, oc], 0.0), [], [RP + "vm%d" % h2])
                    dve(lambda e, h2=h2, oc=oc: e.memset(Unm[h2][:, oc], 0.0), [], [RP + "unm%d" % h2])
                proj_chunk(wrk, wrkr, 0, p, tg, XR, RP + "xr", 0, rawset)
                yield
                proj_chunk(wrk, wrkr, 128, 8 + p, tg, XK, RP + "xk", 1, rawset)
                yield
                proj_chunk(wv, wvr, 0, 16 + p, tg, XV, RP + "xv", 2, rawset)
                yield
                bk, br = psbank()
                pe(lambda e, bk=bk: e.matmul(bk[:, 0:n], lhsT=W2b[0:64, pc], rhs=LORA[0:64, 0, t0:t0 + n], start=True, stop=True), ["GB:w2b", "R3:lora"], [br])
                act(lambda e, bk=bk: e.activation(out=S_[:, 0:n], in_=bk[:, 0:n], func=AF.Sigmoid, bias=vec("w0", p), scale=1.0), [br, "C:vec"], [RP + "s"])
                bk, br = psbank()
                pe(lambda e, bk=bk: e.matmul(bk[:, 0:n], lhsT=A2b[64:128, pc], rhs=LORA[64:128, 0, t0:t0 + n], start=True, stop=True), ["GB:a2b", "R3:lora"], [br])
                act(lambda e, bk=bk: e.activation(out=A_[:, 0:n], in_=bk[:, 0:n], func=AF.Sigmoid, bias=vec("a0", p), scale=1.0), [br, "C:vec"], [RP + "a"])
                bk, br = psbank()
                pe(lambda e, bk=bk: e.matmul(bk[:, 0:n], lhsT=G2b[:, pc], rhs=LORA[:, 1, t0:t0 + n], start=True, stop=True), ["GB:g2b", "R3:lora"], [br])
                act(lambda e, bk=bk: e.activation(out=G_[:, 0:n], in_=bk[:, 0:n], func=AF.Copy), [br], [RP + "g"])
                yield
                dve(lambda e: e.tensor_scalar(out=KAP[:, 0:n], in0=XK[:, 0:n], scalar1=vec("kk", p), scalar2=None, op0=ALU.mult), [RP + "xk", "C:vec"], [RP + "kap"])
                dve(lambda e: e.tensor_tensor(out=T1[:, 0:n], in0=KAP[:, 0:n], in1=KAP[:, 0:n], op=ALU.mult), [RP + "kap"], [RP + "t1"])
                bk, br = psbank()
                pe(lambda e, bk=bk: e.matmul(bk[:, 0:n], lhsT=cst("bd2"), rhs=T1[:, 0:n], start=True, stop=True), [RP + "t1", "C:cn"], [br])
                dve(lambda e, bk=bk: e.tensor_scalar_max(out=T2[:, 0:n], in0=bk[:, 0:n], scalar1=1e-24), [br], [RP + "t2"])
                act(lambda e: e.activation(out=T2[:, 0:n], in_=T2[:, 0:n], func=AF.Sqrt), [RP + "t2"], [RP + "t2"])
                dve(lambda e: e.reciprocal(out=T2[:, 0:n], in_=T2[:, 0:n]), [RP + "t2"], [RP + "t2"])
                dve(lambda e: e.tensor_tensor(out=KAP[:, 0:n], in0=KAP[:, 0:n], in1=T2[:, 0:n], op=ALU.mult), [RP + "kap", RP + "t2"], [RP + "kap"])
                yield
                dve(lambda e: e.tensor_scalar(out=T1[:, 0:n], in0=A_[:, 0:n], scalar1=vec("ka", p), scalar2=omka[:, p:p + 1], op0=ALU.mult, op1=ALU.add),
                    [RP + "a", "C:vec", "C:der"], [RP + "t1"])
                dve(lambda e: e.tensor_tensor(out=XK[:, 0:n], in0=XK[:, 0:n], in1=T1[:, 0:n], op=ALU.mult), [RP + "xk", RP + "t1", RP + "kap"], [RP + "xk"])
                yield
                dve(lambda e: e.tensor_tensor(out=T1[:, 0:n], in0=XR[:, 0:n], in1=XK[:, 0:n], op=ALU.mult), [RP + "xr", RP + "xk"], [RP + "t1"])
                dve(lambda e: e.tensor_scalar(out=T1[:, 0:n], in0=T1[:, 0:n], scalar1=vec("rk", p), scalar2=None, op0=ALU.mult), [RP + "t1", "C:vec"], [RP + "t1"])
                bk, br = psbank()
                pe(lambda e, bk=bk: e.matmul(bk[:, 0:n], lhsT=cst("bd2"), rhs=T1[:, 0:n], start=True, stop=True), [RP + "t1", "C:cn"], [br])
                dve(lambda e, bk=bk: e.tensor_tensor(out=BON[:, 0:n], in0=XV[:, 0:n], in1=bk[:, 0:n], op=ALU.mult), [RP + "xv", br], [RP + "bon"])
                yield
                dve(lambda e: e.tensor_scalar(out=S_[:, 0:n], in0=S_[:, 0:n], scalar1=-DEC_C, scalar2=None, op0=ALU.mult), [RP + "s"], [RP + "s"])
                if kind == "p":
                    for i in range(nt):
                        dve(lambda e, i=i: e.tensor_tensor_scan(out=CB[:, i * 128:(i + 1) * 128], data0=cst("ones"), data1=S_[:, i * 128:(i + 1) * 128],
                                                                initial=0.0, op0=ALU.mult, op1=ALU.add), [RP + "s", "C:cn"], [RP + "cb"])
                    nseg_t, sl = 1, 128
                else:
                    dve(lambda e: e.tensor_tensor_scan(out=CB[:, 0:128], data0=cst("pat8"), data1=S_[:, 0:128], initial=0.0, op0=ALU.mult, op1=ALU.add),
                        [RP + "s", "C:cn"], [RP + "cb"])
                    nseg_t, sl = 16, 8
                nseg = nseg_t * nt
                act(lambda e: e.activation(out=T1[:, 0:n], in_=CB[:, 0:n], func=AF.Exp), [RP + "cb", RP + "t1"], [RP + "t1"])
                dve(lambda e: e.tensor_tensor(out=QR[:, 1, 0:n], in0=XR[:, 0:n], in1=T1[:, 0:n], op=ALU.mult), [RP + "xr", RP + "t1"], [RP + "qr1"])
                act(lambda e: e.activation(out=GAM[:, 0:nseg].unsqueeze(2), in_=T1[:, 0:n].rearrange("p (s j) -> p s j", j=sl)[:, :, sl - 1:sl], func=AF.Copy),
                    [RP + "t1"], [RP + "gam"])
                dve(lambda e: e.tensor_tensor(out=T2[:, 0:n], in0=CB[:, 0:n], in1=S_[:, 0:n], op=ALU.subtract), [RP + "cb", RP + "s", RP + "t2"], [RP + "t2"])
                act(lambda e: e.activation(out=T2[:, 0:n], in_=T2[:, 0:n], func=AF.Exp), [RP + "t2"], [RP + "t2"])
                dve(lambda e: e.tensor_tensor(out=QR[:, 0, 0:n], in0=KAP[:, 0:n], in1=T2[:, 0:n], op=ALU.mult), [RP + "kap", RP + "t2"], [RP + "qr0"])
                act(lambda e: e.activation(out=T1[:, 0:n], in_=CB[:, 0:n], func=AF.Exp, scale=-1.0), [RP + "cb", RP + "qr1", RP + "gam"], [RP + "t1"])
                dve(lambda e: e.tensor_tensor(out=KB[:, 0, 0:n], in0=XK[:, 0:n], in1=T1[:, 0:n], op=ALU.mult), [RP + "xk", RP + "t1"], [RP + "kb0"])
                dve(lambda e: e.tensor_tensor(out=T2[:, 0:n], in0=A_[:, 0:n], in1=KAP[:, 0:n], op=ALU.mult), [RP + "a", RP + "kap", RP + "qr0"], [RP + "t2"])
                dve(lambda e: e.tensor_tensor(out=KB[:, 1, 0:n], in0=T2[:, 0:n], in1=T1[:, 0:n], op=ALU.mult), [RP + "t2", RP + "t1"], [RP + "kb1"])
                act(lambda e: e.activation(out=VXb[:, 0:n], in_=XV[:, 0:n], func=AF.Copy), [RP + "xv"], [RP + "vxb"])
                yield
                for ci, (src, sres) in enumerate([(VXb, RP + "vxb"), (KB[:, 0, :], RP + "kb0"), (KB[:, 1, :], RP + "kb1")]):
                    bk, br = psbank()
                    bkb = bk.bitcast(BF16)
                    for i in range(nt):
                        pe(lambda e, i=i, bkb=bkb, src=src: e.transpose(out=bkb[:, i * 128:(i + 1) * 128], in_=src[:, i * 128:(i + 1) * 128], identity=identb),
                           [sres, "C:identb"], [br])
                    dve(lambda e, bkb=bkb, ci=ci: e.tensor_copy(out=TMv[:, ci, 0:nt, :], in_=bkb[:, 0:n].rearrange("p (i e) -> p i e", i=nt)), [br], [RP + "tmv%d" % ci])
                if kind == "s":
                    dve(lambda e: e.memset(SBD0, 0.0), [], [RP + "sbd0"])
                    dve(lambda e: e.memset(ZK, 0.0), [], [RP + "zk"])
                    for h2 in range(2):
                        sp_dma(SBD0[h2 * 64:(h2 + 1) * 64, :, h2 * 64:(h2 + 1) * 64], st_rw[l, :, 2 * p + h2].rearrange("s i j -> i s j"), [RP + "sbd0"], [RP + "sbd0_%d" % h2], "sbd%d" % h2)
                    for g4 in range(4):
                        bk, br = psbank()
                        for s4 in range(4):
                            sg = g4 * 4 + s4
                            pe(lambda e, bk=bk, s4=s4, sg=sg: e.matmul(bk[:, s4 * 128:(s4 + 1) * 128], lhsT=SBD0[:, sg, :], rhs=identf, start=True, stop=True),
                               [RP + "sbd0", RP + "sbd0_0", RP + "sbd0_1", "C:cn"], [br])
                        dve(lambda e, bk=bk, g4=g4: e.tensor_copy(out=HF[:, g4 * 4:(g4 + 1) * 4, :], in_=bk[:, 0:512].rearrange("p (s e) -> p s e", s=4)),
                            [br], [RP + "hf"])
                        act(lambda e, g4=g4: e.activation(out=HBb[:, g4 * 4:(g4 + 1) * 4, :], in_=HF[:, g4 * 4:(g4 + 1) * 4, :], func=AF.Copy),
                            [RP + "hf"], [RP + "hbb"])
                for i in range(nt):
                    cs = slice(i * 128, (i + 1) * 128)
                    sfx = "_p" if kind == "p" else "_s"
                    nlev = 7 if kind == "p" else 3
                    for h2 in range(2):
                        yield
                        q = i * 2 + h2
                        hr = slice(h2 * 64, (h2 + 1) * 64)
                        bk1, br1 = psbank()
                        pe(lambda e, bk1=bk1, hr=hr, cs=cs: e.matmul(bk1[:, 0:128], lhsT=QR[hr, 0, cs], rhs=KB[hr, 1, cs], start=True, stop=True), [RP + "qr0", RP + "kb1"], [br1])
                        dve(lambda e, bk1=bk1, h2=h2: e.scalar_tensor_tensor(out=Pm[q], in0=bk1[:, 0:128], scalar=-1.0, in1=cst("rwL" + sfx), op0=ALU.mult, op1=ALU.mult),
                            [br1, "C:cn"], [RP + "pm%d" % q])
                        bk2, br2 = psbank()
                        pe(lambda e, bk2=bk2, hr=hr, cs=cs: e.matmul(bk2[:, 0:256].rearrange("p (c t) -> p c t", c=2), lhsT=KB[hr, 1, cs], rhs=QR[hr, :, cs], start=True, stop=True),
                           [RP + "qr0", RP + "qr1", RP + "kb1"], [br2])
                        dve(lambda e, bk2=bk2, h2=h2: e.scalar_tensor_tensor(out=PTm[q], in0=bk2[:, 0:128], scalar=-1.0, in1=cst("rwST" + sfx), op0=ALU.mult, op1=ALU.mult),
                            [br2, "C:cn"], [RP + "ptm%d" % q])
                        dve(lambda e, bk2=bk2, h2=h2: e.tensor_tensor(out=ARB[q], in0=bk2[:, 128:256], in1=cst("rwIT" + sfx), op=ALU.mult), [br2, "C:cn"], [RP + "arb%d" % q])
                        bk3, br3 = psbank()
                        pe(lambda e, bk3=bk3, hr=hr, cs=cs: e.matmul(bk3[:, 0:256].rearrange("p (c t) -> p c t", c=2), lhsT=KB[hr, 0, cs], rhs=QR[hr, :, cs], start=True, stop=True),
                           [RP + "qr0", RP + "qr1", RP + "kb0"], [br3])
                        dve(lambda e, bk3=bk3, h2=h2: e.tensor_tensor(out=AKK[q], in0=bk3[:, 0:128], in1=cst("rwST" + sfx), op=ALU.mult), [br3, "C:cn"], [RP + "akk%d" % q])
                        dve(lambda e, bk3=bk3, h2=h2: e.tensor_tensor(out=ARK[q], in0=bk3[:, 128:256], in1=cst("rwIT" + sfx), op=ALU.mult), [br3, "C:cn"], [RP + "ark%d" % q])
                        dve(lambda e, h2=h2: e.tensor_tensor(out=TTf[q], in0=PTm[q], in1=identf, op=ALU.add), [RP + "ptm%d" % q, "C:cn"], [RP + "ttf%d" % q])
                        act(lambda e, h2=h2: e.activation(out=TTb[q], in_=TTf[q], func=AF.Copy), [RP + "ttf%d" % q], [RP + "ttb%d" % q])
                for lev in range(1, nlev):
                    for i in range(nt):
                        for h2 in range(2):
                            yield
                            q = i * 2 + h2
                            bka, bra = psbank()
                            pe(lambda e, bka=bka, h2=h2: e.matmul(bka[:, 0:128], lhsT=PTm[q], rhs=Pm[q], start=True, stop=True), [RP + "pm%d" % q, RP + "ptm%d" % q], [bra])
                            pe(lambda e, bka=bka, h2=h2: e.matmul(bka[:, 128:256], lhsT=Pm[q], rhs=PTm[q], start=True, stop=True), [RP + "pm%d" % q, RP + "ptm%d" % q], [bra])
                            act(lambda e, bka=bka, h2=h2: e.activation(out=Pm[q], in_=bka[:, 0:128], func=AF.Copy), [bra], [RP + "pm%d" % q])
                            dve(lambda e, bka=bka, h2=h2: e.tensor_copy(out=PTm[q], in_=bka[:, 128:256]), [bra], [RP + "ptm%d" % q])
                            bkt, brt = psbank()
                            pe(lambda e, bkt=bkt, h2=h2: e.matmul(bkt[:, 0:128], lhsT=Pm[q], rhs=TTb[q], start=True, stop=True), [RP + "pm%d" % q, RP + "ttb%d" % q], [brt])
                            dve(lambda e, bkt=bkt, h2=h2: e.tensor_tensor(out=TTf[q], in0=TTf[q], in1=bkt[:, 0:128], op=ALU.add), [RP + "ttf%d" % q, brt], [RP + "ttf%d" % q])
                            act(lambda e, h2=h2: e.activation(out=TTb[q], in_=TTf[q], func=AF.Copy), [RP + "ttf%d" % q], [RP + "ttb%d" % q])
                for i in range(nt):
                    cs = slice(i * 128, (i + 1) * 128)
                    yield
                    bkW, brW = psbank()
                    if kind == "p":
                        act(lambda e: e.activation(out=Hb, in_=RWS[:, p, :], func=AF.Copy), [rwsn], [RP + "hb"])
                        pe(lambda e, bkW=bkW, cs=cs: e.matmul(bkW[:, 0:128], lhsT=QR[:, 0, cs], rhs=Hb, start=True, stop=False), [RP + "qr0", RP + "hb"], [brW])
                    else:
                        for sg in range(16):
                            dve(lambda e, sg=sg: e.tensor_copy(out=ZK[:, sg, sg * 8:(sg + 1) * 8], in_=QR[:, 0, sg * 8:(sg + 1) * 8]), [RP + "qr0"], [RP + "zk"])
                        for sg in range(16):
                            pe(lambda e, bkW=bkW, sg=sg: e.matmul(bkW[:, 0:128], lhsT=ZK[:, sg, :], rhs=HBb[:, sg, :], start=(sg == 0), stop=False), [RP + "zk", RP + "hbb"], [brW])
                    for h2 in range(2):
                        hc = slice(h2 * 64, (h2 + 1) * 64)
                        pe(lambda e, bkW=bkW, h2=h2, hc=hc, i=i: e.matmul(bkW[:, hc], lhsT=AKK[i * 2 + h2], rhs=TMv[:, 0, i, hc], start=False, stop=(h2 == 1)), [RP + "akk%d" % (i * 2 + h2), RP + "tmv0"], [brW])
                    act(lambda e, bkW=bkW: e.activation(out=Wb, in_=bkW[:, 0:128], func=AF.Copy), [brW], [RP + "wb"])
                    yield
                    bkU, brU = psbank()
                    for h2 in range(2):
                        hc = slice(h2 * 64, (h2 + 1) * 64)
                        pe(lambda e, bkU=bkU, h2=h2, hc=hc: e.matmul(bkU[:, hc], lhsT=TTb[i * 2 + h2], rhs=Wb[:, hc], start=True, stop=True), [RP + "ttb%d" % (i * 2 + h2), RP + "wb"], [brU])
                    act(lambda e, bkU=bkU: e.activation(out=Un, in_=bkU[:, 0:128], func=AF.Copy, scale=-1.0), [brU], [RP + "un"])
                    yield
                    for h2 in range(2):
                        hc = slice(h2 * 64, (h2 + 1) * 64)
                        oc = slice((1 - h2) * 64, (2 - h2) * 64)
                        dve(lambda e, h2=h2, hc=hc, i=i: e.tensor_copy(out=Vm[h2][:, hc], in_=TMv[:, 0, i, hc]), [RP + "tmv0"], [RP + "vm%d" % h2])
                        dve(lambda e, h2=h2, hc=hc: e.tensor_copy(out=Unm[h2][:, hc], in_=Un[:, hc]), [RP + "un"], [RP + "unm%d" % h2])
                    bkO, brO = psbank()
                    for h2 in range(2):
                        pe(lambda e, bkO=bkO, h2=h2: e.matmul(bkO[:, 0:128], lhsT=Vm[h2], rhs=ARK[i * 2 + h2], start=(h2 == 0), stop=False), [RP + "vm%d" % h2, RP + "ark%d" % (i * 2 + h2)], [brO])
                        pe(lambda e, bkO=bkO, h2=h2: e.matmul(bkO[:, 0:128], lhsT=Unm[h2], rhs=ARB[i * 2 + h2], start=False, stop=False), [RP + "unm%d" % h2, RP + "arb%d" % (i * 2 + h2)], [brO])
                    if kind == "p":
                        pe(lambda e, bkO=bkO, cs=cs: e.matmul(bkO[:, 0:128], lhsT=Hb, rhs=QR[:, 1, cs], start=False, stop=True), [RP + "hb", RP + "qr1"], [brO])
                    else:
                        for sg in range(16):
                            pe(lambda e, bkO=bkO, sg=sg: e.matmul(bkO[:, sg * 8:(sg + 1) * 8], lhsT=HBb[:, sg, :], rhs=QR[:, 1, sg * 8:(sg + 1) * 8], start=False, stop=(sg == 15)),
                               [RP + "hbb", RP + "qr1"], [brO])
                    act(lambda e, bkO=bkO, cs=cs: e.activation(out=ORAW[:, cs], in_=bkO[:, 0:128], func=AF.Copy), [brO], [RP + "oraw"])
                    yield
                    if kind == "p":
                        bkH, brH = psbank()
                        pe(lambda e, bkH=bkH, i=i: e.matmul(bkH[:, 0:128], lhsT=TMv[:, 1, i, :], rhs=TMv[:, 0, i, :], start=True, stop=False), [RP + "tmv1", RP + "tmv0"], [brH])
                        pe(lambda e, bkH=bkH, i=i: e.matmul(bkH[:, 0:128], lhsT=TMv[:, 2, i, :], rhs=Un, start=False, stop=True), [RP + "tmv2", RP + "un"], [brH])
                        dve(lambda e, bkH=bkH: e.tensor_tensor(out=HT1, in0=bkH[:, 0:128], in1=cst("bd2b"), op=ALU.mult), [brH, "C:cn"], [RP + "ht1"])
                        dve(lambda e: e.tensor_tensor(out=HT1, in0=HT1, in1=RWS[:, p, :], op=ALU.add), [RP + "ht1", rwsn], [RP + "ht1"])
                        dve(lambda e, i=i: e.tensor_scalar(out=RWS[:, p, :], in0=HT1, scalar1=GAM[:, i:i + 1], scalar2=None, op0=ALU.mult), [RP + "ht1", RP + "gam", RP + "hb"], [rwsn])
                    else:
                        ind = cst("ind16")
                        dve(lambda e: e.tensor_tensor(out=VBK, in0=TMv[:, 0, 0:1, :].to_broadcast([128, 16, 128]), in1=ind.unsqueeze(2).to_broadcast([128, 16, 128]), op=ALU.mult),
                            [RP + "tmv0", "C:cn"], [RP + "vbk"])
                        dve(lambda e: e.tensor_tensor(out=UNB, in0=Un.unsqueeze(1).to_broadcast([128, 16, 128]), in1=ind.unsqueeze(2).to_broadcast([128, 16, 128]), op=ALU.mult),
                            [RP + "un", "C:cn"], [RP + "unb"])
                        for g4 in range(4):
                            g4s = slice(g4 * 4, (g4 + 1) * 4)
                            bkH, brH = psbank()
                            pe(lambda e, bkH=bkH, g4s=g4s: e.matmul(bkH[:, 0:512], lhsT=TMv[:, 1, 0, :], rhs=VBK[:, g4s, :].rearrange("p s e -> p (s e)"), start=True, stop=False),
                               [RP + "tmv1", RP + "vbk"], [brH])
                            pe(lambda e, bkH=bkH, g4s=g4s: e.matmul(bkH[:, 0:512], lhsT=TMv[:, 2, 0, :], rhs=UNB[:, g4s, :].rearrange("p s e -> p (s e)"), start=False, stop=True),
                               [RP + "tmv2", RP + "unb"], [brH])
                            b3 = bkH[:, 0:512].rearrange("p (s e) -> p s e", s=4)
                            dve(lambda e, b3=b3, g4s=g4s: e.tensor_tensor(out=SBD0[:, g4s, :], in0=b3, in1=cst("bd2b").unsqueeze(1).to_broadcast([128, 4, 128]), op=ALU.mult),
                                [brH, "C:cn", RP + "sbd0_0", RP + "sbd0_1"], [RP + "sbd0"])
                            dve(lambda e, g4s=g4s: e.tensor_tensor(out=HF[:, g4s, :], in0=HF[:, g4s, :], in1=SBD0[:, g4s, :], op=ALU.add), [RP + "sbd0", RP + "hf"], [RP + "hf"])
                        dve(lambda e: e.tensor_tensor(out=HF, in0=HF, in1=GAM[:, 0:16].unsqueeze(2).to_broadcast([128, 16, 128]), op=ALU.mult), [RP + "hf", RP + "gam"], [RP + "hf"])
                        for g4 in range(4):
                            bk, br = psbank()
                            for s4 in range(4):
                                sg = g4 * 4 + s4
                                pe(lambda e, bk=bk, s4=s4, sg=sg: e.matmul(bk[:, s4 * 128:(s4 + 1) * 128], lhsT=HF[:, sg, :], rhs=identf, start=True, stop=True),
                                   [RP + "hf", "C:cn"], [br])
                            dve(lambda e, bk=bk, g4=g4: e.tensor_copy(out=SBD0[:, g4 * 4:(g4 + 1) * 4, :], in_=bk[:, 0:512].rearrange("p (s e) -> p s e", s=4)),
                                [br, RP + "sbd0"], [RP + "sbd0"])
                        for h2 in range(2):
                            sp_dma(o_rw_s[l, :, 2 * p + h2].rearrange("s i j -> i s j"), SBD0[h2 * 64:(h2 + 1) * 64, :, h2 * 64:(h2 + 1) * 64], [RP + "sbd0"], [], "sbd%d" % h2)
                yield
                bk, br = psbank()
                pe(lambda e, bk=bk: e.matmul(bk[:, 0:n], lhsT=cst("bd2"), rhs=ORAW[:, 0:n], start=True, stop=True), [RP + "oraw", "C:cn"], [br])
                act(lambda e, bk=bk: e.activation(out=T1[:, 0:n], in_=bk[:, 0:n], func=AF.Copy, scale=1.0 / 64), [br, RP + "t1"], [RP + "t1"])
                dve(lambda e: e.tensor_tensor(out=ORAW[:, 0:n], in0=ORAW[:, 0:n], in1=T1[:, 0:n], op=ALU.subtract), [RP + "oraw", RP + "t1"], [RP + "oraw"])
                act(lambda e: e.activation(out=T2[:, 0:n], in_=ORAW[:, 0:n], func=AF.Square), [RP + "oraw", RP + "t2"], [RP + "t2"])
                bk, br = psbank()
                pe(lambda e, bk=bk: e.matmul(bk[:, 0:n], lhsT=cst("bd2"), rhs=T2[:, 0:n], start=True, stop=True), [RP + "t2", "C:cn"], [br])
                act(lambda e, bk=bk: e.activation(out=T1[:, 0:n], in_=bk[:, 0:n], func=AF.Sqrt, bias=epsgn, scale=1.0 / 64), [br, "C:cn", RP + "t1"], [RP + "t1"])
                dve(lambda e: e.reciprocal(out=T1[:, 0:n], in_=T1[:, 0:n]), [RP + "t1"], [RP + "t1"])
                dve(lambda e: e.tensor_tensor(out=ORAW[:, 0:n], in0=ORAW[:, 0:n], in1=T1[:, 0:n], op=ALU.mult), [RP + "oraw", RP + "t1"], [RP + "oraw"])
                dve(lambda e: e.tensor_scalar(out=ORAW[:, 0:n], in0=ORAW[:, 0:n], scalar1=vec("lnw", p), scalar2=vec("lnb", p), op0=ALU.mult, op1=ALU.add),
                    [RP + "oraw", "C:vec"], [RP + "oraw"])
                dve(lambda e: e.tensor_tensor(out=ORAW[:, 0:n], in0=ORAW[:, 0:n], in1=BON[:, 0:n], op=ALU.add), [RP + "oraw", RP + "bon"], [RP + "oraw"])
                dve(lambda e: e.tensor_tensor(out=o_all[:, 8 + p, t0:t0 + n], in0=ORAW[:, 0:n], in1=G_[:, 0:n], op=ALU.mult), [RP + "oraw", RP + "g"], ["OA:c%d" % (8 + p)])
                yield

            def pair_gen(p, wrk, wrkr, wv, wvr, RP, setoff, rawset):
                for tg in rw_tgs:
                    if tg["kind"] == "p":
                        yield from unit(p, tg, wrk, wrkr, wv, wvr, RP, setoff, SETSZ, rawset)

            for cpl in range(RWP // 2):
                P.fence_group(["W"])
                pa, pb = 2 * cpl, 2 * cpl + 1
                wts = []
                for gi, p in enumerate((pa, pb)):
                    dst = wslots[gi][:, 0:4096]
                    P.dma("pool", lambda e, dst=dst, p=p: e.dma_start(out=dst, in_=w_rwrk[l, p]), writes=["W:r%d" % gi], slot="Wr%d" % gi)
                    dv = wslots[2][:, gi * 2048:(gi + 1) * 2048]
                    P.dma("pool", lambda e, dv=dv, p=p: e.dma_start(out=dv, in_=w_rwv[l, p]), writes=["W:v%d" % gi], slot="Wv%d" % gi)
                    wts.append((dst.rearrange("p (k c) -> p k c", k=16), "W:r%d" % gi, dv.rearrange("p (k c) -> p k c", k=16), "W:v%d" % gi))
                gens = [pair_gen(pa, *wts[0], "RA:", mark1, RAWSET[0]), pair_gen(pb, *wts[1], "RB:", mark1 + SETSZ, RAWSET[1])]
                alive = list(gens)
                while alive:
                    for g in list(alive):
                        try:
                            next(g)
                        except StopIteration:
                            alive.remove(g)
                if has_s:
                    stg = [tg for tg in rw_tgs if tg["kind"] == "s"][0]
                    for gi, p in enumerate((pa, pb)):
                        P.fence_group(["RA", "RB"])
                        for _ in unit(p, stg, *wts[gi], "RA:", mark1, 2 * SETSZ, RAWSET[gi]):
                            pass
                        P.fence_group(["RA", "RB"])
                if bi == 1:
                    for p in (pa, pb):
                        P.fence_group(["RA", "RB"])
                        RR3.reset(mark1)
                        ST = f32buf(RR3, 128)
                        bk, br = psbank()
                        pe(lambda e, bk=bk, p=p: e.matmul(bk[:, 0:128], lhsT=RWS[:, p, :], rhs=identf, start=True, stop=True), ["C:rws%d" % p, "C:cn"], [br])
                        act(lambda e, bk=bk: e.activation(out=ST, in_=bk[:, 0:128], func=AF.Copy), [br], ["RA:st"])
                        for h2 in range(2):
                            sp_dma(o_rw_p[l, 2 * p + h2], ST[h2 * 64:(h2 + 1) * 64, h2 * 64:(h2 + 1) * 64], ["RA:st"], [], "rwst%d_%d" % (p % 2, h2))
            P.fence_group(["R3", "RA", "RB", "W", "GB"])
            RR3.reset(mark1)
            if has_s:
                OT = f32buf(RR3, SHW)
                for c0 in range(0, 26, 4):
                    ncn = min(4, 26 - c0)
                    bk, br = psbank()
                    for c in range(ncn):
                        pe(lambda e, c=c, c0=c0, bk=bk: e.matmul(bk[0:16, c * 128:(c + 1) * 128], lhsT=LASTs[:, c0 + c, :], rhs=identf, start=True, stop=True), ["R3:lasts%d" % (c0 + c), "C:cn"], [br])
                    act(lambda e, c0=c0, ncn=ncn, bk=bk: e.activation(out=OT[0:16, c0 * 128:(c0 + ncn) * 128], in_=bk[0:16, 0:ncn * 128], func=AF.Copy), [br], ["R3:ot"])
                sp_dma(o_sh_s[l], OT[0:16, :], ["R3:ot"], [], "ot")
            if bi == 1:
                OT2 = f32buf(RR3, SHW)
                for c0 in range(0, 26, 4):
                    ncn = min(4, 26 - c0)
                    bk, br = psbank()
                    for c in range(ncn):
                        pe(lambda e, c=c, c0=c0, bk=bk: e.matmul(bk[0:1, c * 128:(c + 1) * 128], lhsT=PRVC[:, c0 + c:c0 + c + 1], rhs=identf, start=True, stop=True), ["C:prvc%d" % (c0 + c), "C:cn"], [br])
                    act(lambda e, c0=c0, ncn=ncn, bk=bk: e.activation(out=OT2[0:1, c0 * 128:(c0 + ncn) * 128], in_=bk[0:1, 0:ncn * 128], func=AF.Copy), [br], ["R3:ot2"])
                sp_dma(o_sh_p[l], OT2[0:1, :], ["R3:ot2"], [], "ot2")

        def p1_sconv(l, blk, bi):
            P.fence("R3")
            RR3.reset()
            has_s = any(tg["kind"] == "s" for tg in blk["tg"])
            CVs = f32buf(RR3, 8 * 32).rearrange("p (c s j) -> p c s j", c=8, s=16)
            CVOs = f32buf(RR3, 8 * 32).rearrange("p (c s j) -> p c s j", c=8, s=16)
            if has_s:
                CVI = f32buf(RR3, 1024)
                sp_dma(CVI[0:32, :], st_cv[l].rearrange("s j c -> (s j) c"), [], ["R3:cvi"], "cvi")
                for c0 in range(0, 8, 4):
                    bk, br = psbank()
                    for c in range(4):
                        pe(lambda e, c=c, c0=c0, bk=bk: e.matmul(bk[:, c * 32:(c + 1) * 32], lhsT=CVI[0:32, (c0 + c) * 128:(c0 + c + 1) * 128], rhs=identf[0:32, 0:32], start=True, stop=True),
                           ["R3:cvi", "C:cn"], [br])
                    dve(lambda e, c0=c0, bk=bk: e.tensor_copy(out=CVs[:, c0:c0 + 4].rearrange("p c s j -> p c (s j)"), in_=bk[:, 0:128].rearrange("p (c x) -> p c x", c=4)), [br], ["R3:cvs"])
            mark = RR3.cur
            for c in range(8):
                P.fence("R3")
                RR3.reset(mark)
                wbc_, wbcr = wload(w_scbc[l, c], 4096)
                wbc = wbc_.rearrange("p (k c) -> p k c", k=16)
                wh_, whr = wload(w_sch[l, c], 2048)
                wh = wh_.rearrange("p (k c) -> p k c", k=16)
                Bb = f32buf(RR3, 512); Cc = f32buf(RR3, 512); U = f32buf(RR3, 520); CV = f32buf(RR3, 512)
                Us = f32buf(RR3, 160).rearrange("p (s t) -> p s t", s=16)
                cw = VEC[:, VOFF["cw"][0] + c * 3: VOFF["cw"][0] + c * 3 + 3]
                for tg in blk["tg"]:
                    t0, n, kind = tg["t0"], tg["n"], tg["kind"]
                    bkB, brB = mm_fm(wbc, wbcr, 0, 0, hT, hres, t0, n, 16)
                    act(lambda e, bkB=bkB: e.activation(out=Bb[:, 0:n], in_=bkB[:, 0:n], func=AF.Copy), [brB], ["R3:bb"])
                    bkC, brC = mm_fm(wbc, wbcr, 0, 128, hT, hres, t0, n, 16)
                    act(lambda e, bkC=bkC: e.activation(out=Cc[:, 0:n], in_=bkC[:, 0:n], func=AF.Copy), [brC], ["R3:cc"])
                    bkH, brH = mm_fm(wh, whr, 0, 0, hT, hres, t0, n, 16)
                    if kind == "p":
                        if t0 == 0:
                            act(lambda e: e.activation(out=U[:, 0:2], in_=CVH[:, c, :], func=AF.Copy), ["C:cvh"], ["R3:u"])
                        dve(lambda e, bkH=bkH: e.tensor_tensor(out=U[:, 2:2 + n], in0=Cc[:, 0:n], in1=bkH[:, 0:n], op=ALU.mult), ["R3:cc", brH], ["R3:u"])
                        dve(lambda e: e.tensor_scalar(out=CV[:, 0:n], in0=U[:, 0:n], scalar1=cw[:, 0:1], scalar2=None, op0=ALU.mult), ["R3:u", "C:vec"], ["R3:cv"])
                        dve(lambda e: e.scalar_tensor_tensor(out=CV[:, 0:n], in0=U[:, 1:1 + n], scalar=cw[:, 1:2], in1=CV[:, 0:n], op0=ALU.mult, op1=ALU.add), ["R3:u", "R3:cv", "C:vec"], ["R3:cv"])
                        dve(lambda e: e.scalar_tensor_tensor(out=CV[:, 0:n], in0=U[:, 2:2 + n], scalar=cw[:, 2:3], in1=CV[:, 0:n], op0=ALU.mult, op1=ALU.add), ["R3:u", "R3:cv", "C:vec"], ["R3:cv"])
                        dve(lambda e, t0=t0: e.tensor_tensor(out=o_all[:, 16 + c, t0:t0 + n], in0=Bb[:, 0:n], in1=CV[:, 0:n], op=ALU.mult), ["R3:bb", "R3:cv"], ["OA:c%d" % (16 + c)])
                        act(lambda e: e.activation(out=CVH[:, c, :], in_=U[:, n:n + 2], func=AF.Copy), ["R3:u"], ["C:cvh"])
                        act(lambda e: e.activation(out=U[:, 0:2], in_=CVH[:, c, :], func=AF.Copy), ["C:cvh", "R3:cv"], ["R3:u"])
                    else:
                        act(lambda e: e.activation(out=Us[:, :, 0:2], in_=CVs[:, c, :, :], func=AF.Copy), ["R3:cvs"], ["R3:us"])
                        dve(lambda e, bkH=bkH: e.tensor_tensor(out=Us[:, :, 2:10], in0=Cc[:, 0:128].rearrange("p (s t) -> p s t", s=16), in1=bkH[:, 0:128].rearrange("p (s t) -> p s t", s=16), op=ALU.mult),
                            ["R3:cc", brH], ["R3:us"])
                        cv3 = CV[:, 0:128].rearrange("p (s t) -> p s t", s=16)
                        dve(lambda e: e.tensor_scalar(out=cv3, in0=Us[:, :, 0:8], scalar1=cw[:, 0:1], scalar2=None, op0=ALU.mult), ["R3:us", "C:vec"], ["R3:cv"])
                        dve(lambda e: e.scalar_tensor_tensor(out=cv3, in0=Us[:, :, 1:9], scalar=cw[:, 1:2], in1=cv3, op0=ALU.mult, op1=ALU.add), ["R3:us", "R3:cv", "C:vec"], ["R3:cv"])
                        dve(lambda e: e.scalar_tensor_tensor(out=cv3, in0=Us[:, :, 2:10], scalar=cw[:, 2:3], in1=cv3, op0=ALU.mult, op1=ALU.add), ["R3:us", "R3:cv", "C:vec"], ["R3:cv"])
                        dve(lambda e, t0=t0: e.tensor_tensor(out=o_all[:, 16 + c, t0:t0 + 128], in0=Bb[:, 0:128], in1=CV[:, 0:128], op=ALU.mult), ["R3:bb", "R3:cv"], ["OA:c%d" % (16 + c)])
                        act(lambda e: e.activation(out=CVOs[:, c, :, :], in_=Us[:, :, 8:10], func=AF.Copy), ["R3:us"], ["R3:cvos"])
            P.fence("R3")
            RR3.reset(mark)
            if has_s:
                OC = f32buf(RR3, 1024)
                for c0 in range(0, 8, 4):
                    bk, br = psbank()
                    for c in range(4):
                        pe(lambda e, c=c, c0=c0, bk=bk: e.matmul(bk[0:32, c * 128:(c + 1) * 128], lhsT=CVOs[:, c0 + c].rearrange("p s j -> p (s j)"), rhs=identf, start=True, stop=True), ["R3:cvos", "C:cn"], [br])
                    act(lambda e, c0=c0, bk=bk: e.activation(out=OC[0:32, c0 * 128:(c0 + 4) * 128], in_=bk[0:32, 0:512], func=AF.Copy), [br], ["R3:oc"])
                sp_dma(o_cv_s[l].rearrange("s j c -> (s j) c"), OC[0:32, :], ["R3:oc"], [], "oc")
            if bi == 1:
                OC2 = f32buf(RR3, 1024)
                for c0 in range(0, 8, 4):
                    bk, br = psbank()
                    for c in range(4):
                        pe(lambda e, c=c, c0=c0, bk=bk: e.matmul(bk[0:2, c * 128:(c + 1) * 128], lhsT=CVH[:, c0 + c, :], rhs=identf, start=True, stop=True), ["C:cvh", "C:cn"], [br])
                    act(lambda e, c0=c0, bk=bk: e.activation(out=OC2[0:2, c0 * 128:(c0 + 4) * 128], in_=bk[0:2, 0:512], func=AF.Copy), [br], ["R3:oc2"])
                sp_dma(o_cv_p[l], OC2[0:2, :], ["R3:oc2"], [], "oc2")

        def p2(l, blk):
            fence_all()
            RR3.reset()
            T = blk["T"]
            mT = RR3.alloc(SZ_HT, BF16).rearrange("p (k t) -> p k t", k=16)
            SG = [f32buf(RR3, 512) for _ in range(2)]
            MM = [f32buf(RR3, 512) for _ in range(3)]
            TP = [f32buf(RR3, 512) for _ in range(2)]
            ctr = 0
            oares = ["OA:c%d" % k for k in range(24)]
            for c in range(16):
                wp_, wpr = wload(w_p[l, c], 3072)
                wp = wp_.rearrange("p (k c) -> p k c", k=24)
                for j in range(3):
                    wg_, wgr = wload(w_g[l, c, j], 2048)
                    wg = wg_.rearrange("p (k c) -> p k c", k=16)
                    for ti, (t0, n) in enumerate(tgroups(T)):
                        mm = MM[ti]; mres = "R3:mm%d" % ti
                        sg = SG[ctr % 2]; sres = "R3:sg%d" % (ctr % 2)
                        tp = TP[ctr % 2]; tres = "R3:tp%d" % (ctr % 2)
                        ctr += 1
                        bkg, brg = mm_fm(wg, wgr, 0, 0, hT, hres, t0, n, 16)
                        act(lambda e, bkg=bkg, sg=sg: e.activation(out=sg[:, 0:n], in_=bkg[:, 0:n], func=AF.Sigmoid), [brg], [sres])
                        bky, bry = mm_fm(wp, wpr, 8 * j, 0, o_all[:, 8 * j:8 * j + 8, :], lambda a, b, j=j: oares[8 * j:8 * j + 8], t0, n, 8)
                        if j == 0:
                            dve(lambda e, bky=bky, sg=sg, mm=mm: e.tensor_tensor(out=mm[:, 0:n], in0=sg[:, 0:n], in1=bky[:, 0:n], op=ALU.mult), [sres, bry], [mres])
                        elif j == 1:
                            dve(lambda e, bky=bky, sg=sg, tp=tp: e.tensor_tensor(out=tp[:, 0:n], in0=sg[:, 0:n], in1=bky[:, 0:n], op=ALU.mult), [sres, bry], [tres])
                            dve(lambda e, mm=mm, tp=tp: e.tensor_tensor(out=mm[:, 0:n], in0=mm[:, 0:n], in1=tp[:, 0:n], op=ALU.add), [mres, tres], [mres])
                        else:
                            dve(lambda e, bky=bky, sg=sg, tp=tp: e.tensor_tensor(out=tp[:, 0:n], in0=sg[:, 0:n], in1=bky[:, 0:n], op=ALU.mult), [sres, bry], [tres])
                            dve(lambda e, mm=mm, tp=tp, t0=t0, c=c: e.tensor_tensor(out=mT[:, c, t0:t0 + n], in0=mm[:, 0:n], in1=tp[:, 0:n], op=ALU.add), [mres, tres],
                                ["R3:mT%d" % i for i in range(t0 // 128, (t0 + n) // 128)])
            return mT

        def p3(l, blk, mT):
            P.fence("HT", "OA", "GB", "PS", "R3")
            tiles = blk["tiles"]
            nt = len(tiles)
            ROA.reset()
            MIXa = [f32buf(ROA, D) for _ in range(6)]
            junk = bfbuf(ROA, D)
            RR3.reset(SZ_HT)
            MIXb = [f32buf(RR3, D) for _ in range(3)]
            XT = f32buf(RR3, D)
            xn = bfbuf(RR3, D)
            MIX = MIXa + MIXb
            mres = lambda i: ("OA:mix%d" % i) if i < 6 else ("R3:mix%d" % i)
            STAT = f32buf(ROA, 9 * 8 + 16)
            sp_dma(GB, gb_d[l, 0].partition_broadcast(128), [], ["GB:g"], "gbl")
            for ct in range(8):
                wo_, wor = wload(w_o[l, ct], 4096)
                wo = wo_.rearrange("p (k c) -> p k c", k=16)
                for i in range(nt):
                    bk, br = psbank()
                    for k in range(16):
                        pe(lambda e, bk=bk, k=k, i=i: e.matmul(bk[:, 0:256], lhsT=mT[:, k, i * 128:(i + 1) * 128], rhs=wo[:, k, :], start=(k == 0), stop=(k == 15)),
                           [wor, "R3:mT%d" % i], [br])
                    act(lambda e, bk=bk, i=i, ct=ct: e.activation(out=MIX[i][:, ct * 256:(ct + 1) * 256], in_=bk[:, 0:256], func=AF.Copy), [br], [mres(i)])
                    act(lambda e, bk=bk, i=i, ct=ct: e.activation(out=junk[:, 0:256], in_=bk[:, 0:256], func=AF.Square, accum_out=STAT[:, i * 8 + ct:i * 8 + ct + 1]), [br], ["OA:junk", "OA:stat%d" % i])
            S2 = STAT[:, 72:88]
            for i, g in enumerate(tiles):
                dve(lambda e, i=i: e.tensor_reduce(out=S2[:, 0:1], in_=STAT[:, i * 8:(i + 1) * 8], axis=AX.X, op=ALU.add), ["OA:stat%d" % i], ["OA:s2"])
                rsqrt_col(S2[:, 1:2], S2[:, 0:1], 1.0 / D, eps6, ["OA:s2", "C:cn"], "OA:s2r", S2[:, 2:3])
                act(lambda e, i=i: e.activation(out=MIX[i], in_=MIX[i], func=AF.Copy, scale=S2[:, 1:2]), [mres(i), "OA:s2r"], [mres(i)])
                dve(lambda e, i=i: e.tensor_tensor(out=MIX[i], in0=MIX[i], in1=GB, op=ALU.mult), [mres(i), "GB:g"], [mres(i)])
                sp_dma(XT, xsrc(l, g), [yres(l - 1, g)] if l > 0 else [], ["R3:xt"], "xt3")
                dve(lambda e, i=i: e.tensor_tensor(out=MIX[i], in0=MIX[i], in1=XT, op=ALU.add), [mres(i), "R3:xt"], [mres(i)])
                sp_dma(x1s[i * 128:(i + 1) * 128, :], MIX[i], [mres(i)], ["DR:x1_%d" % i], "mixo%d" % i)
                act(lambda e, i=i: e.activation(out=junk, in_=MIX[i], func=AF.Square, accum_out=S2[:, 4:5]), [mres(i)], ["OA:junk", "OA:s3"])
                rsqrt_col(S2[:, 5:6], S2[:, 4:5], 1.0 / D, eps6, ["OA:s3", "C:cn"], "OA:s3r", S2[:, 6:7])
                act(lambda e, i=i: e.activation(out=xn, in_=MIX[i], func=AF.Copy, scale=S2[:, 5:6]), [mres(i), "OA:s3r"], ["R3:xn"])
                for half in range(2):
                    bk, br = psbank()
                    bkb = bk.bitcast(BF16)
                    for c in range(8):
                        cc = half * 8 + c
                        pe(lambda e, c=c, cc=cc, bkb=bkb: e.transpose(out=bkb[:, c * 128:(c + 1) * 128], in_=xn[:, cc * 128:(cc + 1) * 128], identity=identb), ["R3:xn", "C:identb"], [br])
                    g3 = vec("gpl", half * 8, 8).unsqueeze(2).to_broadcast([128, 8, 128])
                    dve(lambda e, half=half, bkb=bkb, g3=g3, i=i: e.tensor_tensor(out=hT[:, half * 8:half * 8 + 8, i * 128:(i + 1) * 128],
                                                                              in0=bkb[:, 0:1024].rearrange("p (c t) -> p c t", c=8), in1=g3, op=ALU.mult), [br, "C:vec"], ["HT:t%d" % i])

        def p4(l, blk):
            P.fence("OA", "R3", "GB", "PS")
            tiles = blk["tiles"]
            nt = len(tiles)
            T = blk["T"]
            RR3.reset()
            YA = [f32buf(RR3, D) for _ in range(nt)]
            ROA.reset()
            AT = ROA.alloc(2 * 8 * 1152, BF16).rearrange("p (k t) -> p k t", k=8)
            X1 = [f32buf(ROA, D) for _ in range(2)]
            junk = bfbuf(ROA, D)
            RT = [f32buf(ROA, 512) for _ in range(2)]
            STAT = f32buf(ROA, 16)
            sp_dma(GB, gb_d[l, 1].partition_broadcast(128), [], ["GB:g"], "gbl")
            rc = 0
            for e8 in range(8):
                for q in range(4):
                    w1_, w1r = wload(w_f1[l, e8 * 4 + q], 4096)
                    w1 = w1_.rearrange("p (k c) -> p k c", k=16)
                    for hc in range(2):
                        for (t0, n) in tgroups(T):
                            bk, br = mm_fm(w1, w1r, 0, hc * 128, hT, hres, t0, n, 16)
                            rt = RT[rc % 2]; rres = "OA:rt%d" % (rc % 2)
                            rc += 1
                            act(lambda e, bk=bk, rt=rt: e.activation(out=rt[:, 0:n], in_=bk[:, 0:n], func=AF.Relu), [br], [rres])
                            dve(lambda e, rt=rt, q=q, hc=hc, t0=t0: e.tensor_tensor(out=AT[:, q * 2 + hc, t0:t0 + n], in0=rt[:, 0:n], in1=rt[:, 0:n], op=ALU.mult), [rres],
                                ["OA:at%d" % i for i in range(t0 // 128, (t0 + n) // 128)])
                for cg in range(4):
                    w2_, w2r = wload(w_f2[l, e8, cg], 4096)
                    w2 = w2_.rearrange("p (k c) -> p k c", k=8)
                    for i in range(nt):
                        bk, br = psbank()
                        for k in range(8):
                            pe(lambda e, bk=bk, k=k, i=i: e.matmul(bk[:, 0:512], lhsT=AT[:, k, i * 128:(i + 1) * 128], rhs=w2[:, k, :], start=(k == 0), stop=(k == 7)),
                               [w2r, "OA:at%d" % i], [br])
                        ya = YA[i][:, cg * 512:(cg + 1) * 512]
                        if e8 == 0:
                            act(lambda e, bk=bk, ya=ya: e.activation(out=ya, in_=bk[:, 0:512], func=AF.Copy), [br], ["R3:ya%d_%d" % (i, cg)])
                        else:
                            dve(lambda e, bk=bk, ya=ya: e.tensor_tensor(out=ya, in0=ya, in1=bk[:, 0:512], op=ALU.add), [br, "R3:ya%d_%d" % (i, cg)], ["R3:ya%d_%d" % (i, cg)])
            for i, g in enumerate(tiles):
                yr = ["R3:ya%d_%d" % (i, cg) for cg in range(4)]
                s = i % 2
                sp_dma(X1[s], x1s[i * 128:(i + 1) * 128, :], ["DR:x1_%d" % i], ["OA:x1%d" % s], "x1l%d" % s)
                act(lambda e, i=i: e.activation(out=junk, in_=YA[i], func=AF.Square, accum_out=STAT[:, 0:1]), yr, ["OA:junk", "OA:st"])
                rsqrt_col(STAT[:, 1:2], STAT[:, 0:1], 1.0 / D, eps6, ["OA:st", "C:cn"], "OA:str", STAT[:, 2:3])
                act(lambda e, i=i: e.activation(out=YA[i], in_=YA[i], func=AF.Copy, scale=STAT[:, 1:2]), yr + ["OA:str"], yr)
                dve(lambda e, i=i: e.tensor_tensor(out=YA[i], in0=YA[i], in1=GB, op=ALU.mult), yr + ["GB:g"], yr)
                dve(lambda e, i=i, s=s: e.tensor_tensor(out=YA[i], in0=YA[i], in1=X1[s], op=ALU.add), yr + ["OA:x1%d" % s], yr)
                sp_dma(ydst(l, g), YA[i], yr, [yres(l, g)], "yo%d" % i)

        import os as _os
        _STOP = _os.environ.get('KSTOP', '')
        MARKS.clear()

        def mark(name, l, bi):
            MARKS.append((name, l, bi, len(P.ops["pe"]), len(P.ops["act"]), len(P.ops["dve"])))
        _CKN = int(_os.environ.get('KSTOPN', '0'))
        _ckc = [0]

        def ck(name):
            _ckc[0] += 1
            if (_CKN and _ckc[0] == _CKN) or (name == _os.environ.get('KSTOPNAME', '-')):
                print("STOP at checkpoint", _ckc[0], name, flush=True)
                raise _Stop()
        globals()['_ck'] = ck
        for l in range(int(_os.environ.get('KLAYERS', L))):
          try:
            layer_setup(l)
            _stop = _STOP
            for bi, blk in enumerate(BLOCKS):
                mark("p0", l, bi)
                p0(l, blk)
                if _stop == 'p0': break
                mark("hg", l, bi)
                p1_hgrn(l, blk, bi)
                if _stop == 'hg': break
                mark("rw", l, bi)
                p1_rwkv(l, blk, bi)
                if _stop == 'rw': break
                mark("sc", l, bi)
                p1_sconv(l, blk, bi)
                if _stop == 'sc': break
                mark("p2", l, bi)
                mT = p2(l, blk)
                if _stop == 'p2': break
                mark("p3", l, bi)
                if dbg and l == 0 and bi == 0:
                    P.dma("pool", lambda e: e.dma_start(out=dbg_out["oall"], in_=o_all), ["OA:c%d" % k for k in range(24)], [], "dbg_oall")
                    P.dma("pool", lambda e: e.dma_start(out=dbg_out["mT"], in_=mT), ["R3:mT%d" % i for i in range(9)], [], "dbg_mT")
                    P.dma("pool", lambda e: e.dma_start(out=dbg_out["hT"], in_=hT), ["HT:t%d" % i for i in range(9)], [], "dbg_hT")
                p3(l, blk, mT)
                if dbg and l == 0 and bi == 0:
                    P.dma("pool", lambda e: e.dma_start(out=dbg_out["h2T"], in_=hT), ["HT:t%d" % i for i in range(9)], [], "dbg_h2T")
                mark("p4", l, bi)
                p4(l, blk)
          except _Stop:
            break
        P.finish()
        P.emit(nc)
    return nc


def _tile_w(w, col_lists):
    K = w.shape[0]
    nk = K // 128
    cols = np.concatenate(col_lists)
    sub = w[:, cols]
    return np.ascontiguousarray(sub.reshape(nk, 128, len(cols)).transpose(1, 0, 2).reshape(128, nk * len(cols)))


def _vec_fm(v):
    return np.ascontiguousarray(v.reshape(-1, 128).T)


_CACHE = {}


def prepare_shared(inp):
    r = np.arange
    OFF_RW = 4096
    OFF_SC = OFF_RW + SHW
    OFF_G = OFF_SC + 3072
    sh = {}
    w_in = inp["w_in"]
    w_hg = np.empty((L, HGH, 2, 128, 4096), np.float32)
    w_rwl = np.empty((L, 128, 4096), np.float32)
    w_rwrk = np.empty((L, RWP, 128, 4096), np.float32)
    w_rwv = np.empty((L, RWP, 128, 2048), np.float32)
    w_scbc = np.empty((L, 8, 128, 4096), np.float32)
    w_sch = np.empty((L, 8, 128, 2048), np.float32)
    w_g = np.empty((L, 16, 3, 128, 2048), np.float32)
    w_p = np.empty((L, 16, 128, 3072), np.float32)
    w_o = np.empty((L, 8, 128, 4096), np.float32)
    w_f1 = np.empty((L, 32, 128, 4096), np.float32)
    w_f2 = np.empty((L, 8, 4, 128, 4096), np.float32)
    vecs = np.zeros((L, 128, NV), np.float32)
    for l in range(L):
        W = w_in[l]
        for h in range(HGH):
            w_hg[l, h, 0] = _tile_w(W, [r(h * 128, h * 128 + 128), 1024 + r(h * 128, h * 128 + 128)])
            w_hg[l, h, 1] = _tile_w(W, [3072 + r(h * 128, h * 128 + 128), 2048 + r(h * 128, h * 128 + 128)])
        w_rwl[l] = _tile_w(W, [OFF_RW + 3072 + r(0, 256)])
        for p in range(RWP):
            w_rwrk[l, p] = _tile_w(W, [OFF_RW + r(p * 128, p * 128 + 128), OFF_RW + 1024 + r(p * 128, p * 128 + 128)])
            w_rwv[l, p] = _tile_w(W, [OFF_RW + 2048 + r(p * 128, p * 128 + 128)])
        for c in range(8):
            w_scbc[l, c] = _tile_w(W, [OFF_SC + r(c * 128, c * 128 + 128), OFF_SC + 1024 + r(c * 128, c * 128 + 128)])
            w_sch[l, c] = _tile_w(W, [OFF_SC + 2048 + r(c * 128, c * 128 + 128)])
        for c in range(16):
            for j in range(3):
                w_g[l, c, j] = _tile_w(W, [OFF_G + j * D + r(c * 128, c * 128 + 128)])
            cc = [r(c * 128, c * 128 + 128)]
            w_p[l, c] = np.concatenate([_tile_w(inp["w_pa"][l], cc), _tile_w(inp["w_pb"][l], cc), _tile_w(inp["w_pc"][l], cc)], axis=1)
        for ct in range(8):
            w_o[l, ct] = _tile_w(inp["w_o"][l], [r(ct * 256, ct * 256 + 256)])
        for t in range(32):
            w_f1[l, t] = _tile_w(inp["w_ff1"][l], [r(t * 256, t * 256 + 256)])
        for e8 in range(8):
            for cg in range(4):
                w_f2[l, e8, cg] = _tile_w(inp["w_ff2"][l][e8 * 1024:(e8 + 1) * 1024], [r(cg * 512, cg * 512 + 512)])

        def put(name, arr):
            o_, w_ = VOFF[name]
            vecs[l, :, o_:o_ + w_] = arr
        put("gpm", _vec_fm(inp["norm_pre_mix"][l]))
        put("gpl", _vec_fm(inp["norm_pre_mlp"][l]))
        put("lb0", _vec_fm(inp["hg_lb_logits"][0]))
        put("lb1", _vec_fm(inp["hg_lb_logits"][1]))
        put("hgn", _vec_fm(inp["hg_norm"][l]))
        put("mu", _vec_fm(inp["rw_mu"][l]))
        put("w0", _vec_fm(inp["rw_w0"][l]))
        put("a0", _vec_fm(inp["rw_a0"][l]))
        put("kk", _vec_fm(inp["rw_k_k"][l]))
        put("ka", _vec_fm(inp["rw_k_a"][l]))
        put("rk", _vec_fm(inp["rw_r_k"][l].reshape(-1)))
        put("lnw", _vec_fm(inp["rw_ln_w"][l]))
        put("lnb", _vec_fm(inp["rw_ln_b"][l]))
        cw = inp["sc_conv_w"][l]
        put("cw", np.ascontiguousarray(cw.reshape(3, 8, 128).transpose(2, 1, 0).reshape(128, 24)))
    sh.update(w_hg=w_hg, w_rwl=w_rwl, w_rwrk=w_rwrk, w_rwv=w_rwv, w_scbc=w_scbc, w_sch=w_sch, w_g=w_g, w_p=w_p, w_o=w_o,
              w_f1=w_f1, w_f2=w_f2, vecs=vecs,
              gb=np.ascontiguousarray(np.stack([inp["norm_post_mix"], inp["norm_post_mlp"]], axis=1)).astype(np.float32),
              lw2=np.ascontiguousarray(inp["rw_w2"]), la2=np.ascontiguousarray(inp["rw_a2"]), lg2=np.ascontiguousarray(inp["rw_g2"]),
              consts=CONSTS, masks=MASKS)
    return sh


def make_in_maps(inp):
    inp = {k: np.asarray(v) for k, v in inp.items()}
    sh = prepare_shared(inp)
    maps = []
    for c in range(NCORE):
        b = c // 2
        m = dict(sh)
        m["xp"] = np.ascontiguousarray(inp["x_prompt"][b])
        m["xs"] = np.ascontiguousarray(inp["x_sample"][16 * c:16 * c + 16].reshape(128, D))
        m["st_hg"] = np.ascontiguousarray(inp["state_hgrn"][:, 16 * c:16 * c + 16])
        m["st_rw"] = np.ascontiguousarray(inp["state_rwkv"][:, 16 * c:16 * c + 16])
        m["st_sh"] = np.ascontiguousarray(inp["state_rwkv_shift"][:, 16 * c:16 * c + 16])
        m["st_cv"] = np.ascontiguousarray(inp["state_conv"][:, 16 * c:16 * c + 16])
        maps.append(m)
    return maps


def assemble(results):
    R = results
    yp = np.stack([R[2 * b]["yp"] for b in range(4)]).reshape(4, 2048, D)
    ys = np.concatenate([R[c]["ys"].reshape(16, 8, D) for c in range(NCORE)], axis=0)
    hg_p = np.stack([R[2 * b]["o_hg_p"] for b in range(4)], axis=1)
    rw_p = np.stack([R[2 * b]["o_rw_p"] for b in range(4)], axis=1)
    sh_p = np.stack([R[2 * b]["o_sh_p"][:, 0] for b in range(4)], axis=1)
    cv_p = np.stack([R[2 * b]["o_cv_p"] for b in range(4)], axis=1)
    hg_s = np.concatenate([R[c]["o_hg_s"] for c in range(NCORE)], axis=1)
    rw_s = np.concatenate([R[c]["o_rw_s"] for c in range(NCORE)], axis=1)
    sh_s = np.concatenate([R[c]["o_sh_s"] for c in range(NCORE)], axis=1)
    cv_s = np.concatenate([R[c]["o_cv_s"] for c in range(NCORE)], axis=1)
    outs = (yp, ys, hg_p, rw_p, sh_p, cv_p, hg_s, rw_s, sh_s, cv_s)
    return tuple(np.ascontiguousarray(o, dtype=np.float32) for o in outs)


def kernel(**inputs):
    nc = build_program()
    maps = make_in_maps(inputs)
    res = run_bass_kernel_spmd(nc, maps, core_ids=list(range(NCORE)))
    return assemble(res.results)
```

```python
import contextlib
import numpy as np
import concourse.bass as bass
import concourse.mybir as mybir
from concourse.bass_utils import run_bass_kernel_spmd

F32 = mybir.dt.float32
BF16 = mybir.dt.bfloat16
AF = mybir.ActivationFunctionType
ALU = mybir.AluOpType
AX = mybir.AxisListType

D = 2048
L = 2
NCORE = 8
HGH = 8
RWP = 8
SHW = 3328
DFF = 8192
EPS = 1e-6
GN_EPS = 64e-5
DEC_C = float(np.exp(-0.5))


class Op:
    __slots__ = ("eng", "fn", "deps", "chan", "seq", "signal", "is_dma")

    def __init__(self, eng, fn, chan, is_dma):
        self.eng = eng
        self.fn = fn
        self.chan = chan
        self.is_dma = is_dma
        self.deps = {}
        self.seq = 0
        self.signal = False


class _Rec:
    def __init__(self):
        self.call = None

    def __getattr__(self, name):
        def f(*a, **k):
            self.call = (name, a, k)
            return None
        return f


class _Stop(Exception):
    pass


class Prog:
    ENGS = ("pe", "act", "dve", "pool", "sp")

    def __init__(self):
        self.ops = {e: [] for e in self.ENGS}
        self.chan_ops = {}
        self.last_w = {}
        self.readers = {}
        self.waited = {e: {} for e in self.ENGS}
        self.fences = {}
        self.touched = {}

    def fence(self, *regions):
        for r in regions:
            f = self.fences.setdefault(r, {})
            for c, s in self.touched.get(r, {}).items():
                if f.get(c, 0) < s:
                    f[c] = s
            self.touched[r] = {}

    filler = None

    def _add(self, eng, fn, reads, writes, chan, is_dma):
        rec = _Rec()
        fn(rec)
        assert rec.call is not None
        lst = self.chan_ops.setdefault(chan, [])
        if (not is_dma) and eng == "dve" and self.filler is not None:
            last = 0
            for r in reads:
                o = self.last_w.get(r)
                if o is not None and o.chan == "dve":
                    last = max(last, o.seq)
            for w in writes:
                o = self.last_w.get(w)
                if o is not None and o.chan == "dve":
                    last = max(last, o.seq)
                for o in self.readers.get(w, ()):
                    if o.chan == "dve":
                        last = max(last, o.seq)
            if last and len(lst) + 1 - last == 1:
                f = Op(eng, self.filler, chan, False)
                lst.append(f)
                f.seq = len(lst)
                self.ops[eng].append(f)
        op = Op(eng, rec.call, chan, is_dma)
        lst.append(op)
        op.seq = len(lst)
        deps = {}
        skip_own = (not is_dma) and eng == "dve" and self.filler is not None
        pe_pe = (eng == "pe" and not is_dma)

        def need(c, s):
            if pe_pe and c == "pe":
                return
            if skip_own and c == "dve":
                return
            if deps.get(c, 0) < s:
                deps[c] = s

        regs = set()
        for r in reads:
            o = self.last_w.get(r)
            if o is not None:
                need(o.chan, o.seq)
            regs.add(r.split(":")[0])
        for w in writes:
            o = self.last_w.get(w)
            if o is not None:
                need(o.chan, o.seq)
            for o in self.readers.get(w, ()):
                need(o.chan, o.seq)
            regs.add(w.split(":")[0])
        for rg in regs:
            for c, s in self.fences.get(rg, {}).items():
                need(c, s)
            t = self.touched.setdefault(rg, {})
            if t.get(chan, 0) < op.seq:
                t[chan] = op.seq
        w_seen = self.waited[eng]
        for c, s in deps.items():
            if w_seen.get(c, 0) < s:
                w_seen[c] = s
                op.deps[c] = s
                self.chan_ops[c][s - 1].signal = True
        for r in reads:
            self.readers.setdefault(r, []).append(op)
        for w in writes:
            self.last_w[w] = op
            self.readers[w] = []
        self.ops[eng].append(op)
        return op

    def op(self, eng, fn, reads=(), writes=()):
        return self._add(eng, fn, reads, writes, eng, False)

    def dma(self, eng, fn, reads=(), writes=(), slot=None):
        return self._add(eng, fn, reads, writes, "dma:" + slot, True)

    def finish(self, eng="sp"):
        op = Op(eng, None, eng, False)
        for c, lst in self.chan_ops.items():
            if c.startswith("dma:") and lst:
                s = len(lst)
                if self.waited[eng].get(c, 0) < s:
                    op.deps[c] = s
                    lst[-1].signal = True
        self.ops[eng].append(op)

    def emit(self, nc):
        semval = {}
        for c, lst in self.chan_ops.items():
            n = 0
            for o in lst:
                if o.signal:
                    n += 1
                semval[(c, o.seq)] = n
        nsig = {c: sum(1 for o in lst if o.signal) for c, lst in self.chan_ops.items()}
        chans = [c for c in self.chan_ops if nsig[c] > 0]
        print("channels", len(chans), "ops", {e: len(v) for e, v in self.ops.items()},
              "maxsig", max(nsig.values()) if nsig else 0, flush=True)
        with contextlib.ExitStack() as st:
            sems = {}
            for i, c in enumerate(chans):
                sems[c] = st.enter_context(nc.semaphore("s%d" % i))
            block = st.enter_context(nc.Block())

            def run(engname, e):
                for o in self.ops[engname]:
                    for c, s in o.deps.items():
                        mult = 16 if c.startswith("dma:") else 1
                        e.wait_ge(sems[c], semval[(c, s)] * mult)
                    if o.fn is None:
                        continue
                    name, a, k = o.fn
                    ins = getattr(e, name)(*a, **k)
                    if o.signal:
                        ins.then_inc(sems[o.chan], 16 if o.is_dma else 1)

            @block.tensor
            def _(e):
                run("pe", e)

            @block.scalar
            def _(e):
                run("act", e)

            @block.vector
            def _(e):
                run("dve", e)

            @block.gpsimd
            def _(e):
                run("pool", e)

            @block.sync
            def _(e):
                run("sp", e)


def make_consts():
    t = np.arange(128)
    same64 = (t[:, None] // 64) == (t[None, :] // 64)
    same8 = (t[:, None] // 8) == (t[None, :] // 8)
    ge = t[None, :] >= t[:, None]
    gt = t[None, :] > t[:, None]
    lt = t[None, :] < t[:, None]
    keep = {}
    keep["ident"] = np.eye(128)
    keep["ones"] = np.ones((128, 128))
    keep["bd2"] = same64
    keep["pat8"] = np.tile((t[None, :] % 8 != 0), (128, 1))
    keep["ind2"] = (t[:, None] // 64) == np.arange(2)[None, :]
    keep["ind16"] = (t[:, None] // 8) == np.arange(16)[None, :]
    sc = np.zeros((128, 8))
    sc[:, 0] = EPS
    sc[:, 1] = GN_EPS
    keep["sc"] = sc
    masks = {}
    masks["hgM_p"] = ge & same64
    masks["hgM_s"] = ge & same8
    masks["rwL_p"] = lt
    masks["rwL_s"] = lt & same8
    masks["rwST_p"] = gt
    masks["rwST_s"] = gt & same8
    masks["rwIT_p"] = ge
    masks["rwIT_s"] = ge & same8
    masks["bd2b"] = same64

    def pack(items):
        off = {}
        cols = []
        o = 0
        for k, v in items.items():
            v = np.asarray(v, dtype=np.float32)
            off[k] = (o, v.shape[1])
            o += v.shape[1]
            cols.append(v)
        return np.concatenate(cols, axis=1), off

    a, ao = pack(keep)
    b, bo = pack(masks)
    return a, ao, b, bo


CONSTS, COFF, MASKS, MOFF = make_consts()
NCONST = CONSTS.shape[1]
NMASK = MASKS.shape[1]

VOFF = {}
_o = 0
for _n, _w in [("gpm", 16), ("gpl", 16), ("lb0", 8), ("lb1", 8), ("hgn", 8), ("mu", 26), ("w0", 8), ("a0", 8),
               ("kk", 8), ("ka", 8), ("rk", 8), ("lnw", 8), ("lnb", 8), ("cw", 24)]:
    VOFF[_n] = (_o, _w)
    _o += _w
NV = _o

BLOCKS = [
    dict(tgs=[("p", 0, 4), ("p", 4, 4), ("s", 16, 1)]),
    dict(tgs=[("p", 8, 4), ("p", 12, 4)]),
]
for _b in BLOCKS:
    tiles = []
    tgs = []
    c = 0
    for kind, g0, nt in _b["tgs"]:
        tgs.append(dict(kind=kind, g0=g0, nt=nt, t0=c, n=nt * 128))
        for i in range(nt):
            tiles.append(g0 + i)
        c += nt * 128
    _b["tiles"] = tiles
    _b["T"] = c
    _b["tg"] = tgs


def build_program(dbg=None):
    nc = bass.Bass("TRN2", target_bir_lowering=False)
    P = Prog()

    def din(name, shape):
        return nc.dram_tensor(name, list(shape), F32, kind="ExternalInput").ap()

    def dout(name, shape):
        return nc.dram_tensor(name, list(shape), F32, kind="ExternalOutput").ap()

    xp = din("xp", [2048, D])
    xs = din("xs", [128, D])
    consts_d = din("consts", [128, NCONST])
    masks_d = din("masks", [128, NMASK])
    vecs_d = din("vecs", [L, 128, NV])
    gb_d = din("gb", [L, 2, D])
    w_hg = din("w_hg", [L, HGH, 2, 128, 4096])
    w_rwl = din("w_rwl", [L, 128, 4096])
    w_rwrk = din("w_rwrk", [L, RWP, 128, 4096])
    w_rwv = din("w_rwv", [L, RWP, 128, 2048])
    w_scbc = din("w_scbc", [L, 8, 128, 4096])
    w_sch = din("w_sch", [L, 8, 128, 2048])
    w_g = din("w_g", [L, 16, 3, 128, 2048])
    w_p = din("w_p", [L, 16, 128, 3072])
    w_o = din("w_o", [L, 8, 128, 4096])
    w_f1 = din("w_f1", [L, 32, 128, 4096])
    w_f2 = din("w_f2", [L, 8, 4, 128, 4096])
    w2_d = din("lw2", [L, 64, 1024])
    a2_d = din("la2", [L, 64, 1024])
    g2_d = din("lg2", [L, 128, 1024])
    st_hg = din("st_hg", [L, 16, HGH, 128, 128])
    st_rw = din("st_rw", [L, 16, 16, 64, 64])
    st_sh = din("st_sh", [L, 16, SHW])
    st_cv = din("st_cv", [L, 16, 2, 1024])

    yp = dout("yp", [2048, D])
    ys = dout("ys", [128, D])
    o_hg_p = dout("o_hg_p", [L, HGH, 128, 128])
    o_rw_p = dout("o_rw_p", [L, 16, 64, 64])
    o_sh_p = dout("o_sh_p", [L, 1, SHW])
    o_cv_p = dout("o_cv_p", [L, 2, 1024])
    o_hg_s = dout("o_hg_s", [L, 16, HGH, 128, 128])
    o_rw_s = dout("o_rw_s", [L, 16, 16, 64, 64])
    o_sh_s = dout("o_sh_s", [L, 16, SHW])
    o_cv_s = dout("o_cv_s", [L, 16, 2, 1024])
    dbg_out = {}
    if dbg:
        for n, sh in dbg.items():
            dbg_out[n] = dout("dbg_" + n, sh)

    y1 = nc.dram_tensor("y1s", [17 * 128, D], F32).ap()
    x1s = nc.dram_tensor("x1s", [9 * 128, D], F32).ap()

    st = contextlib.ExitStack()
    with st:
        SZ_HT = 36864
        SZ_OA = 55296
        SZ_R3 = 73728
        SZ_W = 3 * 8192
        SZ_GB = 8192
        SZ_C = 14080
        total = SZ_HT + SZ_OA + SZ_R3 + SZ_W + SZ_GB + SZ_C
        arena = st.enter_context(nc.sbuf_tensor("arena", [128, total // 4], F32))
        base = {}
        o = 0
        for n, s in [("HT", SZ_HT), ("OA", SZ_OA), ("R3", SZ_R3), ("W", SZ_W), ("GB", SZ_GB), ("C", SZ_C)]:
            base[n] = o
            o += s

        class Region:
            def __init__(self, name, size):
                self.name = name
                self.size = size
                self.cur = 0

            def reset(self, at=0):
                self.cur = at

            def alloc(self, nbytes, dtype=F32, shape=None, parts=None):
                nbytes = (nbytes + 31) // 32 * 32
                assert self.cur + nbytes <= self.size, (self.name, self.cur, nbytes, self.size)
                off = base[self.name] + self.cur
                self.cur += nbytes
                ap = arena[:, off // 4:(off + nbytes) // 4]
                if dtype == BF16:
                    ap = ap.bitcast(BF16)
                return ap

        def f32buf(reg, n):
            return reg.alloc(4 * n)[:, 0:n]

        def bfbuf(reg, n):
            return reg.alloc(2 * n, BF16)[:, 0:n]

        RHT = Region("HT", SZ_HT)
        ROA = Region("OA", SZ_OA)
        RR3 = Region("R3", SZ_R3)
        RW = Region("W", SZ_W)
        RGB = Region("GB", SZ_GB)
        RC = Region("C", SZ_C)

        hT = RHT.alloc(SZ_HT, BF16).rearrange("p (k t) -> p k t", k=16)
        o_all = ROA.alloc(SZ_OA, BF16).rearrange("p (k t) -> p k t", k=24)
        wslots = [RW.alloc(8192, BF16) for _ in range(3)]
        GB = RGB.alloc(8192)

        CN = f32buf(RC, NCONST)
        VEC = f32buf(RC, NV)
        DER = f32buf(RC, 40)
        identb = bfbuf(RC, 128)
        HGS = f32buf(RC, HGH * 128).rearrange("p (h e) -> p h e", h=HGH)
        RWS = f32buf(RC, RWP * 128).rearrange("p (h e) -> p h e", h=RWP)

        MK = bfbuf(RC, NMASK)
        _scr = f32buf(RC, 8)
        P.filler = ("memset", (_scr[:, 0:1], 0.0), {})
        PRVC = f32buf(RC, 32)
        CVH = f32buf(RC, 16).rearrange("p (c j) -> p c j", c=8)

        def cst(name):
            if name in MOFF:
                o_, w_ = MOFF[name]
                return MK[:, o_:o_ + w_]
            o_, w_ = COFF[name]
            return CN[:, o_:o_ + w_]

        def vec(name, c=None, n=1):
            o_, w_ = VOFF[name]
            if c is None:
                return VEC[:, o_:o_ + w_]
            return VEC[:, o_ + c:o_ + c + n]

        banks = [st.enter_context(nc.psum_tensor("ps%d" % i, [128, 512], F32)) for i in range(8)]
        bank_ctr = [0]

        def psbank():
            i = bank_ctr[0] % 8
            bank_ctr[0] += 1
            return banks[i], "PS:%d" % i

        wctr = [0]

        def wload(src, n):
            s = wctr[0] % 3
            wctr[0] += 1
            dst = wslots[s][:, 0:n]
            P.dma("pool", lambda e: e.dma_start(out=dst, in_=src), writes=["W:%d" % s], slot="W%d" % s)
            return wslots[s][:, 0:n], "W:%d" % s

        def act(fn, reads, writes):
            P.op("act", fn, reads, writes)

        def dve(fn, reads, writes):
            P.op("dve", fn, reads, writes)

        def pe(fn, reads, writes):
            P.op("pe", fn, reads, writes)

        def sp_dma(out, in_, reads, writes, slot):
            P.dma("sp", lambda e: e.dma_start(out=out, in_=in_), reads, writes, slot)

        def rsqrt_col(dst, src, scale, eps_ap, rd, wr, tmp):
            act(lambda e: e.activation(out=tmp, in_=src, func=AF.Sqrt, bias=eps_ap, scale=scale), rd, [wr + ".t"])
            dve(lambda e: e.reciprocal(out=dst, in_=tmp), [wr + ".t"], [wr])

        eps6 = cst("sc")[:, 0:1]
        epsgn = cst("sc")[:, 1:2]
        zero_c = cst("sc")[:, 2:3]

        def hres(t0, n):
            return ["HT:t%d" % i for i in range(t0 // 128, (t0 + n + 127) // 128)]

        def tgroups(T):
            r = []
            t = 0
            while t < T:
                n = min(512, T - t)
                r.append((t, n))
                t += n
            return r

        def dbg_dump(name, ap, res):
            if dbg and name in dbg_out:
                sp_dma(dbg_out[name], ap, res, [], "dbg_" + name)

        sp_dma(CN, consts_d, [], ["C:cn0"], "cn")
        RR3.reset()
        _mtmp = f32buf(RR3, NMASK)
        sp_dma(_mtmp, masks_d, [], ["R3:mtmp"], "mtmp")
        dve(lambda e: e.tensor_copy(out=MK, in_=_mtmp), ["R3:mtmp", "C:cn0"], ["C:cn"])
        dve(lambda e: e.tensor_copy(out=identb, in_=cst("ident")), ["C:cn"], ["C:identb"])
        identf = cst("ident")

        def xsrc(l, g):
            if l == 0:
                return xp[g * 128:(g + 1) * 128, :] if g < 16 else xs
            return y1[g * 128:(g + 1) * 128, :]

        def ydst(l, g):
            if l == 0:
                return y1[g * 128:(g + 1) * 128, :]
            return yp[g * 128:(g + 1) * 128, :] if g < 16 else ys

        def yres(l, g):
            return "DR:y%d_%d" % (l, g)

        def fence_all():
            P.fence("HT", "OA", "R3", "GB", "PS")

        def layer_setup(l):
            sp_dma(VEC, vecs_d[l], [], ["C:vec"], "vec")
            lb = DER[:, 0:8]
            oml = DER[:, 8:16]
            omka = DER[:, 16:24]
            if l == 0:
                dve(lambda e: e.memset(lb, 0.0), ["C:vec"], ["C:der"])
            else:
                dve(lambda e: e.tensor_tensor(out=lb, in0=vec("lb1"), in1=vec("lb0"), op=ALU.subtract), ["C:vec"], ["C:der"])
                act(lambda e: e.activation(out=lb, in_=lb, func=AF.Sigmoid), ["C:der"], ["C:der"])
            dve(lambda e: e.tensor_scalar(out=oml, in0=lb, scalar1=-1.0, scalar2=1.0, op0=ALU.mult, op1=ALU.add), ["C:der"], ["C:der"])
            dve(lambda e: e.tensor_scalar(out=omka, in0=vec("ka"), scalar1=-1.0, scalar2=1.0, op0=ALU.mult, op1=ALU.add), ["C:vec", "C:der"], ["C:der"])
            dve(lambda e: e.memset(HGS, 0.0), [], ["C:hgs"])
            dve(lambda e: e.memset(RWS, 0.0), [], ["C:rws"])
            dve(lambda e: e.memset(PRVC, 0.0), [], ["C:prvc"])
            dve(lambda e: e.memset(CVH, 0.0), [], ["C:cvh"])

        def norm_transpose(src_ap, src_res, ss_src, dstT, i, gname, reg, tagp):
            junk = tagp["junk"]
            ssc = tagp["ss"]
            xn = tagp["xn"]
            act(lambda e: e.activation(out=junk, in_=src_ap, func=AF.Square, accum_out=ssc[:, 0:1]), [src_res], ["R3:junk", "R3:ss"])
            rsqrt_col(ssc[:, 1:2], ssc[:, 0:1], 1.0 / D, eps6, ["R3:ss", "C:cn"], "R3:rstd", ssc[:, 2:3])
            act(lambda e: e.activation(out=xn, in_=src_ap, func=AF.Copy, scale=ssc[:, 1:2]), [src_res, "R3:rstd"], ["R3:xn"])
            for half in range(2):
                bk, br = psbank()
                bkb = bk.bitcast(BF16)
                for c in range(8):
                    cc = half * 8 + c
                    pe(lambda e, c=c, cc=cc, bkb=bkb: e.transpose(out=bkb[:, c * 128:(c + 1) * 128], in_=xn[:, cc * 128:(cc + 1) * 128], identity=identb),
                       ["R3:xn", "C:identb"], [br])
                g3 = vec(gname, half * 8, 8).unsqueeze(2).to_broadcast([128, 8, 128])
                dve(lambda e, half=half, bkb=bkb, g3=g3: e.tensor_tensor(
                    out=dstT[:, half * 8:half * 8 + 8, i * 128:(i + 1) * 128],
                    in0=bkb[:, 0:1024].rearrange("p (c t) -> p c t", c=8), in1=g3, op=ALU.mult),
                    [br, "C:vec"], ["HT:t%d" % i])

        def p0(l, blk):
            fence_all()
            RR3.reset()
            XT = [f32buf(RR3, D) for _ in range(2)]
            tagp = dict(junk=bfbuf(RR3, D), ss=f32buf(RR3, 8), xn=bfbuf(RR3, D))
            for i, g in enumerate(blk["tiles"]):
                s = i % 2
                sp_dma(XT[s], xsrc(l, g), [yres(l - 1, g)] if l > 0 else [], ["R3:xt%d" % s], "xt%d" % s)
                norm_transpose(XT[s], "R3:xt%d" % s, None, hT, i, "gpm", RR3, tagp)

        def mm_fm(wt, wres, kidx, c0, inT, in_res_fn, t0, n, nk):
            bk, br = psbank()
            for k in range(nk):
                pe(lambda e, k=k, bk=bk: e.matmul(bk[:, 0:n], lhsT=wt[:, kidx + k, c0:c0 + 128], rhs=inT[:, k, t0:t0 + n],
                                                  start=(k == 0), stop=(k == nk - 1)),
                   [wres] + in_res_fn(t0, n), [br])
            return bk, br

        def p1_hgrn(l, blk, bi):
            lb = DER[:, 0:8]
            oml = DER[:, 8:16]
            for h in range(HGH):
                P.fence("R3")
                RR3.reset()
                wA_, wAr = wload(w_hg[l, h, 0], 4096)
                wA = wA_.rearrange("p (k c) -> p k c", k=16)
                wB_, wBr = wload(w_hg[l, h, 1], 4096)
                wB = wB_.rearrange("p (k c) -> p k c", k=16)
                NB = 512
                SQ = f32buf(RR3, NB); FF = f32buf(RR3, NB); SOG = f32buf(RR3, NB); KF = f32buf(RR3, NB)
                BC = f32buf(RR3, NB); TM = f32buf(RR3, NB); TM2 = f32buf(RR3, NB); ORAW = f32buf(RR3, NB)
                QT = bfbuf(RR3, NB); KT = bfbuf(RR3, NB); KHT = bfbuf(RR3, NB)
                V = bfbuf(RR3, NB).rearrange("p (i e) -> p i e", i=4)
                KH = bfbuf(RR3, NB).rearrange("p (i e) -> p i e", i=4)
                AT = bfbuf(RR3, 128)
                VB = bfbuf(RR3, 2048).rearrange("p (s e) -> p s e", s=16)
                GAM = f32buf(RR3, 16)
                SB = [bfbuf(RR3, 128) for _ in range(2)]
                SS = f32buf(RR3, 2048).rearrange("p (s e) -> p s e", s=16)
                SSb = bfbuf(RR3, 2048).rearrange("p (s e) -> p s e", s=16)
                sbc = [0]
                for tg in blk["tg"]:
                    t0, n, kind, nt = tg["t0"], tg["n"], tg["kind"], tg["nt"]
                    for (dst, wt, wr, c0, fn, nm) in [(SQ, wA, wAr, 0, AF.Silu, "sq"), (FF, wA, wAr, 128, AF.Sigmoid, "ff"), (SOG, wB, wBr, 0, AF.Silu, "sog")]:
                        bk, br = mm_fm(wt, wr, 0, c0, hT, hres, t0, n, 16)
                        act(lambda e, dst=dst, bk=bk, fn=fn: e.activation(out=dst[:, 0:n], in_=bk[:, 0:n], func=fn), [br], ["R3:" + nm])
                    bk, br = psbank()
                    for i in range(nt):
                        for k in range(16):
                            pe(lambda e, i=i, k=k, bk=bk: e.matmul(bk[:, i * 128:(i + 1) * 128], lhsT=hT[:, k, t0 + i * 128:t0 + (i + 1) * 128],
                                                                   rhs=wB[:, k, 128:256], start=(k == 0), stop=(k == 15)),
                               [wBr] + hres(t0 + i * 128, 128), [br])
                    dve(lambda e, bk=bk: e.tensor_copy(out=V[:, 0:nt, :], in_=bk[:, 0:n].rearrange("p (i e) -> p i e", i=nt)), [br], ["R3:v"])
                    dve(lambda e: e.tensor_scalar(out=FF[:, 0:n], in0=FF[:, 0:n], scalar1=oml[:, h:h + 1], scalar2=lb[:, h:h + 1], op0=ALU.mult, op1=ALU.add),
                        ["R3:ff", "C:der"], ["R3:ff"])
                    dve(lambda e: e.tensor_scalar(out=KF[:, 0:n], in0=FF[:, 0:n], scalar1=-1.0, scalar2=1.0, op0=ALU.mult, op1=ALU.add), ["R3:ff"], ["R3:kf"])
                    dve(lambda e: e.tensor_scalar_max(out=FF[:, 0:n], in0=FF[:, 0:n], scalar1=1e-30), ["R3:ff"], ["R3:ff"])
                    act(lambda e: e.activation(out=FF[:, 0:n], in_=FF[:, 0:n], func=AF.Ln), ["R3:ff"], ["R3:ff"])
                    if kind == "p":
                        for c in range(nt * 2):
                            dve(lambda e, c=c: e.tensor_tensor_scan(out=BC[:, c * 64:(c + 1) * 64], data0=cst("ones")[:, 0:64], data1=FF[:, c * 64:(c + 1) * 64],
                                                                    initial=0.0, op0=ALU.mult, op1=ALU.add), ["R3:ff", "C:cn"], ["R3:bc"])
                        nseg_t, sl = 2, 64
                    else:
                        dve(lambda e: e.tensor_tensor_scan(out=BC[:, 0:128], data0=cst("pat8"), data1=FF[:, 0:128], initial=0.0, op0=ALU.mult, op1=ALU.add),
                            ["R3:ff", "C:cn"], ["R3:bc"])
                        nseg_t, sl = 16, 8
                    nseg = nseg_t * nt
                    act(lambda e: e.activation(out=TM[:, 0:n], in_=BC[:, 0:n], func=AF.Exp), ["R3:bc"], ["R3:tm"])
                    dve(lambda e: e.tensor_tensor(out=QT[:, 0:n], in0=SQ[:, 0:n], in1=TM[:, 0:n], op=ALU.mult), ["R3:sq", "R3:tm"], ["R3:qt"])
                    act(lambda e: e.activation(out=TM2[:, 0:n], in_=BC[:, 0:n], func=AF.Exp, scale=-1.0), ["R3:bc"], ["R3:tm2"])
                    dve(lambda e: e.tensor_tensor(out=KT[:, 0:n], in0=KF[:, 0:n], in1=TM2[:, 0:n], op=ALU.mult), ["R3:kf", "R3:tm2"], ["R3:kt"])
                    B3 = BC[:, 0:n].rearrange("p (s j) -> p s j", j=sl)
                    bend = B3[:, :, sl - 1:sl]
                    dve(lambda e: e.tensor_tensor(out=TM[:, 0:n].rearrange("p (s j) -> p s j", j=sl), in0=bend.to_broadcast([128, nseg, sl]), in1=B3, op=ALU.subtract),
                        ["R3:bc", "R3:qt"], ["R3:tm"])
                    act(lambda e: e.activation(out=TM[:, 0:n], in_=TM[:, 0:n], func=AF.Exp), ["R3:tm"], ["R3:tm"])
                    dve(lambda e: e.tensor_tensor(out=KHT[:, 0:n], in0=KF[:, 0:n], in1=TM[:, 0:n], op=ALU.mult), ["R3:kf", "R3:tm"], ["R3:kht"])
                    act(lambda e: e.activation(out=GAM[:, 0:nseg].unsqueeze(2), in_=bend, func=AF.Exp), ["R3:bc"], ["R3:gam"])
                    dve(lambda e: e.tensor_scalar(out=SOG[:, 0:n], in0=SOG[:, 0:n], scalar1=vec("hgn", h), scalar2=None, op0=ALU.mult), ["R3:sog", "C:vec"], ["R3:sog"])
                    bk, br = psbank()
                    bkb = bk.bitcast(BF16)
                    for i in range(nt):
                        pe(lambda e, i=i, bkb=bkb: e.transpose(out=bkb[:, i * 128:(i + 1) * 128], in_=KHT[:, i * 128:(i + 1) * 128], identity=identb),
                           ["R3:kht", "C:identb"], [br])
                    dve(lambda e, bkb=bkb: e.tensor_copy(out=KH[:, 0:nt, :], in_=bkb[:, 0:n].rearrange("p (i e) -> p i e", i=nt)), [br], ["R3:kh"])
                    if kind == "s":
                        sp_dma(SS, st_hg[l, :, h].rearrange("s d e -> d s e"), [], ["R3:ss_"], "hgss")
                        act(lambda e: e.activation(out=SSb, in_=SS, func=AF.Copy), ["R3:ss_"], ["R3:ssb"])
                    for i in range(nt):
                        cs = slice(i * 128, (i + 1) * 128)
                        bkA, brA = psbank()
                        pe(lambda e, bkA=bkA, cs=cs: e.matmul(bkA[:, 0:128], lhsT=KT[:, cs], rhs=QT[:, cs], start=True, stop=True), ["R3:kt", "R3:qt"], [brA])
                        mk = cst("hgM_p") if kind == "p" else cst("hgM_s")
                        dve(lambda e, bkA=bkA, mk=mk: e.tensor_tensor(out=AT, in0=bkA[:, 0:128], in1=mk, op=ALU.mult), [brA, "C:cn"], ["R3:at"])
                        ind = cst("ind2") if kind == "p" else cst("ind16")
                        dve(lambda e, i=i, ind=ind: e.tensor_tensor(out=VB[:, 0:nseg_t, :], in0=V[:, i:i + 1, :].to_broadcast([128, nseg_t, 128]),
                                                                    in1=ind.unsqueeze(2).to_broadcast([128, nseg_t, 128]), op=ALU.mult), ["R3:v", "C:cn"], ["R3:vb"])
                        bkO, brO = psbank()
                        pe(lambda e, bkO=bkO, i=i: e.matmul(bkO[:, 0:128], lhsT=V[:, i, :], rhs=AT, start=True, stop=False), ["R3:v", "R3:at"], [brO])
                        if kind == "p":
                            bkD, brD = psbank()
                            pe(lambda e, bkD=bkD, i=i: e.matmul(bkD[:, 0:256], lhsT=KH[:, i, :], rhs=VB[:, 0:2, :].rearrange("p s e -> p (s e)"), start=True, stop=True),
                               ["R3:kh", "R3:vb"], [brD])
                            for sg in range(2):
                                sb = SB[sbc[0] % 2]
                                sbr = "R3:sb%d" % (sbc[0] % 2)
                                sbc[0] += 1
                                act(lambda e, sb=sb: e.activation(out=sb, in_=HGS[:, h, :], func=AF.Copy), ["C:hgs"], [sbr])
                                cc = slice(i * 128 + sg * 64, i * 128 + sg * 64 + 64)
                                pe(lambda e, bkO=bkO, sb=sb, cc=cc, sg=sg: e.matmul(bkO[:, sg * 64:(sg + 1) * 64], lhsT=sb, rhs=QT[:, cc], start=False, stop=(sg == 1)),
                                   [sbr, "R3:qt"], [brO])
                                gi = i * 2 + sg
                                dve(lambda e, bkD=bkD, sg=sg, gi=gi: e.scalar_tensor_tensor(out=HGS[:, h, :], in0=HGS[:, h, :], scalar=GAM[:, gi:gi + 1],
                                                                                            in1=bkD[:, sg * 128:(sg + 1) * 128], op0=ALU.mult, op1=ALU.add),
                                    ["C:hgs", "R3:gam", brD], ["C:hgs"])
                        else:
                            for sg in range(16):
                                pe(lambda e, bkO=bkO, sg=sg: e.matmul(bkO[:, sg * 8:(sg + 1) * 8], lhsT=SSb[:, sg, :], rhs=QT[:, sg * 8:(sg + 1) * 8], start=False, stop=(sg == 15)),
                                   ["R3:ssb", "R3:qt"], [brO])
                            dve(lambda e: e.tensor_tensor(out=SS, in0=SS, in1=GAM[:, 0:16].unsqueeze(2).to_broadcast([128, 16, 128]), op=ALU.mult),
                                ["R3:ss_", "R3:gam", "R3:ssb"], ["R3:ss_"])
                            for g4 in range(4):
                                bkD, brD = psbank()
                                pe(lambda e, bkD=bkD, g4=g4: e.matmul(bkD[:, 0:512], lhsT=KH[:, 0, :], rhs=VB[:, g4 * 4:(g4 + 1) * 4, :].rearrange("p s e -> p (s e)"),
                                                                      start=True, stop=True), ["R3:kh", "R3:vb"], [brD])
                                dve(lambda e, bkD=bkD, g4=g4: e.tensor_tensor(out=SS[:, g4 * 4:(g4 + 1) * 4, :], in0=SS[:, g4 * 4:(g4 + 1) * 4, :],
                                                                              in1=bkD[:, 0:512].rearrange("p (s e) -> p s e", s=4), op=ALU.add),
                                    ["R3:ss_", brD], ["R3:ss_"])
                            sp_dma(o_hg_s[l, :, h].rearrange("s d e -> d s e"), SS, ["R3:ss_"], [], "hgss")
                        act(lambda e, bkO=bkO, cs=cs: e.activation(out=ORAW[:, cs], in_=bkO[:, 0:128], func=AF.Copy), [brO], ["R3:oraw"])
                    act(lambda e: e.activation(out=TM2[:, 0:n], in_=ORAW[:, 0:n], func=AF.Square), ["R3:oraw"], ["R3:tm2"])
                    bk, br = psbank()
                    pe(lambda e, bk=bk: e.matmul(bk[:, 0:n], lhsT=cst("ones"), rhs=TM2[:, 0:n], start=True, stop=True), ["R3:tm2", "C:cn"], [br])
                    act(lambda e, bk=bk: e.activation(out=TM[:, 0:n], in_=bk[:, 0:n], func=AF.Sqrt, bias=eps6, scale=1.0 / 128), [br, "C:cn"], ["R3:tm"])
                    dve(lambda e: e.reciprocal(out=TM[:, 0:n], in_=TM[:, 0:n]), ["R3:tm"], ["R3:tm"])
                    dve(lambda e: e.tensor_tensor(out=TM[:, 0:n], in0=TM[:, 0:n], in1=ORAW[:, 0:n], op=ALU.mult), ["R3:tm", "R3:oraw"], ["R3:tm"])
                    dve(lambda e: e.tensor_tensor(out=o_all[:, h, t0:t0 + n], in0=TM[:, 0:n], in1=SOG[:, 0:n], op=ALU.mult), ["R3:tm", "R3:sog"], ["OA:c%d" % h])
                if bi == 1:
                    sp_dma(o_hg_p[l, h], HGS[:, h, :], ["C:hgs"], [], "hgs_out%d" % h)

        def p1_rwkv(l, blk, bi):
            omka = DER[:, 16:24]
            P.fence("R3")
            RR3.reset()
            LORA = bfbuf(RR3, 2 * 1152).rearrange("p (c t) -> p c t", c=2)
            W2b = bfbuf(RR3, 1024); A2b = bfbuf(RR3, 1024); G2b = bfbuf(RR3, 1024)
            SHs = f32buf(RR3, 26 * 16).rearrange("p (c s) -> p c s", c=26)
            LASTs = f32buf(RR3, 26 * 16).rearrange("p (c s) -> p c s", c=26)
            RAW = f32buf(RR3, 3 * 520).rearrange("p (j t) -> p j t", j=3)
            RAWs = f32buf(RR3, 3 * 144).rearrange("p (j s t) -> p j s t", j=3, s=16)
            mark1 = RR3.cur
            P.dma("pool", lambda e: e.dma_start(out=W2b[0:64, :], in_=w2_d[l]), [], ["R3:w2b"], "w2b")
            P.dma("pool", lambda e: e.dma_start(out=A2b[64:128, :], in_=a2_d[l]), [], ["R3:a2b"], "a2b")
            P.dma("pool", lambda e: e.dma_start(out=G2b, in_=g2_d[l]), [], ["R3:g2b"], "g2b")
            has_s = any(tg["kind"] == "s" for tg in blk["tg"])
            if has_s:
                SHI = f32buf(RR3, SHW)
                sp_dma(SHI[0:16, :], st_sh[l], [], ["R3:shi"], "shi")
                for c0 in range(0, 26, 4):
                    ncn = min(4, 26 - c0)
                    bk, br = psbank()
                    for c in range(ncn):
                        pe(lambda e, c=c, c0=c0, bk=bk: e.matmul(bk[:, c * 16:(c + 1) * 16], lhsT=SHI[0:16, (c0 + c) * 128:(c0 + c + 1) * 128], rhs=identf[0:16, 0:16],
                                                                 start=True, stop=True), ["R3:shi", "C:cn"], [br])
                    dve(lambda e, c0=c0, ncn=ncn, bk=bk: e.tensor_copy(out=SHs[:, c0:c0 + ncn, :], in_=bk[:, 0:ncn * 16].rearrange("p (c s) -> p c s", c=ncn)), [br], ["R3:shs"])

            def proj_chunk(wt, wr, c0, chunk, tg, dstX, dres, j):
                t0, n, kind = tg["t0"], tg["n"], tg["kind"]
                bk, br = mm_fm(wt, wr, 0, c0, hT, hres, t0, n, 16)
                mu_c = vec("mu", chunk)
                rr = "R3:raw%d" % j
                if kind == "p":
                    if t0 == 0:
                        act(lambda e: e.activation(out=RAW[:, j, 0:1], in_=PRVC[:, chunk:chunk + 1], func=AF.Copy), ["C:prvc"], [rr])
                    act(lambda e, bk=bk: e.activation(out=RAW[:, j, 1:1 + n], in_=bk[:, 0:n], func=AF.Copy), [br], [rr])
                    dve(lambda e: e.tensor_tensor(out=dstX[:, 0:n], in0=RAW[:, j, 0:n], in1=RAW[:, j, 1:1 + n], op=ALU.subtract), [rr], [dres])
                    dve(lambda e: e.scalar_tensor_tensor(out=dstX[:, 0:n], in0=dstX[:, 0:n], scalar=mu_c, in1=RAW[:, j, 1:1 + n], op0=ALU.mult, op1=ALU.add),
                        [dres, rr, "C:vec"], [dres])
                    act(lambda e: e.activation(out=PRVC[:, chunk:chunk + 1], in_=RAW[:, j, n:n + 1], func=AF.Copy), [rr], ["C:prvc"])
                    act(lambda e: e.activation(out=RAW[:, j, 0:1], in_=PRVC[:, chunk:chunk + 1], func=AF.Copy), ["C:prvc", dres], [rr])
                else:
                    rs_ = "R3:raws%d" % j
                    act(lambda e: e.activation(out=RAWs[:, j, :, 0:1], in_=SHs[:, chunk, :].unsqueeze(2), func=AF.Copy), ["R3:shs"], [rs_])
                    act(lambda e, bk=bk: e.activation(out=RAWs[:, j, :, 1:9], in_=bk[:, 0:128].rearrange("p (s t) -> p s t", s=16), func=AF.Copy), [br], [rs_])
                    d3 = dstX[:, 0:128].rearrange("p (s t) -> p s t", s=16)
                    dve(lambda e: e.tensor_tensor(out=d3, in0=RAWs[:, j, :, 0:8], in1=RAWs[:, j, :, 1:9], op=ALU.subtract), [rs_], [dres])
                    dve(lambda e: e.scalar_tensor_tensor(out=d3, in0=d3, scalar=mu_c, in1=RAWs[:, j, :, 1:9], op0=ALU.mult, op1=ALU.add),
                        [dres, rs_, "C:vec"], [dres])
                    act(lambda e: e.activation(out=LASTs[:, chunk, :].unsqueeze(2), in_=RAWs[:, j, :, 8:9], func=AF.Copy), [rs_], ["R3:lasts"])

            XL = [f32buf(RR3, 512) for _ in range(2)]
            wl_, wlr = wload(w_rwl[l], 4096)
            wl = wl_.rearrange("p (k c) -> p k c", k=16)
            for tg in blk["tg"]:
                t0, n = tg["t0"], tg["n"]
                for j in range(2):
                    proj_chunk(wl, wlr, j * 128, 24 + j, tg, XL[j], "R3:xl%d" % j, j)
                act(lambda e, t0=t0, n=n: e.activation(out=LORA[0:64, 0, t0:t0 + n], in_=XL[0][0:64, 0:n], func=AF.Tanh), ["R3:xl0"], ["R3:lora"])
                act(lambda e, t0=t0, n=n: e.activation(out=LORA[64:128, 0, t0:t0 + n], in_=XL[0][64:128, 0:n], func=AF.Copy), ["R3:xl0"], ["R3:lora"])
                act(lambda e, t0=t0, n=n: e.activation(out=LORA[:, 1, t0:t0 + n], in_=XL[1][:, 0:n], func=AF.Sigmoid), ["R3:xl1"], ["R3:lora"])
            if _STOP == 'rwl': raise _Stop()

            for p in range(RWP):
                wrk_, wrkr = wload(w_rwrk[l, p], 4096)
                wrk = wrk_.rearrange("p (k c) -> p k c", k=16)
                wv_, wvr = wload(w_rwv[l, p], 2048)
                wv = wv_.rearrange("p (k c) -> p k c", k=16)
                pc = slice(p * 128, (p + 1) * 128)
                for tg in blk["tg"]:
                    t0, n, kind, nt = tg["t0"], tg["n"], tg["kind"], tg["nt"]
                    P.fence("R3")
                    RR3.reset(mark1)
                    NB = n
                    XR = f32buf(RR3, NB); XK = f32buf(RR3, NB); XV = f32buf(RR3, NB)
                    S_ = f32buf(RR3, NB); A_ = f32buf(RR3, NB); G_ = f32buf(RR3, NB); KAP = f32buf(RR3, NB)
                    CB = f32buf(RR3, NB); T1 = f32buf(RR3, NB); T2 = f32buf(RR3, NB); BON = f32buf(RR3, NB); ORAW = f32buf(RR3, NB)
                    QR = bfbuf(RR3, 2 * NB).rearrange("p (c t) -> p c t", c=2)
                    KB = bfbuf(RR3, 2 * NB).rearrange("p (c t) -> p c t", c=2)
                    VXb = bfbuf(RR3, NB)
                    TMv = bfbuf(RR3, 3 * NB).rearrange("p (c i e) -> p c i e", c=3, i=nt)
                    GAM = f32buf(RR3, 16)
                    Pm = [bfbuf(RR3, 128) for _ in range(2 * nt)]; PTm = [bfbuf(RR3, 128) for _ in range(2 * nt)]
                    TTf = [f32buf(RR3, 128) for _ in range(2 * nt)]; TTb = [bfbuf(RR3, 128) for _ in range(2 * nt)]
                    AKK = [bfbuf(RR3, 128) for _ in range(2 * nt)]; ARK = [bfbuf(RR3, 128) for _ in range(2 * nt)]; ARB = [bfbuf(RR3, 128) for _ in range(2 * nt)]
                    Wb = bfbuf(RR3, 128); Un = bfbuf(RR3, 128)
                    Vm = [bfbuf(RR3, 128) for _ in range(2)]; Unm = [bfbuf(RR3, 128) for _ in range(2)]
                    Hb = bfbuf(RR3, 128)
                    HT1 = f32buf(RR3, 128)
                    if kind == "s":
                        SBD0 = f32buf(RR3, 2048).rearrange("p (s e) -> p s e", s=16)
                        HF = f32buf(RR3, 2048).rearrange("p (s e) -> p s e", s=16)
                        HBb = bfbuf(RR3, 2048).rearrange("p (s e) -> p s e", s=16)
                        ZK = bfbuf(RR3, 2048).rearrange("p (s t) -> p s t", s=16)
                        VBK = bfbuf(RR3, 2048).rearrange("p (s e) -> p s e", s=16)
                        UNB = bfbuf(RR3, 2048).rearrange("p (s e) -> p s e", s=16)
                    proj_chunk(wrk, wrkr, 0, p, tg, XR, "R3:xr", 0)
                    proj_chunk(wrk, wrkr, 128, 8 + p, tg, XK, "R3:xk", 1)
                    proj_chunk(wv, wvr, 0, 16 + p, tg, XV, "R3:xv", 2)
                    if kind == "s": _ck("s_proj")
                    bk, br = psbank()
                    pe(lambda e, bk=bk: e.matmul(bk[:, 0:n], lhsT=W2b[0:64, pc], rhs=LORA[0:64, 0, t0:t0 + n], start=True, stop=True), ["R3:w2b", "R3:lora"], [br])
                    act(lambda e, bk=bk: e.activation(out=S_[:, 0:n], in_=bk[:, 0:n], func=AF.Sigmoid, bias=vec("w0", p), scale=1.0), [br, "C:vec"], ["R3:s"])
                    bk, br = psbank()
                    pe(lambda e, bk=bk: e.matmul(bk[:, 0:n], lhsT=A2b[64:128, pc], rhs=LORA[64:128, 0, t0:t0 + n], start=True, stop=True), ["R3:a2b", "R3:lora"], [br])
                    act(lambda e, bk=bk: e.activation(out=A_[:, 0:n], in_=bk[:, 0:n], func=AF.Sigmoid, bias=vec("a0", p), scale=1.0), [br, "C:vec"], ["R3:a"])
                    bk, br = psbank()
                    pe(lambda e, bk=bk: e.matmul(bk[:, 0:n], lhsT=G2b[:, pc], rhs=LORA[:, 1, t0:t0 + n], start=True, stop=True), ["R3:g2b", "R3:lora"], [br])
                    act(lambda e, bk=bk: e.activation(out=G_[:, 0:n], in_=bk[:, 0:n], func=AF.Copy), [br], ["R3:g"])
                    if kind == "s": _ck("s_lora")
                    dve(lambda e: e.tensor_scalar(out=KAP[:, 0:n], in0=XK[:, 0:n], scalar1=vec("kk", p), scalar2=None, op0=ALU.mult), ["R3:xk", "C:vec"], ["R3:kap"])
                    dve(lambda e: e.tensor_tensor(out=T1[:, 0:n], in0=KAP[:, 0:n], in1=KAP[:, 0:n], op=ALU.mult), ["R3:kap"], ["R3:t1"])
                    bk, br = psbank()
                    pe(lambda e, bk=bk: e.matmul(bk[:, 0:n], lhsT=cst("bd2"), rhs=T1[:, 0:n], start=True, stop=True), ["R3:t1", "C:cn"], [br])
                    dve(lambda e, bk=bk: e.tensor_scalar_max(out=T2[:, 0:n], in0=bk[:, 0:n], scalar1=1e-24), [br], ["R3:t2"])
                    act(lambda e: e.activation(out=T2[:, 0:n], in_=T2[:, 0:n], func=AF.Sqrt), ["R3:t2"], ["R3:t2"])
                    dve(lambda e: e.reciprocal(out=T2[:, 0:n], in_=T2[:, 0:n]), ["R3:t2"], ["R3:t2"])
                    dve(lambda e: e.tensor_tensor(out=KAP[:, 0:n], in0=KAP[:, 0:n], in1=T2[:, 0:n], op=ALU.mult), ["R3:kap", "R3:t2"], ["R3:kap"])
                    dve(lambda e: e.tensor_scalar(out=T1[:, 0:n], in0=A_[:, 0:n], scalar1=vec("ka", p), scalar2=omka[:, p:p + 1], op0=ALU.mult, op1=ALU.add),
                        ["R3:a", "C:vec", "C:der"], ["R3:t1"])
                    dve(lambda e: e.tensor_tensor(out=XK[:, 0:n], in0=XK[:, 0:n], in1=T1[:, 0:n], op=ALU.mult), ["R3:xk", "R3:t1", "R3:kap"], ["R3:xk"])
                    dve(lambda e: e.tensor_tensor(out=T1[:, 0:n], in0=XR[:, 0:n], in1=XK[:, 0:n], op=ALU.mult), ["R3:xr", "R3:xk"], ["R3:t1"])
                    dve(lambda e: e.tensor_scalar(out=T1[:, 0:n], in0=T1[:, 0:n], scalar1=vec("rk", p), scalar2=None, op0=ALU.mult), ["R3:t1", "C:vec"], ["R3:t1"])
                    bk, br = psbank()
                    pe(lambda e, bk=bk: e.matmul(bk[:, 0:n], lhsT=cst("bd2"), rhs=T1[:, 0:n], start=True, stop=True), ["R3:t1", "C:cn"], [br])
                    dve(lambda e, bk=bk: e.tensor_tensor(out=BON[:, 0:n], in0=XV[:, 0:n], in1=bk[:, 0:n], op=ALU.mult), ["R3:xv", br], ["R3:bon"])
                    if kind == "s": _ck("s_bon")
                    dve(lambda e: e.tensor_scalar(out=S_[:, 0:n], in0=S_[:, 0:n], scalar1=-DEC_C, scalar2=None, op0=ALU.mult), ["R3:s"], ["R3:s"])
                    if kind == "p":
                        for i in range(nt):
                            dve(lambda e, i=i: e.tensor_tensor_scan(out=CB[:, i * 128:(i + 1) * 128], data0=cst("ones"), data1=S_[:, i * 128:(i + 1) * 128],
                                                                    initial=0.0, op0=ALU.mult, op1=ALU.add), ["R3:s", "C:cn"], ["R3:cb"])
                        nseg_t, sl = 1, 128
                    else:
                        dve(lambda e: e.tensor_tensor_scan(out=CB[:, 0:128], data0=cst("pat8"), data1=S_[:, 0:128], initial=0.0, op0=ALU.mult, op1=ALU.add),
                            ["R3:s", "C:cn"], ["R3:cb"])
                        nseg_t, sl = 16, 8
                    nseg = nseg_t * nt
                    act(lambda e: e.activation(out=T1[:, 0:n], in_=CB[:, 0:n], func=AF.Exp), ["R3:cb", "R3:t1"], ["R3:t1"])
                    dve(lambda e: e.tensor_tensor(out=QR[:, 1, 0:n], in0=XR[:, 0:n], in1=T1[:, 0:n], op=ALU.mult), ["R3:xr", "R3:t1"], ["R3:qr1"])
                    act(lambda e: e.activation(out=GAM[:, 0:nseg].unsqueeze(2), in_=T1[:, 0:n].rearrange("p (s j) -> p s j", j=sl)[:, :, sl - 1:sl], func=AF.Copy),
                        ["R3:t1"], ["R3:gam"])
                    dve(lambda e: e.tensor_tensor(out=T2[:, 0:n], in0=CB[:, 0:n], in1=S_[:, 0:n], op=ALU.subtract), ["R3:cb", "R3:s", "R3:t2"], ["R3:t2"])
                    act(lambda e: e.activation(out=T2[:, 0:n], in_=T2[:, 0:n], func=AF.Exp), ["R3:t2"], ["R3:t2"])
                    dve(lambda e: e.tensor_tensor(out=QR[:, 0, 0:n], in0=KAP[:, 0:n], in1=T2[:, 0:n], op=ALU.mult), ["R3:kap", "R3:t2"], ["R3:qr0"])
                    act(lambda e: e.activation(out=T1[:, 0:n], in_=CB[:, 0:n], func=AF.Exp, scale=-1.0), ["R3:cb", "R3:qr1", "R3:gam"], ["R3:t1"])
                    dve(lambda e: e.tensor_tensor(out=KB[:, 0, 0:n], in0=XK[:, 0:n], in1=T1[:, 0:n], op=ALU.mult), ["R3:xk", "R3:t1"], ["R3:kb0"])
                    dve(lambda e: e.tensor_tensor(out=T2[:, 0:n], in0=A_[:, 0:n], in1=KAP[:, 0:n], op=ALU.mult), ["R3:a", "R3:kap", "R3:qr0"], ["R3:t2"])
                    dve(lambda e: e.tensor_tensor(out=KB[:, 1, 0:n], in0=T2[:, 0:n], in1=T1[:, 0:n], op=ALU.mult), ["R3:t2", "R3:t1"], ["R3:kb1"])
                    act(lambda e: e.activation(out=VXb[:, 0:n], in_=XV[:, 0:n], func=AF.Copy), ["R3:xv"], ["R3:vxb"])
                    if kind == "s": _ck("s_elem")
                    for ci, (src, sres) in enumerate([(VXb, "R3:vxb"), (KB[:, 0, :], "R3:kb0"), (KB[:, 1, :], "R3:kb1")]):
                        bk, br = psbank()
                        bkb = bk.bitcast(BF16)
                        for i in range(nt):
                            pe(lambda e, i=i, bkb=bkb, src=src: e.transpose(out=bkb[:, i * 128:(i + 1) * 128], in_=src[:, i * 128:(i + 1) * 128], identity=identb),
                               [sres, "C:identb"], [br])
                        dve(lambda e, bkb=bkb, ci=ci: e.tensor_copy(out=TMv[:, ci, 0:nt, :], in_=bkb[:, 0:n].rearrange("p (i e) -> p i e", i=nt)), [br], ["R3:tmv%d" % ci])
                    if kind == "s": _ck("s_tm")
                    if _STOP == 'rwa': raise _Stop()
                    if kind == "s":
                        dve(lambda e: e.memset(SBD0, 0.0), [], ["R3:sbd0"])
                        dve(lambda e: e.memset(ZK, 0.0), [], ["R3:zk"])
                        for h2 in range(2):
                            sp_dma(SBD0[h2 * 64:(h2 + 1) * 64, :, h2 * 64:(h2 + 1) * 64], st_rw[l, :, 2 * p + h2].rearrange("s i j -> i s j"), ["R3:sbd0"], ["R3:sbd0_%d" % h2], "sbd%d" % h2)
                        _ck("s_dma")
                        for g4 in range(4):
                            bk, br = psbank()
                            for s4 in range(4):
                                sg = g4 * 4 + s4
                                pe(lambda e, bk=bk, s4=s4, sg=sg: e.matmul(bk[:, s4 * 128:(s4 + 1) * 128], lhsT=SBD0[:, sg, :], rhs=identf, start=True, stop=True),
                                   ["R3:sbd0", "R3:sbd0_0", "R3:sbd0_1", "C:cn"], [br])
                            dve(lambda e, bk=bk, g4=g4: e.tensor_copy(out=HF[:, g4 * 4:(g4 + 1) * 4, :], in_=bk[:, 0:512].rearrange("p (s e) -> p s e", s=4)),
                                [br], ["R3:hf"])
                            act(lambda e, g4=g4: e.activation(out=HBb[:, g4 * 4:(g4 + 1) * 4, :], in_=HF[:, g4 * 4:(g4 + 1) * 4, :], func=AF.Copy),
                                ["R3:hf"], ["R3:hbb"])
                        _ck("s_loaded")
                    for i in range(nt):
                        cs = slice(i * 128, (i + 1) * 128)
                        sfx = "_p" if kind == "p" else "_s"
                        nlev = 7 if kind == "p" else 3
                        for h2 in range(2):
                            q = i * 2 + h2
                            hr = slice(h2 * 64, (h2 + 1) * 64)
                            bk1, br1 = psbank()
                            pe(lambda e, bk1=bk1, hr=hr, cs=cs: e.matmul(bk1[:, 0:128], lhsT=QR[hr, 0, cs], rhs=KB[hr, 1, cs], start=True, stop=True), ["R3:qr0", "R3:kb1"], [br1])
                            dve(lambda e, bk1=bk1, h2=h2: e.scalar_tensor_tensor(out=Pm[q], in0=bk1[:, 0:128], scalar=-1.0, in1=cst("rwL" + sfx), op0=ALU.mult, op1=ALU.mult),
                                [br1, "C:cn"], ["R3:pm%d" % q])
                            bk2, br2 = psbank()
                            pe(lambda e, bk2=bk2, hr=hr, cs=cs: e.matmul(bk2[:, 0:256].rearrange("p (c t) -> p c t", c=2), lhsT=KB[hr, 1, cs], rhs=QR[hr, :, cs], start=True, stop=True),
                               ["R3:qr0", "R3:qr1", "R3:kb1"], [br2])
                            dve(lambda e, bk2=bk2, h2=h2: e.scalar_tensor_tensor(out=PTm[q], in0=bk2[:, 0:128], scalar=-1.0, in1=cst("rwST" + sfx), op0=ALU.mult, op1=ALU.mult),
                                [br2, "C:cn"], ["R3:ptm%d" % q])
                            dve(lambda e, bk2=bk2, h2=h2: e.tensor_tensor(out=ARB[q], in0=bk2[:, 128:256], in1=cst("rwIT" + sfx), op=ALU.mult), [br2, "C:cn"], ["R3:arb%d" % q])
                            bk3, br3 = psbank()
                            pe(lambda e, bk3=bk3, hr=hr, cs=cs: e.matmul(bk3[:, 0:256].rearrange("p (c t) -> p c t", c=2), lhsT=KB[hr, 0, cs], rhs=QR[hr, :, cs], start=True, stop=True),
                               ["R3:qr0", "R3:qr1", "R3:kb0"], [br3])
                            dve(lambda e, bk3=bk3, h2=h2: e.tensor_tensor(out=AKK[q], in0=bk3[:, 0:128], in1=cst("rwST" + sfx), op=ALU.mult), [br3, "C:cn"], ["R3:akk%d" % q])
                            dve(lambda e, bk3=bk3, h2=h2: e.tensor_tensor(out=ARK[q], in0=bk3[:, 128:256], in1=cst("rwIT" + sfx), op=ALU.mult), [br3, "C:cn"], ["R3:ark%d" % q])
                            dve(lambda e, h2=h2: e.tensor_tensor(out=TTf[q], in0=PTm[q], in1=identf, op=ALU.add), ["R3:ptm%d" % q, "C:cn"], ["R3:ttf%d" % q])
                            act(lambda e, h2=h2: e.activation(out=TTb[q], in_=TTf[q], func=AF.Copy), ["R3:ttf%d" % q], ["R3:ttb%d" % q])
                    for lev in range(1, nlev):
                        for i in range(nt):
                            for h2 in range(2):
                                q = i * 2 + h2
                                bka, bra = psbank()
                                pe(lambda e, bka=bka, h2=h2: e.matmul(bka[:, 0:128], lhsT=PTm[q], rhs=Pm[q], start=True, stop=True), ["R3:pm%d" % q, "R3:ptm%d" % q], [bra])
                                pe(lambda e, bka=bka, h2=h2: e.matmul(bka[:, 128:256], lhsT=Pm[q], rhs=PTm[q], start=True, stop=True), ["R3:pm%d" % q, "R3:ptm%d" % q], [bra])
                                act(lambda e, bka=bka, h2=h2: e.activation(out=Pm[q], in_=bka[:, 0:128], func=AF.Copy), [bra], ["R3:pm%d" % q])
                                dve(lambda e, bka=bka, h2=h2: e.tensor_copy(out=PTm[q], in_=bka[:, 128:256]), [bra], ["R3:ptm%d" % q])
                                bkt, brt = psbank()
                                pe(lambda e, bkt=bkt, h2=h2: e.matmul(bkt[:, 0:128], lhsT=Pm[q], rhs=TTb[q], start=True, stop=True), ["R3:pm%d" % q, "R3:ttb%d" % q], [brt])
                                dve(lambda e, bkt=bkt, h2=h2: e.tensor_tensor(out=TTf[q], in0=TTf[q], in1=bkt[:, 0:128], op=ALU.add), ["R3:ttf%d" % q, brt], ["R3:ttf%d" % q])
                                act(lambda e, h2=h2: e.activation(out=TTb[q], in_=TTf[q], func=AF.Copy), ["R3:ttf%d" % q], ["R3:ttb%d" % q])
                    for i in range(nt):
                        cs = slice(i * 128, (i + 1) * 128)
                        if kind == "s": _ck("s_inv")
                        if _STOP == 'rwb': raise _Stop()
                        bkW, brW = psbank()
                        if kind == "p":
                            act(lambda e: e.activation(out=Hb, in_=RWS[:, p, :], func=AF.Copy), ["C:rws"], ["R3:hb"])
                            pe(lambda e, bkW=bkW, cs=cs: e.matmul(bkW[:, 0:128], lhsT=QR[:, 0, cs], rhs=Hb, start=True, stop=False), ["R3:qr0", "R3:hb"], [brW])
                        else:
                            for sg in range(16):
                                dve(lambda e, sg=sg: e.tensor_copy(out=ZK[:, sg, sg * 8:(sg + 1) * 8], in_=QR[:, 0, sg * 8:(sg + 1) * 8]), ["R3:qr0"], ["R3:zk"])
                            for sg in range(16):
                                pe(lambda e, bkW=bkW, sg=sg: e.matmul(bkW[:, 0:128], lhsT=ZK[:, sg, :], rhs=HBb[:, sg, :], start=(sg == 0), stop=False), ["R3:zk", "R3:hbb"], [brW])
                        for h2 in range(2):
                            hc = slice(h2 * 64, (h2 + 1) * 64)
                            pe(lambda e, bkW=bkW, h2=h2, hc=hc, i=i: e.matmul(bkW[:, hc], lhsT=AKK[i * 2 + h2], rhs=TMv[:, 0, i, hc], start=False, stop=(h2 == 1)), ["R3:akk%d" % (i * 2 + h2), "R3:tmv0"], [brW])
                        act(lambda e, bkW=bkW: e.activation(out=Wb, in_=bkW[:, 0:128], func=AF.Copy), [brW], ["R3:wb"])
                        if kind == "s": _ck("s_W")
                        bkU, brU = psbank()
                        for h2 in range(2):
                            hc = slice(h2 * 64, (h2 + 1) * 64)
                            pe(lambda e, bkU=bkU, h2=h2, hc=hc: e.matmul(bkU[:, hc], lhsT=TTb[i * 2 + h2], rhs=Wb[:, hc], start=True, stop=True), ["R3:ttb%d" % (i * 2 + h2), "R3:wb"], [brU])
                        act(lambda e, bkU=bkU: e.activation(out=Un, in_=bkU[:, 0:128], func=AF.Copy, scale=-1.0), [brU], ["R3:un"])
                        if kind == "s": _ck("s_U")
                        for h2 in range(2):
                            hc = slice(h2 * 64, (h2 + 1) * 64)
                            oc = slice((1 - h2) * 64, (2 - h2) * 64)
                            dve(lambda e, h2=h2, oc=oc: e.memset(Vm[h2][:, oc], 0.0), [], ["R3:vm%d" % h2])
                            dve(lambda e, h2=h2, hc=hc, i=i: e.tensor_copy(out=Vm[h2][:, hc], in_=TMv[:, 0, i, hc]), ["R3:tmv0"], ["R3:vm%d" % h2])
                            dve(lambda e, h2=h2, oc=oc: e.memset(Unm[h2][:, oc], 0.0), [], ["R3:unm%d" % h2])
                            dve(lambda e, h2=h2, hc=hc: e.tensor_copy(out=Unm[h2][:, hc], in_=Un[:, hc]), ["R3:un"], ["R3:unm%d" % h2])
                        bkO, brO = psbank()
                        for h2 in range(2):
                            pe(lambda e, bkO=bkO, h2=h2: e.matmul(bkO[:, 0:128], lhsT=Vm[h2], rhs=ARK[i * 2 + h2], start=(h2 == 0), stop=False), ["R3:vm%d" % h2, "R3:ark%d" % (i * 2 + h2)], [brO])
                            pe(lambda e, bkO=bkO, h2=h2: e.matmul(bkO[:, 0:128], lhsT=Unm[h2], rhs=ARB[i * 2 + h2], start=False, stop=False), ["R3:unm%d" % h2, "R3:arb%d" % (i * 2 + h2)], [brO])
                        if kind == "p":
                            pe(lambda e, bkO=bkO, cs=cs: e.matmul(bkO[:, 0:128], lhsT=Hb, rhs=QR[:, 1, cs], start=False, stop=True), ["R3:hb", "R3:qr1"], [brO])
                        else:
                            for sg in range(16):
                                pe(lambda e, bkO=bkO, sg=sg: e.matmul(bkO[:, sg * 8:(sg + 1) * 8], lhsT=HBb[:, sg, :], rhs=QR[:, 1, sg * 8:(sg + 1) * 8], start=False, stop=(sg == 15)),
                                   ["R3:hbb", "R3:qr1"], [brO])
                        act(lambda e, bkO=bkO, cs=cs: e.activation(out=ORAW[:, cs], in_=bkO[:, 0:128], func=AF.Copy), [brO], ["R3:oraw"])
                        if kind == "s": _ck("s_O")
                        if kind == "p":
                            bkH, brH = psbank()
                            pe(lambda e, bkH=bkH, i=i: e.matmul(bkH[:, 0:128], lhsT=TMv[:, 1, i, :], rhs=TMv[:, 0, i, :], start=True, stop=False), ["R3:tmv1", "R3:tmv0"], [brH])
                            pe(lambda e, bkH=bkH, i=i: e.matmul(bkH[:, 0:128], lhsT=TMv[:, 2, i, :], rhs=Un, start=False, stop=True), ["R3:tmv2", "R3:un"], [brH])
                            dve(lambda e, bkH=bkH: e.tensor_tensor(out=HT1, in0=bkH[:, 0:128], in1=cst("bd2b"), op=ALU.mult), [brH, "C:cn"], ["R3:ht1"])
                            dve(lambda e: e.tensor_tensor(out=HT1, in0=HT1, in1=RWS[:, p, :], op=ALU.add), ["R3:ht1", "C:rws"], ["R3:ht1"])
                            dve(lambda e, i=i: e.tensor_scalar(out=RWS[:, p, :], in0=HT1, scalar1=GAM[:, i:i + 1], scalar2=None, op0=ALU.mult), ["R3:ht1", "R3:gam", "R3:hb"], ["C:rws"])
                            if _STOP == 'rwc': raise _Stop()
                        else:
                            ind = cst("ind16")
                            dve(lambda e: e.tensor_tensor(out=VBK, in0=TMv[:, 0, 0:1, :].to_broadcast([128, 16, 128]), in1=ind.unsqueeze(2).to_broadcast([128, 16, 128]), op=ALU.mult),
                                ["R3:tmv0", "C:cn"], ["R3:vbk"])
                            dve(lambda e: e.tensor_tensor(out=UNB, in0=Un.unsqueeze(1).to_broadcast([128, 16, 128]), in1=ind.unsqueeze(2).to_broadcast([128, 16, 128]), op=ALU.mult),
                                ["R3:un", "C:cn"], ["R3:unb"])
                            _ck("s_blk")
                            for g4 in range(4):
                                g4s = slice(g4 * 4, (g4 + 1) * 4)
                                bkH, brH = psbank()
                                pe(lambda e, bkH=bkH, g4s=g4s: e.matmul(bkH[:, 0:512], lhsT=TMv[:, 1, 0, :], rhs=VBK[:, g4s, :].rearrange("p s e -> p (s e)"), start=True, stop=False),
                                   ["R3:tmv1", "R3:vbk"], [brH])
                                pe(lambda e, bkH=bkH, g4s=g4s: e.matmul(bkH[:, 0:512], lhsT=TMv[:, 2, 0, :], rhs=UNB[:, g4s, :].rearrange("p s e -> p (s e)"), start=False, stop=True),
                                   ["R3:tmv2", "R3:unb"], [brH])
                                b3 = bkH[:, 0:512].rearrange("p (s e) -> p s e", s=4)
                                dve(lambda e, b3=b3, g4s=g4s: e.tensor_tensor(out=SBD0[:, g4s, :], in0=b3, in1=cst("bd2b").unsqueeze(1).to_broadcast([128, 4, 128]), op=ALU.mult),
                                    [brH, "C:cn", "R3:sbd0_0", "R3:sbd0_1"], ["R3:sbd0"])
                                dve(lambda e, g4s=g4s: e.tensor_tensor(out=HF[:, g4s, :], in0=HF[:, g4s, :], in1=SBD0[:, g4s, :], op=ALU.add), ["R3:sbd0", "R3:hf"], ["R3:hf"])
                            dve(lambda e: e.tensor_tensor(out=HF, in0=HF, in1=GAM[:, 0:16].unsqueeze(2).to_broadcast([128, 16, 128]), op=ALU.mult), ["R3:hf", "R3:gam"], ["R3:hf"])
                            _ck("s_hf")
                            for g4 in range(4):
                                bk, br = psbank()
                                for s4 in range(4):
                                    sg = g4 * 4 + s4
                                    pe(lambda e, bk=bk, s4=s4, sg=sg: e.matmul(bk[:, s4 * 128:(s4 + 1) * 128], lhsT=HF[:, sg, :], rhs=identf, start=True, stop=True),
                                       ["R3:hf", "C:cn"], [br])
                                dve(lambda e, bk=bk, g4=g4: e.tensor_copy(out=SBD0[:, g4 * 4:(g4 + 1) * 4, :], in_=bk[:, 0:512].rearrange("p (s e) -> p s e", s=4)),
                                    [br, "R3:sbd0"], ["R3:sbd0"])
                            _ck("s_tr")
                            for h2 in range(2):
                                sp_dma(o_rw_s[l, :, 2 * p + h2].rearrange("s i j -> i s j"), SBD0[h2 * 64:(h2 + 1) * 64, :, h2 * 64:(h2 + 1) * 64], ["R3:sbd0"], [], "sbd%d" % h2)
                    bk, br = psbank()
                    pe(lambda e, bk=bk: e.matmul(bk[:, 0:n], lhsT=cst("bd2"), rhs=ORAW[:, 0:n], start=True, stop=True), ["R3:oraw", "C:cn"], [br])
                    act(lambda e, bk=bk: e.activation(out=T1[:, 0:n], in_=bk[:, 0:n], func=AF.Copy, scale=1.0 / 64), [br, "R3:t1"], ["R3:t1"])
                    dve(lambda e: e.tensor_tensor(out=ORAW[:, 0:n], in0=ORAW[:, 0:n], in1=T1[:, 0:n], op=ALU.subtract), ["R3:oraw", "R3:t1"], ["R3:oraw"])
                    act(lambda e: e.activation(out=T2[:, 0:n], in_=ORAW[:, 0:n], func=AF.Square), ["R3:oraw", "R3:t2"], ["R3:t2"])
                    bk, br = psbank()
                    pe(lambda e, bk=bk: e.matmul(bk[:, 0:n], lhsT=cst("bd2"), rhs=T2[:, 0:n], start=True, stop=True), ["R3:t2", "C:cn"], [br])
                    act(lambda e, bk=bk: e.activation(out=T1[:, 0:n], in_=bk[:, 0:n], func=AF.Sqrt, bias=epsgn, scale=1.0 / 64), [br, "C:cn", "R3:t1"], ["R3:t1"])
                    dve(lambda e: e.reciprocal(out=T1[:, 0:n], in_=T1[:, 0:n]), ["R3:t1"], ["R3:t1"])
                    dve(lambda e: e.tensor_tensor(out=ORAW[:, 0:n], in0=ORAW[:, 0:n], in1=T1[:, 0:n], op=ALU.mult), ["R3:oraw", "R3:t1"], ["R3:oraw"])
                    dve(lambda e: e.tensor_scalar(out=ORAW[:, 0:n], in0=ORAW[:, 0:n], scalar1=vec("lnw", p), scalar2=vec("lnb", p), op0=ALU.mult, op1=ALU.add),
                        ["R3:oraw", "C:vec"], ["R3:oraw"])
                    dve(lambda e: e.tensor_tensor(out=ORAW[:, 0:n], in0=ORAW[:, 0:n], in1=BON[:, 0:n], op=ALU.add), ["R3:oraw", "R3:bon"], ["R3:oraw"])
                    dve(lambda e: e.tensor_tensor(out=o_all[:, 8 + p, t0:t0 + n], in0=ORAW[:, 0:n], in1=G_[:, 0:n], op=ALU.mult), ["R3:oraw", "R3:g"], ["OA:c%d" % (8 + p)])
                    if _STOP == 'rwd' and kind == 'p' and t0 > 0: raise _Stop()
                    if _STOP == 'rwe' and kind == 's': raise _Stop()
                if bi == 1:
                    P.fence("R3")
                    RR3.reset(mark1)
                    ST = f32buf(RR3, 128)
                    bk, br = psbank()
                    pe(lambda e, bk=bk: e.matmul(bk[:, 0:128], lhsT=RWS[:, p, :], rhs=identf, start=True, stop=True), ["C:rws", "C:cn"], [br])
                    act(lambda e, bk=bk: e.activation(out=ST, in_=bk[:, 0:128], func=AF.Copy), [br], ["R3:st"])
                    for h2 in range(2):
                        sp_dma(o_rw_p[l, 2 * p + h2], ST[h2 * 64:(h2 + 1) * 64, h2 * 64:(h2 + 1) * 64], ["R3:st"], [], "rwst%d_%d" % (p % 2, h2))
            P.fence("R3")
            RR3.reset(mark1)
            if has_s:
                OT = f32buf(RR3, SHW)
                for c0 in range(0, 26, 4):
                    ncn = min(4, 26 - c0)
                    bk, br = psbank()
                    for c in range(ncn):
                        pe(lambda e, c=c, c0=c0, bk=bk: e.matmul(bk[0:16, c * 128:(c + 1) * 128], lhsT=LASTs[:, c0 + c, :], rhs=identf, start=True, stop=True), ["R3:lasts", "C:cn"], [br])
                    act(lambda e, c0=c0, ncn=ncn, bk=bk: e.activation(out=OT[0:16, c0 * 128:(c0 + ncn) * 128], in_=bk[0:16, 0:ncn * 128], func=AF.Copy), [br], ["R3:ot"])
                sp_dma(o_sh_s[l], OT[0:16, :], ["R3:ot"], [], "ot")
            if bi == 1:
                OT2 = f32buf(RR3, SHW)
                for c0 in range(0, 26, 4):
                    ncn = min(4, 26 - c0)
                    bk, br = psbank()
                    for c in range(ncn):
                        pe(lambda e, c=c, c0=c0, bk=bk: e.matmul(bk[0:1, c * 128:(c + 1) * 128], lhsT=PRVC[:, c0 + c:c0 + c + 1], rhs=identf, start=True, stop=True), ["C:prvc", "C:cn"], [br])
                    act(lambda e, c0=c0, ncn=ncn, bk=bk: e.activation(out=OT2[0:1, c0 * 128:(c0 + ncn) * 128], in_=bk[0:1, 0:ncn * 128], func=AF.Copy), [br], ["R3:ot2"])
                sp_dma(o_sh_p[l], OT2[0:1, :], ["R3:ot2"], [], "ot2")

        def p1_sconv(l, blk, bi):
            P.fence("R3")
            RR3.reset()
            has_s = any(tg["kind"] == "s" for tg in blk["tg"])
            CVs = f32buf(RR3, 8 * 32).rearrange("p (c s j) -> p c s j", c=8, s=16)
            CVOs = f32buf(RR3, 8 * 32).rearrange("p (c s j) -> p c s j", c=8, s=16)
            if has_s:
                CVI = f32buf(RR3, 1024)
                sp_dma(CVI[0:32, :], st_cv[l].rearrange("s j c -> (s j) c"), [], ["R3:cvi"], "cvi")
                for c0 in range(0, 8, 4):
                    bk, br = psbank()
                    for c in range(4):
                        pe(lambda e, c=c, c0=c0, bk=bk: e.matmul(bk[:, c * 32:(c + 1) * 32], lhsT=CVI[0:32, (c0 + c) * 128:(c0 + c + 1) * 128], rhs=identf[0:32, 0:32], start=True, stop=True),
                           ["R3:cvi", "C:cn"], [br])
                    dve(lambda e, c0=c0, bk=bk: e.tensor_copy(out=CVs[:, c0:c0 + 4].rearrange("p c s j -> p c (s j)"), in_=bk[:, 0:128].rearrange("p (c x) -> p c x", c=4)), [br], ["R3:cvs"])
            mark = RR3.cur
            for c in range(8):
                P.fence("R3")
                RR3.reset(mark)
                wbc_, wbcr = wload(w_scbc[l, c], 4096)
                wbc = wbc_.rearrange("p (k c) -> p k c", k=16)
                wh_, whr = wload(w_sch[l, c], 2048)
                wh = wh_.rearrange("p (k c) -> p k c", k=16)
                Bb = f32buf(RR3, 512); Cc = f32buf(RR3, 512); U = f32buf(RR3, 520); CV = f32buf(RR3, 512)
                Us = f32buf(RR3, 160).rearrange("p (s t) -> p s t", s=16)
                cw = VEC[:, VOFF["cw"][0] + c * 3: VOFF["cw"][0] + c * 3 + 3]
                for tg in blk["tg"]:
                    t0, n, kind = tg["t0"], tg["n"], tg["kind"]
                    bkB, brB = mm_fm(wbc, wbcr, 0, 0, hT, hres, t0, n, 16)
                    act(lambda e, bkB=bkB: e.activation(out=Bb[:, 0:n], in_=bkB[:, 0:n], func=AF.Copy), [brB], ["R3:bb"])
                    bkC, brC = mm_fm(wbc, wbcr, 0, 128, hT, hres, t0, n, 16)
                    act(lambda e, bkC=bkC: e.activation(out=Cc[:, 0:n], in_=bkC[:, 0:n], func=AF.Copy), [brC], ["R3:cc"])
                    bkH, brH = mm_fm(wh, whr, 0, 0, hT, hres, t0, n, 16)
                    if kind == "p":
                        if t0 == 0:
                            act(lambda e: e.activation(out=U[:, 0:2], in_=CVH[:, c, :], func=AF.Copy), ["C:cvh"], ["R3:u"])
                        dve(lambda e, bkH=bkH: e.tensor_tensor(out=U[:, 2:2 + n], in0=Cc[:, 0:n], in1=bkH[:, 0:n], op=ALU.mult), ["R3:cc", brH], ["R3:u"])
                        dve(lambda e: e.tensor_scalar(out=CV[:, 0:n], in0=U[:, 0:n], scalar1=cw[:, 0:1], scalar2=None, op0=ALU.mult), ["R3:u", "C:vec"], ["R3:cv"])
                        dve(lambda e: e.scalar_tensor_tensor(out=CV[:, 0:n], in0=U[:, 1:1 + n], scalar=cw[:, 1:2], in1=CV[:, 0:n], op0=ALU.mult, op1=ALU.add), ["R3:u", "R3:cv", "C:vec"], ["R3:cv"])
                        dve(lambda e: e.scalar_tensor_tensor(out=CV[:, 0:n], in0=U[:, 2:2 + n], scalar=cw[:, 2:3], in1=CV[:, 0:n], op0=ALU.mult, op1=ALU.add), ["R3:u", "R3:cv", "C:vec"], ["R3:cv"])
                        dve(lambda e, t0=t0: e.tensor_tensor(out=o_all[:, 16 + c, t0:t0 + n], in0=Bb[:, 0:n], in1=CV[:, 0:n], op=ALU.mult), ["R3:bb", "R3:cv"], ["OA:c%d" % (16 + c)])
                        act(lambda e: e.activation(out=CVH[:, c, :], in_=U[:, n:n + 2], func=AF.Copy), ["R3:u"], ["C:cvh"])
                        act(lambda e: e.activation(out=U[:, 0:2], in_=CVH[:, c, :], func=AF.Copy), ["C:cvh", "R3:cv"], ["R3:u"])
                    else:
                        act(lambda e: e.activation(out=Us[:, :, 0:2], in_=CVs[:, c, :, :], func=AF.Copy), ["R3:cvs"], ["R3:us"])
                        dve(lambda e, bkH=bkH: e.tensor_tensor(out=Us[:, :, 2:10], in0=Cc[:, 0:128].rearrange("p (s t) -> p s t", s=16), in1=bkH[:, 0:128].rearrange("p (s t) -> p s t", s=16), op=ALU.mult),
                            ["R3:cc", brH], ["R3:us"])
                        cv3 = CV[:, 0:128].rearrange("p (s t) -> p s t", s=16)
                        dve(lambda e: e.tensor_scalar(out=cv3, in0=Us[:, :, 0:8], scalar1=cw[:, 0:1], scalar2=None, op0=ALU.mult), ["R3:us", "C:vec"], ["R3:cv"])
                        dve(lambda e: e.scalar_tensor_tensor(out=cv3, in0=Us[:, :, 1:9], scalar=cw[:, 1:2], in1=cv3, op0=ALU.mult, op1=ALU.add), ["R3:us", "R3:cv", "C:vec"], ["R3:cv"])
                        dve(lambda e: e.scalar_tensor_tensor(out=cv3, in0=Us[:, :, 2:10], scalar=cw[:, 2:3], in1=cv3, op0=ALU.mult, op1=ALU.add), ["R3:us", "R3:cv", "C:vec"], ["R3:cv"])
                        dve(lambda e, t0=t0: e.tensor_tensor(out=o_all[:, 16 + c, t0:t0 + 128], in0=Bb[:, 0:128], in1=CV[:, 0:128], op=ALU.mult), ["R3:bb", "R3:cv"], ["OA:c%d" % (16 + c)])
                        act(lambda e: e.activation(out=CVOs[:, c, :, :], in_=Us[:, :, 8:10], func=AF.Copy), ["R3:us"], ["R3:cvos"])
            P.fence("R3")
            RR3.reset(mark)
            if has_s:
                OC = f32buf(RR3, 1024)
                for c0 in range(0, 8, 4):
                    bk, br = psbank()
                    for c in range(4):
                        pe(lambda e, c=c, c0=c0, bk=bk: e.matmul(bk[0:32, c * 128:(c + 1) * 128], lhsT=CVOs[:, c0 + c].rearrange("p s j -> p (s j)"), rhs=identf, start=True, stop=True), ["R3:cvos", "C:cn"], [br])
                    act(lambda e, c0=c0, bk=bk: e.activation(out=OC[0:32, c0 * 128:(c0 + 4) * 128], in_=bk[0:32, 0:512], func=AF.Copy), [br], ["R3:oc"])
                sp_dma(o_cv_s[l].rearrange("s j c -> (s j) c"), OC[0:32, :], ["R3:oc"], [], "oc")
            if bi == 1:
                OC2 = f32buf(RR3, 1024)
                for c0 in range(0, 8, 4):
                    bk, br = psbank()
                    for c in range(4):
                        pe(lambda e, c=c, c0=c0, bk=bk: e.matmul(bk[0:2, c * 128:(c + 1) * 128], lhsT=CVH[:, c0 + c, :], rhs=identf, start=True, stop=True), ["C:cvh", "C:cn"], [br])
                    act(lambda e, c0=c0, bk=bk: e.activation(out=OC2[0:2, c0 * 128:(c0 + 4) * 128], in_=bk[0:2, 0:512], func=AF.Copy), [br], ["R3:oc2"])
                sp_dma(o_cv_p[l], OC2[0:2, :], ["R3:oc2"], [], "oc2")

        def p2(l, blk):
            fence_all()
            RR3.reset()
            T = blk["T"]
            mT = RR3.alloc(SZ_HT, BF16).rearrange("p (k t) -> p k t", k=16)
            SG = [f32buf(RR3, 512) for _ in range(2)]
            MM = [f32buf(RR3, 512) for _ in range(3)]
            TP = [f32buf(RR3, 512) for _ in range(2)]
            ctr = 0
            oares = ["OA:c%d" % k for k in range(24)]
            for c in range(16):
                wp_, wpr = wload(w_p[l, c], 3072)
                wp = wp_.rearrange("p (k c) -> p k c", k=24)
                for j in range(3):
                    wg_, wgr = wload(w_g[l, c, j], 2048)
                    wg = wg_.rearrange("p (k c) -> p k c", k=16)
                    for ti, (t0, n) in enumerate(tgroups(T)):
                        mm = MM[ti]; mres = "R3:mm%d" % ti
                        sg = SG[ctr % 2]; sres = "R3:sg%d" % (ctr % 2)
                        tp = TP[ctr % 2]; tres = "R3:tp%d" % (ctr % 2)
                        ctr += 1
                        bkg, brg = mm_fm(wg, wgr, 0, 0, hT, hres, t0, n, 16)
                        act(lambda e, bkg=bkg, sg=sg: e.activation(out=sg[:, 0:n], in_=bkg[:, 0:n], func=AF.Sigmoid), [brg], [sres])
                        bky, bry = mm_fm(wp, wpr, 8 * j, 0, o_all[:, 8 * j:8 * j + 8, :], lambda a, b, j=j: oares[8 * j:8 * j + 8], t0, n, 8)
                        if j == 0:
                            dve(lambda e, bky=bky, sg=sg, mm=mm: e.tensor_tensor(out=mm[:, 0:n], in0=sg[:, 0:n], in1=bky[:, 0:n], op=ALU.mult), [sres, bry], [mres])
                        elif j == 1:
                            dve(lambda e, bky=bky, sg=sg, tp=tp: e.tensor_tensor(out=tp[:, 0:n], in0=sg[:, 0:n], in1=bky[:, 0:n], op=ALU.mult), [sres, bry], [tres])
                            dve(lambda e, mm=mm, tp=tp: e.tensor_tensor(out=mm[:, 0:n], in0=mm[:, 0:n], in1=tp[:, 0:n], op=ALU.add), [mres, tres], [mres])
                        else:
                            dve(lambda e, bky=bky, sg=sg, tp=tp: e.tensor_tensor(out=tp[:, 0:n], in0=sg[:, 0:n], in1=bky[:, 0:n], op=ALU.mult), [sres, bry], [tres])
                            dve(lambda e, mm=mm, tp=tp, t0=t0, c=c: e.tensor_tensor(out=mT[:, c, t0:t0 + n], in0=mm[:, 0:n], in1=tp[:, 0:n], op=ALU.add), [mres, tres],
                                ["R3:mT%d" % i for i in range(t0 // 128, (t0 + n) // 128)])
            return mT

        def p3(l, blk, mT):
            P.fence("HT", "OA", "GB", "PS", "R3")
            tiles = blk["tiles"]
            nt = len(tiles)
            ROA.reset()
            MIXa = [f32buf(ROA, D) for _ in range(6)]
            junk = bfbuf(ROA, D)
            RR3.reset(SZ_HT)
            MIXb = [f32buf(RR3, D) for _ in range(3)]
            XT = f32buf(RR3, D)
            xn = bfbuf(RR3, D)
            MIX = MIXa + MIXb
            mres = lambda i: ("OA:mix%d" % i) if i < 6 else ("R3:mix%d" % i)
            STAT = f32buf(ROA, 9 * 8 + 16)
            sp_dma(GB, gb_d[l, 0].partition_broadcast(128), [], ["GB:g"], "gbl")
            for ct in range(8):
                wo_, wor = wload(w_o[l, ct], 4096)
                wo = wo_.rearrange("p (k c) -> p k c", k=16)
                for i in range(nt):
                    bk, br = psbank()
                    for k in range(16):
                        pe(lambda e, bk=bk, k=k, i=i: e.matmul(bk[:, 0:256], lhsT=mT[:, k, i * 128:(i + 1) * 128], rhs=wo[:, k, :], start=(k == 0), stop=(k == 15)),
                           [wor, "R3:mT%d" % i], [br])
                    act(lambda e, bk=bk, i=i, ct=ct: e.activation(out=MIX[i][:, ct * 256:(ct + 1) * 256], in_=bk[:, 0:256], func=AF.Copy), [br], [mres(i)])
                    act(lambda e, bk=bk, i=i, ct=ct: e.activation(out=junk[:, 0:256], in_=bk[:, 0:256], func=AF.Square, accum_out=STAT[:, i * 8 + ct:i * 8 + ct + 1]), [br], ["OA:junk", "OA:stat%d" % i])
            S2 = STAT[:, 72:88]
            for i, g in enumerate(tiles):
                dve(lambda e, i=i: e.tensor_reduce(out=S2[:, 0:1], in_=STAT[:, i * 8:(i + 1) * 8], axis=AX.X, op=ALU.add), ["OA:stat%d" % i], ["OA:s2"])
                rsqrt_col(S2[:, 1:2], S2[:, 0:1], 1.0 / D, eps6, ["OA:s2", "C:cn"], "OA:s2r", S2[:, 2:3])
                act(lambda e, i=i: e.activation(out=MIX[i], in_=MIX[i], func=AF.Copy, scale=S2[:, 1:2]), [mres(i), "OA:s2r"], [mres(i)])
                dve(lambda e, i=i: e.tensor_tensor(out=MIX[i], in0=MIX[i], in1=GB, op=ALU.mult), [mres(i), "GB:g"], [mres(i)])
                sp_dma(XT, xsrc(l, g), [yres(l - 1, g)] if l > 0 else [], ["R3:xt"], "xt3")
                dve(lambda e, i=i: e.tensor_tensor(out=MIX[i], in0=MIX[i], in1=XT, op=ALU.add), [mres(i), "R3:xt"], [mres(i)])
                sp_dma(x1s[i * 128:(i + 1) * 128, :], MIX[i], [mres(i)], ["DR:x1_%d" % i], "mixo%d" % i)
                act(lambda e, i=i: e.activation(out=junk, in_=MIX[i], func=AF.Square, accum_out=S2[:, 4:5]), [mres(i)], ["OA:junk", "OA:s3"])
                rsqrt_col(S2[:, 5:6], S2[:, 4:5], 1.0 / D, eps6, ["OA:s3", "C:cn"], "OA:s3r", S2[:, 6:7])
                act(lambda e, i=i: e.activation(out=xn, in_=MIX[i], func=AF.Copy, scale=S2[:, 5:6]), [mres(i), "OA:s3r"], ["R3:xn"])
                for half in range(2):
                    bk, br = psbank()
                    bkb = bk.bitcast(BF16)
                    for c in range(8):
                        cc = half * 8 + c
                        pe(lambda e, c=c, cc=cc, bkb=bkb: e.transpose(out=bkb[:, c * 128:(c + 1) * 128], in_=xn[:, cc * 128:(cc + 1) * 128], identity=identb), ["R3:xn", "C:identb"], [br])
                    g3 = vec("gpl", half * 8, 8).unsqueeze(2).to_broadcast([128, 8, 128])
                    dve(lambda e, half=half, bkb=bkb, g3=g3, i=i: e.tensor_tensor(out=hT[:, half * 8:half * 8 + 8, i * 128:(i + 1) * 128],
                                                                              in0=bkb[:, 0:1024].rearrange("p (c t) -> p c t", c=8), in1=g3, op=ALU.mult), [br, "C:vec"], ["HT:t%d" % i])

        def p4(l, blk):
            P.fence("OA", "R3", "GB", "PS")
            tiles = blk["tiles"]
            nt = len(tiles)
            T = blk["T"]
            RR3.reset()
            YA = [f32buf(RR3, D) for _ in range(nt)]
            ROA.reset()
            AT = ROA.alloc(2 * 8 * 1152, BF16).rearrange("p (k t) -> p k t", k=8)
            X1 = [f32buf(ROA, D) for _ in range(2)]
            junk = bfbuf(ROA, D)
            RT = [f32buf(ROA, 512) for _ in range(2)]
            STAT = f32buf(ROA, 16)
            sp_dma(GB, gb_d[l, 1].partition_broadcast(128), [], ["GB:g"], "gbl")
            rc = 0
            for e8 in range(8):
                for q in range(4):
                    w1_, w1r = wload(w_f1[l, e8 * 4 + q], 4096)
                    w1 = w1_.rearrange("p (k c) -> p k c", k=16)
                    for hc in range(2):
                        for (t0, n) in tgroups(T):
                            bk, br = mm_fm(w1, w1r, 0, hc * 128, hT, hres, t0, n, 16)
                            rt = RT[rc % 2]; rres = "OA:rt%d" % (rc % 2)
                            rc += 1
                            act(lambda e, bk=bk, rt=rt: e.activation(out=rt[:, 0:n], in_=bk[:, 0:n], func=AF.Relu), [br], [rres])
                            dve(lambda e, rt=rt, q=q, hc=hc, t0=t0: e.tensor_tensor(out=AT[:, q * 2 + hc, t0:t0 + n], in0=rt[:, 0:n], in1=rt[:, 0:n], op=ALU.mult), [rres],
                                ["OA:at%d" % i for i in range(t0 // 128, (t0 + n) // 128)])
                for cg in range(4):
                    w2_, w2r = wload(w_f2[l, e8, cg], 4096)
                    w2 = w2_.rearrange("p (k c) -> p k c", k=8)
                    for i in range(nt):
                        bk, br = psbank()
                        for k in range(8):
                            pe(lambda e, bk=bk, k=k, i=i: e.matmul(bk[:, 0:512], lhsT=AT[:, k, i * 128:(i + 1) * 128], rhs=w2[:, k, :], start=(k == 0), stop=(k == 7)),
                               [w2r, "OA:at%d" % i], [br])
                        ya = YA[i][:, cg * 512:(cg + 1) * 512]
                        if e8 == 0:
                            act(lambda e, bk=bk, ya=ya: e.activation(out=ya, in_=bk[:, 0:512], func=AF.Copy), [br], ["R3:ya%d_%d" % (i, cg)])
                        else:
                            dve(lambda e, bk=bk, ya=ya: e.tensor_tensor(out=ya, in0=ya, in1=bk[:, 0:512], op=ALU.add), [br, "R3:ya%d_%d" % (i, cg)], ["R3:ya%d_%d" % (i, cg)])
            for i, g in enumerate(tiles):
                yr = ["R3:ya%d_%d" % (i, cg) for cg in range(4)]
                s = i % 2
                sp_dma(X1[s], x1s[i * 128:(i + 1) * 128, :], ["DR:x1_%d" % i], ["OA:x1%d" % s], "x1l%d" % s)
                act(lambda e, i=i: e.activation(out=junk, in_=YA[i], func=AF.Square, accum_out=STAT[:, 0:1]), yr, ["OA:junk", "OA:st"])
                rsqrt_col(STAT[:, 1:2], STAT[:, 0:1], 1.0 / D, eps6, ["OA:st", "C:cn"], "OA:str", STAT[:, 2:3])
                act(lambda e, i=i: e.activation(out=YA[i], in_=YA[i], func=AF.Copy, scale=STAT[:, 1:2]), yr + ["OA:str"], yr)
                dve(lambda e, i=i: e.tensor_tensor(out=YA[i], in0=YA[i], in1=GB, op=ALU.mult), yr + ["GB:g"], yr)
                dve(lambda e, i=i, s=s: e.tensor_tensor(out=YA[i], in0=YA[i], in1=X1[s], op=ALU.add), yr + ["OA:x1%d" % s], yr)
                sp_dma(ydst(l, g), YA[i], yr, [yres(l, g)], "yo%d" % i)

        import os as _os
        _STOP = _os.environ.get('KSTOP', '')
        _CKN = int(_os.environ.get('KSTOPN', '0'))
        _ckc = [0]

        def ck(name):
            _ckc[0] += 1
            if (_CKN and _ckc[0] == _CKN) or (name == _os.environ.get('KSTOPNAME', '-')):
                print("STOP at checkpoint", _ckc[0], name, flush=True)
                raise _Stop()
        globals()['_ck'] = ck
        for l in range(int(_os.environ.get('KLAYERS', L))):
          try:
            layer_setup(l)
            _stop = _STOP
            for bi, blk in enumerate(BLOCKS):
                p0(l, blk)
                if _stop == 'p0': break
                p1_hgrn(l, blk, bi)
                if _stop == 'hg': break
                p1_rwkv(l, blk, bi)
                if _stop == 'rw': break
                p1_sconv(l, blk, bi)
                if _stop == 'sc': break
                mT = p2(l, blk)
                if _stop == 'p2': break
                if dbg and l == 0 and bi == 0:
                    P.dma("pool", lambda e: e.dma_start(out=dbg_out["oall"], in_=o_all), ["OA:c%d" % k for k in range(24)], [], "dbg_oall")
                    P.dma("pool", lambda e: e.dma_start(out=dbg_out["mT"], in_=mT), ["R3:mT%d" % i for i in range(9)], [], "dbg_mT")
                    P.dma("pool", lambda e: e.dma_start(out=dbg_out["hT"], in_=hT), ["HT:t%d" % i for i in range(9)], [], "dbg_hT")
                p3(l, blk, mT)
                if dbg and l == 0 and bi == 0:
                    P.dma("pool", lambda e: e.dma_start(out=dbg_out["h2T"], in_=hT), ["HT:t%d" % i for i in range(9)], [], "dbg_h2T")
                p4(l, blk)
          except _Stop:
            break
        P.finish()
        P.emit(nc)
    return nc


def _tile_w(w, col_lists):
    K = w.shape[0]
    nk = K // 128
    cols = np.concatenate(col_lists)
    sub = w[:, cols]
    return np.ascontiguousarray(sub.reshape(nk, 128, len(cols)).transpose(1, 0, 2).reshape(128, nk * len(cols)))


def _vec_fm(v):
    return np.ascontiguousarray(v.reshape(-1, 128).T)


_CACHE = {}


def prepare_shared(inp):
    r = np.arange
    OFF_RW = 4096
    OFF_SC = OFF_RW + SHW
    OFF_G = OFF_SC + 3072
    sh = {}
    w_in = inp["w_in"]
    w_hg = np.empty((L, HGH, 2, 128, 4096), np.float32)
    w_rwl = np.empty((L, 128, 4096), np.float32)
    w_rwrk = np.empty((L, RWP, 128, 4096), np.float32)
    w_rwv = np.empty((L, RWP, 128, 2048), np.float32)
    w_scbc = np.empty((L, 8, 128, 4096), np.float32)
    w_sch = np.empty((L, 8, 128, 2048), np.float32)
    w_g = np.empty((L, 16, 3, 128, 2048), np.float32)
    w_p = np.empty((L, 16, 128, 3072), np.float32)
    w_o = np.empty((L, 8, 128, 4096), np.float32)
    w_f1 = np.empty((L, 32, 128, 4096), np.float32)
    w_f2 = np.empty((L, 8, 4, 128, 4096), np.float32)
    vecs = np.zeros((L, 128, NV), np.float32)
    for l in range(L):
        W = w_in[l]
        for h in range(HGH):
            w_hg[l, h, 0] = _tile_w(W, [r(h * 128, h * 128 + 128), 1024 + r(h * 128, h * 128 + 128)])
            w_hg[l, h, 1] = _tile_w(W, [3072 + r(h * 128, h * 128 + 128), 2048 + r(h * 128, h * 128 + 128)])
        w_rwl[l] = _tile_w(W, [OFF_RW + 3072 + r(0, 256)])
        for p in range(RWP):
            w_rwrk[l, p] = _tile_w(W, [OFF_RW + r(p * 128, p * 128 + 128), OFF_RW + 1024 + r(p * 128, p * 128 + 128)])
            w_rwv[l, p] = _tile_w(W, [OFF_RW + 2048 + r(p * 128, p * 128 + 128)])
        for c in range(8):
            w_scbc[l, c] = _tile_w(W, [OFF_SC + r(c * 128, c * 128 + 128), OFF_SC + 1024 + r(c * 128, c * 128 + 128)])
            w_sch[l, c] = _tile_w(W, [OFF_SC + 2048 + r(c * 128, c * 128 + 128)])
        for c in range(16):
            for j in range(3):
                w_g[l, c, j] = _tile_w(W, [OFF_G + j * D + r(c * 128, c * 128 + 128)])
            cc = [r(c * 128, c * 128 + 128)]
            w_p[l, c] = np.concatenate([_tile_w(inp["w_pa"][l], cc), _tile_w(inp["w_pb"][l], cc), _tile_w(inp["w_pc"][l], cc)], axis=1)
        for ct in range(8):
            w_o[l, ct] = _tile_w(inp["w_o"][l], [r(ct * 256, ct * 256 + 256)])
        for t in range(32):
            w_f1[l, t] = _tile_w(inp["w_ff1"][l], [r(t * 256, t * 256 + 256)])
        for e8 in range(8):
            for cg in range(4):
                w_f2[l, e8, cg] = _tile_w(inp["w_ff2"][l][e8 * 1024:(e8 + 1) * 1024], [r(cg * 512, cg * 512 + 512)])

        def put(name, arr):
            o_, w_ = VOFF[name]
            vecs[l, :, o_:o_ + w_] = arr
        put("gpm", _vec_fm(inp["norm_pre_mix"][l]))
        put("gpl", _vec_fm(inp["norm_pre_mlp"][l]))
        put("lb0", _vec_fm(inp["hg_lb_logits"][0]))
        put("lb1", _vec_fm(inp["hg_lb_logits"][1]))
        put("hgn", _vec_fm(inp["hg_norm"][l]))
        put("mu", _vec_fm(inp["rw_mu"][l]))
        put("w0", _vec_fm(inp["rw_w0"][l]))
        put("a0", _vec_fm(inp["rw_a0"][l]))
        put("kk", _vec_fm(inp["rw_k_k"][l]))
        put("ka", _vec_fm(inp["rw_k_a"][l]))
        put("rk", _vec_fm(inp["rw_r_k"][l].reshape(-1)))
        put("lnw", _vec_fm(inp["rw_ln_w"][l]))
        put("lnb", _vec_fm(inp["rw_ln_b"][l]))
        cw = inp["sc_conv_w"][l]
        put("cw", np.ascontiguousarray(cw.reshape(3, 8, 128).transpose(2, 1, 0).reshape(128, 24)))
    sh.update(w_hg=w_hg, w_rwl=w_rwl, w_rwrk=w_rwrk, w_rwv=w_rwv, w_scbc=w_scbc, w_sch=w_sch, w_g=w_g, w_p=w_p, w_o=w_o,
              w_f1=w_f1, w_f2=w_f2, vecs=vecs,
              gb=np.ascontiguousarray(np.stack([inp["norm_post_mix"], inp["norm_post_mlp"]], axis=1)).astype(np.float32),
              lw2=np.ascontiguousarray(inp["rw_w2"]), la2=np.ascontiguousarray(inp["rw_a2"]), lg2=np.ascontiguousarray(inp["rw_g2"]),
              consts=CONSTS, masks=MASKS)
    return sh


def make_in_maps(inp):
    inp = {k: np.asarray(v) for k, v in inp.items()}
    sh = prepare_shared(inp)
    maps = []
    for c in range(NCORE):
        b = c // 2
        m = dict(sh)
        m["xp"] = np.ascontiguousarray(inp["x_prompt"][b])
        m["xs"] = np.ascontiguousarray(inp["x_sample"][16 * c:16 * c + 16].reshape(128, D))
        m["st_hg"] = np.ascontiguousarray(inp["state_hgrn"][:, 16 * c:16 * c + 16])
        m["st_rw"] = np.ascontiguousarray(inp["state_rwkv"][:, 16 * c:16 * c + 16])
        m["st_sh"] = np.ascontiguousarray(inp["state_rwkv_shift"][:, 16 * c:16 * c + 16])
        m["st_cv"] = np.ascontiguousarray(inp["state_conv"][:, 16 * c:16 * c + 16])
        maps.append(m)
    return maps


def assemble(results):
    R = results
    yp = np.stack([R[2 * b]["yp"] for b in range(4)]).reshape(4, 2048, D)
    ys = np.concatenate([R[c]["ys"].reshape(16, 8, D) for c in range(NCORE)], axis=0)
    hg_p = np.stack([R[2 * b]["o_hg_p"] for b in range(4)], axis=1)
    rw_p = np.stack([R[2 * b]["o_rw_p"] for b in range(4)], axis=1)
    sh_p = np.stack([R[2 * b]["o_sh_p"][:, 0] for b in range(4)], axis=1)
    cv_p = np.stack([R[2 * b]["o_cv_p"] for b in range(4)], axis=1)
    hg_s = np.concatenate([R[c]["o_hg_s"] for c in range(NCORE)], axis=1)
    rw_s = np.concatenate([R[c]["o_rw_s"] for c in range(NCORE)], axis=1)
    sh_s = np.concatenate([R[c]["o_sh_s"] for c in range(NCORE)], axis=1)
    cv_s = np.concatenate([R[c]["o_cv_s"] for c in range(NCORE)], axis=1)
    outs = (yp, ys, hg_p, rw_p, sh_p, cv_p, hg_s, rw_s, sh_s, cv_s)
    return tuple(np.ascontiguousarray(o, dtype=np.float32) for o in outs)


def kernel(**inputs):
    nc = build_program()
    maps = make_in_maps(inputs)
    res = run_bass_kernel_spmd(nc, maps, core_ids=list(range(NCORE)))
    return assemble(res.results)
```
